# Optimizing a Trainium2 kernel written in Bass

```python
import math
import jax, jax.numpy as jnp
from jax import lax
import numpy as np

D_MODEL = 1024
BATCH = 4
SEQ = 8192
DEPTH = 2

N_HEADS = 16
HEAD_DIM = D_MODEL // N_HEADS
MOBA_BLOCK = 256
MOBA_TOPK = 3
Q_CHUNK = 32
ROPE_THETA = 10000.0
RWKV_HEAD = 64
RWKV_HEADS = D_MODEL // RWKV_HEAD
DECAY_LORA = 64
AAA_LORA = 64
GATE_LORA = 128
LNX_EPS = 64e-5
D_FF = ((8 * D_MODEL // 3 + 127) // 128) * 128
CONV_WIDTH = 3
N_MIXERS = 2
N_ATTN_LAYERS = (DEPTH + N_MIXERS - 1) // N_MIXERS
N_RWKV_LAYERS = DEPTH // N_MIXERS
RMS_EPS = 1e-6

kernel_name = 'moba_rwkv7_convffn_hybrid'


def rms_norm(x, g):
    xf = x.astype(jnp.float32)
    y = xf * lax.rsqrt(jnp.mean(xf * xf, axis=-1, keepdims=True) + RMS_EPS)
    return (y * g.astype(jnp.float32)).astype(x.dtype)


def rope_tables(seq):
    inv = 1.0 / (ROPE_THETA ** (jnp.arange(0, HEAD_DIM, 2, dtype=jnp.float32) / HEAD_DIM))
    ang = jnp.arange(seq, dtype=jnp.float32)[:, None] * inv[None, :]
    ang = jnp.concatenate([ang, ang], axis=-1)
    return jnp.cos(ang), jnp.sin(ang)


def apply_rope(t, cos, sin):
    half = HEAD_DIM // 2
    rot = jnp.concatenate([-t[..., half:], t[..., :half]], axis=-1)
    return t * cos.astype(t.dtype) + rot * sin.astype(t.dtype)


def moba_attention(x, w_qkv, w_o, cos, sin):
    B, S, D = x.shape
    H, Dh = N_HEADS, HEAD_DIM
    qkv = x @ w_qkv
    q, k, v = jnp.split(qkv, 3, axis=-1)
    to_heads = lambda t: t.reshape(B, S, H, Dh).transpose(0, 2, 1, 3)
    q = apply_rope(to_heads(q), cos, sin)
    k = apply_rope(to_heads(k), cos, sin)
    v = to_heads(v)
    scale = 1.0 / math.sqrt(Dh)

    nb = -(-S // MOBA_BLOCK)
    pad = nb * MOBA_BLOCK - S
    kb = jnp.pad(k, ((0, 0), (0, 0), (0, pad), (0, 0))).reshape(B, H, nb, MOBA_BLOCK, Dh)
    vb = jnp.pad(v, ((0, 0), (0, 0), (0, pad), (0, 0))).reshape(B, H, nb, MOBA_BLOCK, Dh)

    kmean = jnp.mean(kb.astype(jnp.float32), axis=3)
    gate = jnp.einsum('bhsd,bhnd->bhsn', q.astype(jnp.float32), kmean)
    qblk = jnp.arange(S) // MOBA_BLOCK
    past = jnp.arange(nb)[None, :] < qblk[:, None]
    gate = jnp.where(past, gate, -jnp.inf)
    k_sel = min(MOBA_TOPK, nb)
    _, gidx = lax.top_k(gate, k_sel)
    gvalid = gidx < qblk[:, None]

    bi = jnp.arange(B)[:, None, None, None]
    hi = jnp.arange(H)[None, :, None, None]

    def chunk(c):
        s0 = c * Q_CHUNK
        qc = lax.dynamic_slice_in_dim(q, s0, Q_CHUNK, axis=2)
        idc = lax.dynamic_slice_in_dim(gidx, s0, Q_CHUNK, axis=2)
        vdc = lax.dynamic_slice_in_dim(gvalid, s0, Q_CHUNK, axis=2)
        kg = kb[bi, hi, idc]
        vg = vb[bi, hi, idc]
        s_sel = jnp.einsum('bhqd,bhqjtd->bhqjt', qc, kg) * scale
        s_sel = jnp.where(vdc[..., None], s_sel, -jnp.inf)
        s_sel = s_sel.reshape(B, H, Q_CHUNK, k_sel * MOBA_BLOCK)
        own = s0 // MOBA_BLOCK
        ko = lax.dynamic_index_in_dim(kb, own, axis=2, keepdims=False)
        vo = lax.dynamic_index_in_dim(vb, own, axis=2, keepdims=False)
        s_own = jnp.einsum('bhqd,bhtd->bhqt', qc, ko) * scale
        qpos = s0 + jnp.arange(Q_CHUNK)
        kpos = own * MOBA_BLOCK + jnp.arange(MOBA_BLOCK)
        s_own = jnp.where(kpos[None, :] <= qpos[:, None], s_own, -jnp.inf)
        s_all = jnp.concatenate([s_sel, s_own], axis=-1).astype(jnp.float32)
        p = jax.nn.softmax(s_all, axis=-1).astype(v.dtype)
        p_sel = p[..., :k_sel * MOBA_BLOCK].reshape(B, H, Q_CHUNK, k_sel, MOBA_BLOCK)
        p_own = p[..., k_sel * MOBA_BLOCK:]
        return (jnp.einsum('bhqjt,bhqjtd->bhqd', p_sel, vg)
                + jnp.einsum('bhqt,bhtd->bhqd', p_own, vo))

    out = lax.map(chunk, jnp.arange(S // Q_CHUNK))
    out = out.transpose(1, 0, 3, 2, 4).reshape(B, S, D)
    return out @ w_o


def rwkv7_time_mix(x, mu, w_rkv, w0, w1, w2, a0, a1, a2, g1, g2, k_k, k_a, r_k,
                   lnx_w, lnx_b, w_o):
    B, S, D = x.shape
    H, N = RWKV_HEADS, RWKV_HEAD
    f32 = jnp.float32
    x_prev = jnp.pad(x[:, :-1], ((0, 0), (1, 0), (0, 0)))
    xx = x_prev - x
    xr, xw, xk, xv, xa, xg = [x + xx * mu[n] for n in range(6)]
    rkv = jnp.einsum('nbsd,nde->nbse', jnp.stack([xr, xk, xv]), w_rkv)
    r, k, v = rkv[0], rkv[1], rkv[2]
    w = -jax.nn.softplus(-(w0 + jnp.tanh(xw @ w1) @ w2).astype(f32)) - 0.5
    decay = jnp.exp(-jnp.exp(w))
    a = jax.nn.sigmoid((a0 + (xa @ a1) @ a2).astype(f32))
    g = jax.nn.sigmoid(xg @ g1) @ g2

    heads = lambda t: t.astype(f32).reshape(B, S, H, N)
    r, k, v, decay, a = heads(r), heads(k), heads(v), heads(decay), heads(a)
    kk = k * k_k.astype(f32).reshape(H, N)
    kk = kk / jnp.maximum(jnp.sqrt(jnp.sum(kk * kk, axis=-1, keepdims=True)), 1e-12)
    k = k * (1.0 + (a - 1.0) * k_a.astype(f32).reshape(H, N))

    def step(state, inp):
        r_t, w_t, k_t, v_t, kk_t, a_t = inp
        sa = jnp.einsum('bhvk,bhk->bhv', state, -kk_t)
        state = (state * w_t[:, :, None, :]
                 + sa[..., None] * (kk_t * a_t)[:, :, None, :]
                 + v_t[..., None] * k_t[:, :, None, :])
        y = jnp.einsum('bhvk,bhk->bhv', state, r_t)
        return state, y

    tm = lambda t: jnp.swapaxes(t, 0, 1)
    _, y = lax.scan(step, jnp.zeros((B, H, N, N), f32),
                    (tm(r), tm(decay), tm(k), tm(v), tm(kk), tm(a)))
    y = jnp.swapaxes(y, 0, 1)
    mean = jnp.mean(y, axis=-1, keepdims=True)
    var = jnp.mean(jnp.square(y - mean), axis=-1, keepdims=True)
    yn = ((y - mean) * lax.rsqrt(var + LNX_EPS)).reshape(B, S, D)
    yn = yn * lnx_w.astype(f32) + lnx_b.astype(f32)
    bonus = (jnp.sum(r * k * r_k.astype(f32), axis=-1, keepdims=True) * v).reshape(B, S, D)
    return ((yn + bonus).astype(x.dtype) * g) @ w_o


def conv_ffn(x, w_gate, w_up, conv_w, conv_b, w_down):
    gt = x @ w_gate
    gt = lax.conv_general_dilated(
        gt, conv_w[:, None, :].astype(gt.dtype), window_strides=(1,),
        padding=[(CONV_WIDTH - 1, 0)], dimension_numbers=('NWC', 'WIO', 'NWC'),
        feature_group_count=gt.shape[-1]) + conv_b
    return (jax.nn.silu(gt) * (x @ w_up)) @ w_down


def setup_inputs(seed: int = 0) -> dict:
    key = jax.random.key(seed)
    ks = iter(jax.random.split(key, 40))
    nrm = lambda shape, s: jax.random.normal(next(ks), shape, jnp.float32) * s
    D, F, H, N = D_MODEL, D_FF, RWKV_HEADS, RWKV_HEAD
    NA, NR = N_ATTN_LAYERS, N_RWKV_LAYERS
    return {
        'x': nrm((BATCH, SEQ, D), 1.0),
        'norm_mix': 1.0 + nrm((DEPTH, D), 0.02),
        'norm_ffn': 1.0 + nrm((DEPTH, D), 0.02),
        'norm_final': 1.0 + nrm((D,), 0.02),
        'attn_w_qkv': nrm((NA, D, 3 * D), D ** -0.5),
        'attn_w_o': nrm((NA, D, D), D ** -0.5),
        'rwkv_mu': jax.random.uniform(next(ks), (NR, 6, D), jnp.float32),
        'rwkv_w_rkv': nrm((NR, 3, D, D), D ** -0.5),
        'rwkv_w0': -1.0 + nrm((NR, D), 0.5),
        'rwkv_w1': nrm((NR, D, DECAY_LORA), D ** -0.5),
        'rwkv_w2': nrm((NR, DECAY_LORA, D), 0.5 * DECAY_LORA ** -0.5),
        'rwkv_a0': nrm((NR, D), 0.1),
        'rwkv_a1': nrm((NR, D, AAA_LORA), D ** -0.5),
        'rwkv_a2': nrm((NR, AAA_LORA, D), 0.5 * AAA_LORA ** -0.5),
        'rwkv_g1': nrm((NR, D, GATE_LORA), D ** -0.5),
        'rwkv_g2': nrm((NR, GATE_LORA, D), GATE_LORA ** -0.5),
        'rwkv_k_k': 0.85 + nrm((NR, D), 0.05),
        'rwkv_k_a': 1.0 + nrm((NR, D), 0.05),
        'rwkv_r_k': nrm((NR, H, N), 0.1),
        'rwkv_lnx_w': 1.0 + nrm((NR, D), 0.02),
        'rwkv_lnx_b': nrm((NR, D), 0.02),
        'rwkv_w_o': nrm((NR, D, D), D ** -0.5),
        'ffn_w_gate': nrm((DEPTH, D, F), D ** -0.5),
        'ffn_w_up': nrm((DEPTH, D, F), D ** -0.5),
        'ffn_conv_w': nrm((DEPTH, CONV_WIDTH, F), CONV_WIDTH ** -0.5),
        'ffn_conv_b': nrm((DEPTH, F), 0.02),
        'ffn_w_down': nrm((DEPTH, F, D), F ** -0.5),
    }


def reference(x, norm_mix, norm_ffn, norm_final, attn_w_qkv, attn_w_o,
              rwkv_mu, rwkv_w_rkv, rwkv_w0, rwkv_w1, rwkv_w2, rwkv_a0, rwkv_a1,
              rwkv_a2, rwkv_g1, rwkv_g2, rwkv_k_k, rwkv_k_a, rwkv_r_k,
              rwkv_lnx_w, rwkv_lnx_b, rwkv_w_o,
              ffn_w_gate, ffn_w_up, ffn_conv_w, ffn_conv_b, ffn_w_down):
    S = x.shape[1]
    cos, sin = rope_tables(S)
    h = x
    for i in range(DEPTH):
        xn = rms_norm(h, norm_mix[i])
        j = i // N_MIXERS
        if i % N_MIXERS == 0:
            h = h + moba_attention(xn, attn_w_qkv[j], attn_w_o[j], cos, sin)
        else:
            h = h + rwkv7_time_mix(
                xn, rwkv_mu[j], rwkv_w_rkv[j], rwkv_w0[j], rwkv_w1[j], rwkv_w2[j],
                rwkv_a0[j], rwkv_a1[j], rwkv_a2[j], rwkv_g1[j], rwkv_g2[j],
                rwkv_k_k[j], rwkv_k_a[j], rwkv_r_k[j], rwkv_lnx_w[j], rwkv_lnx_b[j],
                rwkv_w_o[j])
        h = h + conv_ffn(rms_norm(h, norm_ffn[i]), ffn_w_gate[i], ffn_w_up[i],
                         ffn_conv_w[i], ffn_conv_b[i], ffn_w_down[i])
    return rms_norm(h, norm_final)
```

```python
import math
import numpy as np
from contextlib import ExitStack
import concourse.bass as bass
import concourse.mybir as mybir
from concourse.bass_utils import run_bass_kernel_spmd

F32 = mybir.dt.float32
BF16 = mybir.dt.bfloat16
AF = mybir.ActivationFunctionType
ALU = mybir.AluOpType
AX = mybir.AxisListType


class Buf:
    def __init__(self, t, name):
        self.t = t
        self.name = name
        self.w = None
        self.r = {}

    def __getitem__(self, k):
        return self.t[k]


class FW:
    ENG = ("pe", "act", "dve", "pool", "sp")

    def __init__(self, nc, es, n_dma_sems=24):
        self.nc = nc
        self.es = es
        self.es0 = es
        self.eng = {"pe": nc.tensor, "act": nc.scalar, "dve": nc.vector,
                    "pool": nc.gpsimd, "sp": nc.sync}
        self.sem = {}
        self.cnt = {}
        for e in self.ENG:
            self.sem[e] = es.enter_context(nc.semaphore("s_" + e))
            self.cnt[e] = 0
        self.dsem = []
        for i in range(n_dma_sems):
            k = "d%d" % i
            self.sem[k] = es.enter_context(nc.semaphore("s_" + k))
            self.cnt[k] = 0
            self.dsem.append(k)
        self.dnext = {"hw": 0, "sw": 0}
        nsw = n_dma_sems // 3
        self.dpool = {"sw": self.dsem[:nsw], "hw": self.dsem[nsw:]}
        self.seen = {e: {} for e in self.ENG}
        self.nbuf = 0
        self.ninst = 0

    def sbuf(self, shape, dtype, name=None):
        self.nbuf += 1
        name = "%s_%d" % (name or "b", self.nbuf)
        t = self.es.enter_context(self.nc.sbuf_tensor(name, list(shape), dtype))
        return Buf(t, name)

    def psum(self, shape, dtype=F32, name=None):
        self.nbuf += 1
        name = "%s_%d" % (name or "p", self.nbuf)
        t = self.es.enter_context(self.nc.psum_tensor(name, list(shape), dtype))
        return Buf(t, name)

    def dram(self, name, shape, dtype, kind="Internal"):
        t = self.nc.dram_tensor(name, list(shape), dtype, kind=kind)
        return Buf(t.ap(), name)

    def _wait(self, e, ev):
        if ev is None:
            return
        k, v = ev
        if self.seen[e].get(k, 0) >= v:
            return
        if k == e and e == "pe":
            return
        self.eng[e].wait_ge(self.sem[k], v)
        self.seen[e][k] = v
        self.ninst += 1

    def _deps(self, e, reads, writes):
        for b in reads:
            self._wait(e, b.w)
        for b in writes:
            self._wait(e, b.w)
            for k, v in b.r.items():
                self._wait(e, (k, v))

    def _mark(self, ev, reads, writes):
        k, v = ev
        for b in reads:
            b.r[k] = v
        for b in writes:
            b.w = ev
            b.r = {}

    def op(self, e, fn, reads=(), writes=()):
        self._deps(e, reads, writes)
        inst = fn(self.eng[e])
        self.cnt[e] += 1
        inst.then_inc(self.sem[e], 1)
        self.ninst += 1
        self._mark((e, self.cnt[e]), reads, writes)
        return inst

    def dma(self, q, out, in_=None, reads=(), writes=(), **kw):
        pairs = out if in_ is None else [(out, in_)]
        self._deps(q, reads, writes)
        kind = "sw" if q == "pool" else "hw"
        pool_ = self.dpool[kind]
        k = pool_[self.dnext[kind]]
        self.dnext[kind] = (self.dnext[kind] + 1) % len(pool_)
        if self.cnt[k] > 0:
            self._wait(q, (k, self.cnt[k]))
        for (o, i) in pairs:
            inst = self.eng[q].dma_start(out=o, in_=i, **kw)
            self.cnt[k] += 16
            inst.then_inc(self.sem[k], 16)
            self.ninst += 1
        self._mark((k, self.cnt[k]), reads, writes)
        return inst

    def finish(self, bufs, e="sp"):
        for b in bufs:
            self._wait(e, b.w)
        for k in self.dsem:
            if self.cnt[k] > 0:
                self._wait(e, (k, self.cnt[k]))


def _barrier(self):
    for e in self.ENG:
        for k in list(self.ENG) + self.dsem + (["cc"] if "cc" in self.sem else []):
            if k == e:
                continue
            if self.cnt[k] > 0:
                self._wait(e, (k, self.cnt[k]))


FW.barrier = _barrier


def _collective(self, kind, in_ap, out_ap, groups, reads=(), writes=()):
    q = "pool"
    self._deps(q, reads, writes)
    if "cc" not in self.sem:
        self.sem["cc"] = self.es0.enter_context(self.nc.semaphore("s_cc"))
        self.cnt["cc"] = 0
    inst = self.nc.gpsimd.collective_compute(kind, ALU.bypass, replica_groups=groups, ins=[in_ap], outs=[out_ap])
    self.cnt["cc"] += 1
    inst.then_inc(self.sem["cc"], 1)
    self.ninst += 1
    self._mark(("cc", self.cnt["cc"]), reads, writes)
    return inst


FW.collective = _collective

D = 1024
DFF = 2816
NFC = 22
RMS_EPS = 1e-6


def make_ident(fw, dtype=BF16):
    identf = fw.sbuf([128, 128], F32, "identf")
    fw.op("pool", lambda e: e.memset(identf[:], 0.0), writes=[identf])
    fw.op("pool", lambda e: e.affine_select(out=identf[:], in_=identf[:], pattern=[[-1, 128]],
                                            compare_op=ALU.not_equal, fill=1.0, base=0,
                                            channel_multiplier=1), reads=[identf], writes=[identf])
    ident = fw.sbuf([128, 128], BF16, "ident")
    fw.op("dve", lambda e: e.tensor_copy(out=ident[:], in_=identf[:]), reads=[identf], writes=[ident])
    return ident, identf


def phase_f(fw, ident, hin_fn, mix_fn, wo, gffn, wg, wu, wd, cw, cb, gnext, hout, tail, nout, xnS, n_tiles, final):
    nc = fw.nc
    with ExitStack() as es:
        old_es = fw.es
        fw.es = es
        wo_sb = fw.sbuf([128, 8, 1024], BF16, "wo_sb")
        wd_sb = fw.sbuf([128, NFC, 1024], BF16, "wd_sb")
        gB = fw.sbuf([128, 1024], F32, "gB")
        gnB = fw.sbuf([128, 1024], F32, "gnB")
        cw_sb = fw.sbuf([128, NFC, 3], F32, "cw_sb")
        cb_sb = fw.sbuf([128, NFC], F32, "cb_sb")
        wgs = [fw.sbuf([128, 2, 8, 128], BF16, "wgs%d" % i) for i in range(2)]
        wus = [fw.sbuf([128, 2, 8, 128], BF16, "wus%d" % i) for i in range(2)]
        hx = [fw.sbuf([128, 1024], F32, "hx%d" % i) for i in range(2)]
        h1 = [fw.sbuf([128, 1024], F32, "h1_%d" % i) for i in range(4)]
        xn = [fw.sbuf([128, 1024], BF16, "xn%d" % i) for i in range(2)]
        xnT = fw.sbuf([128, 8, 512], BF16, "xnT")
        mx = fw.sbuf([128, 8, 512], BF16, "mx")
        aT = [fw.sbuf([128, 512], BF16, "aT%d" % i) for i in range(NFC)]
        G = [fw.sbuf([128, 514], F32, "G%d" % i) for i in range(2)]
        t1 = [fw.sbuf([128, 512], F32, "t1_%d" % i) for i in range(2)]
        sl = [fw.sbuf([128, 512], F32, "sl%d" % i) for i in range(2)]
        H = fw.sbuf([128, NFC, 2], F32, "H")
        nout_dtype = F32 if final else BF16
        no = [fw.sbuf([128, 1024], nout_dtype, "no%d" % i) for i in range(2)]
        nstg = [fw.sbuf([128, 8, 512], BF16, "nstg%d" % i) for i in range(2)] if not final else None
        stat = [fw.sbuf([128, 4], F32, "stat%d" % i) for i in range(2)]
        stat2 = [fw.sbuf([128, 4], F32, "stat2_%d" % i) for i in range(2)]
        sq = fw.sbuf([128, 1024], F32, "sqjunk")
        pj = [fw.psum([128, 512], F32, "pj%d" % i) for i in range(2)]
        tp = fw.psum([128, 1024], BF16, "tp")
        pg = [fw.psum([128, 512], F32, "pg%d" % i) for i in range(2)]
        pu = [fw.psum([128, 512], F32, "pu%d" % i) for i in range(2)]

        wo_v = wo.t.rearrange("(kc p) n -> p kc n", p=128)
        fw.dma("pool", [(wo_sb[:, kc:kc + 2, :], wo_v[:, kc:kc + 2, :]) for kc in range(0, 8, 2)], writes=[wo_sb])
        wd_v = wd.t.rearrange("(fc p) n -> p fc n", p=128)
        fw.dma("pool", [(wd_sb[:, fc:fc + 2, :], wd_v[:, fc:fc + 2, :]) for fc in range(0, NFC, 2)], writes=[wd_sb])
        fw.dma("sp", gB[:], gffn.t.partition_broadcast(128), writes=[gB])
        fw.dma("sp", gnB[:], gnext.t.partition_broadcast(128), writes=[gnB])
        fw.dma("sp", cw_sb[:], cw.t[:, :, :], writes=[cw_sb])
        fw.dma("sp", cb_sb[:], cb.t[:, :], writes=[cb_sb])

        nsub_total = 1 + 4 * n_tiles
        wcount = [0]

        def front(s_glob, slot):
            hxb = hx[s_glob % 2]
            h1b = h1[slot]
            xnb = xn[s_glob % 2]
            stb = stat[s_glob % 2]
            fw.dma("sp", hxb[:], hin_fn(s_glob), writes=[hxb])
            for half in range(2):
                for kc in range(8):
                    fw.op("pe", lambda e: e.matmul(pj[half][:], lhsT=mx[:, kc, slot * 128:(slot + 1) * 128],
                                                   rhs=wo_sb[:, kc, half * 512:(half + 1) * 512],
                                                   start=(kc == 0), stop=(kc == 7)),
                          reads=[mx, wo_sb], writes=[pj[half]])
            for half in range(2):
                fw.op("dve", lambda e: e.tensor_tensor(out=h1b[:, half * 512:(half + 1) * 512],
                                                       in0=hxb[:, half * 512:(half + 1) * 512],
                                                       in1=pj[half][:], op=ALU.add),
                      reads=[hxb, pj[half]], writes=[h1b])
            rms(h1b, stb)
            fw.op("dve", lambda e: e.scalar_tensor_tensor(out=xnb[:], in0=h1b[:], scalar=stb[:, 2:3], in1=gB[:],
                                                          op0=ALU.mult, op1=ALU.mult),
                  reads=[h1b, stb, gB], writes=[xnb])

        def front_b(s_glob, slot):
            xnb = xn[s_glob % 2]
            for kc in range(8):
                fw.op("pe", lambda e: e.transpose(tp[:, kc * 128:(kc + 1) * 128], xnb[:, kc * 128:(kc + 1) * 128], ident[:]),
                      reads=[xnb, ident], writes=[tp])
            fw.op("act", lambda e: e.activation(out=xnT[:, :, slot * 128:(slot + 1) * 128],
                                                in_=tp[:].rearrange("p (k t) -> p k t", k=8), func=AF.Copy),
                  reads=[tp], writes=[xnT])

        def rms(hb, stb):
            fw.op("dve", lambda e: e.memset(stb[:, 0:1], 0.0), writes=[stb])
            fw.op("act", lambda e: e.activation(out=sq[:], in_=hb[:], func=AF.Square, accum_out=stb[:, 0:1]),
                  reads=[hb], writes=[sq, stb])
            fw.op("act", lambda e: e.activation(out=stb[:, 1:2], in_=stb[:, 0:1], func=AF.Sqrt, scale=1.0 / D, bias=RMS_EPS),
                  reads=[stb], writes=[stb])
            fw.op("dve", lambda e: e.reciprocal(out=stb[:, 2:3], in_=stb[:, 1:2]), reads=[stb], writes=[stb])

        def load_w(j):
            sl_ = wcount[0] % 2
            wcount[0] += 1
            fw.dma("pool", wgs[sl_][:], wg.t[j], writes=[wgs[sl_]])
            fw.dma("pool", wus[sl_][:], wu.t[j], writes=[wus[sl_]])
            return sl_

        fw.dma("sp", [(mx[p0:p1, k0:k1, 0:128], src) for (p0, p1, k0, k1, src) in mix_fn(0, 128)], writes=[mx])
        front(0, 0)
        front_b(0, 0)
        for j in range(NFC // 2):
            ws = load_w(j)
            for jj in range(2):
                fc = 2 * j + jj
                pgb = pg[fc % 2]
                for kc in range(8):
                    fw.op("pe", lambda e: e.matmul(pgb[:, 0:2], lhsT=wgs[ws][:, jj, kc, :], rhs=xnT[:, kc, 126:128],
                                                   start=(kc == 0), stop=(kc == 7)),
                          reads=[wgs[ws], xnT], writes=[pgb])
                fw.op("act", lambda e: e.activation(out=H[:, fc, :], in_=pgb[:, 0:2], func=AF.Copy),
                      reads=[pgb], writes=[H])

        for ti in range(n_tiles):
            c0 = 128 + ti * 512
            fw.dma("sp", [(mx[p0:p1, k0:k1, :], src) for (p0, p1, k0, k1, src) in mix_fn(c0, 512)], writes=[mx])
            for s in range(4):
                front(1 + ti * 4 + s, s)
                if s > 0:
                    front_b(1 + ti * 4 + s - 1, s - 1)
            front_b(1 + ti * 4 + 3, 3)
            for j in range(NFC // 2):
                ws = load_w(j)
                for jj in range(2):
                    fc = 2 * j + jj
                    pgb = pg[fc % 2]
                    pub = pu[fc % 2]
                    Gb = G[fc % 2]
                    t1b = t1[fc % 2]
                    slb = sl[fc % 2]
                    for kc in range(8):
                        fw.op("pe", lambda e: e.matmul(pgb[:], lhsT=wgs[ws][:, jj, kc, :], rhs=xnT[:, kc, :],
                                                       start=(kc == 0), stop=(kc == 7)),
                              reads=[wgs[ws], xnT], writes=[pgb])
                    for kc in range(8):
                        fw.op("pe", lambda e: e.matmul(pub[:], lhsT=wus[ws][:, jj, kc, :], rhs=xnT[:, kc, :],
                                                       start=(kc == 0), stop=(kc == 7)),
                              reads=[wus[ws], xnT], writes=[pub])
                    fw.op("act", lambda e: e.activation(out=Gb[:, 0:2], in_=H[:, fc, :], func=AF.Copy),
                          reads=[H], writes=[Gb])
                    fw.op("act", lambda e: e.activation(out=Gb[:, 2:514], in_=pgb[:], func=AF.Copy), reads=[pgb], writes=[Gb])
                    fw.op("act", lambda e: e.activation(out=H[:, fc, :], in_=Gb[:, 512:514], func=AF.Copy),
                          reads=[Gb], writes=[H])
                    fw.op("act", lambda e: e.activation(out=t1b[:], in_=Gb[:, 0:512], func=AF.Copy, scale=cw_sb[:, fc, 0:1]),
                          reads=[Gb, cw_sb], writes=[t1b])
                    fw.op("dve", lambda e: e.scalar_tensor_tensor(out=t1b[:], in0=Gb[:, 1:513], scalar=cw_sb[:, fc, 1:2],
                                                                  in1=t1b[:], op0=ALU.mult, op1=ALU.add),
                          reads=[Gb, cw_sb, t1b], writes=[t1b])
                    fw.op("dve", lambda e: e.scalar_tensor_tensor(out=t1b[:], in0=Gb[:, 2:514], scalar=cw_sb[:, fc, 2:3],
                                                                  in1=t1b[:], op0=ALU.mult, op1=ALU.add),
                          reads=[Gb, cw_sb, t1b], writes=[t1b])
                    fw.op("act", lambda e: e.activation(out=slb[:], in_=t1b[:], func=AF.Silu, bias=cb_sb[:, fc:fc + 1], scale=1.0),
                          reads=[t1b, cb_sb], writes=[slb])
                    fw.op("dve", lambda e: e.tensor_tensor(out=aT[fc][:], in0=slb[:], in1=pub[:], op=ALU.mult),
                          reads=[slb, pub], writes=[aT[fc]])
            for s in range(4):
                h1b = h1[s]
                r0 = ti * 512 + s * 128
                sidx = ti * 4 + s
                for half in range(2):
                    for fc in range(NFC):
                        fw.op("pe", lambda e: e.matmul(pj[half][:], lhsT=aT[fc][:, s * 128:(s + 1) * 128],
                                                       rhs=wd_sb[:, fc, half * 512:(half + 1) * 512],
                                                       start=(fc == 0), stop=(fc == NFC - 1)),
                              reads=[aT[fc], wd_sb], writes=[pj[half]])
                for half in range(2):
                    fw.op("dve", lambda e: e.tensor_tensor(out=h1b[:, half * 512:(half + 1) * 512],
                                                           in0=h1b[:, half * 512:(half + 1) * 512],
                                                           in1=pj[half][:], op=ALU.add),
                          reads=[h1b, pj[half]], writes=[h1b])
                if hout is not None:
                    fw.dma("sp", hout.t[r0:r0 + 128, :], h1b[:], reads=[h1b])
                    if tail is not None and ti == n_tiles - 1 and s == 3:
                        fw.dma("sp", tail.t[:, :], h1b[:], reads=[h1b])
                stb = stat2[sidx % 2]
                nob = no[sidx % 2]
                rms(h1b, stb)
                fw.op("dve", lambda e: e.scalar_tensor_tensor(out=nob[:], in0=h1b[:], scalar=stb[:, 2:3], in1=gnB[:],
                                                              op0=ALU.mult, op1=ALU.mult),
                      reads=[h1b, stb, gnB], writes=[nob])
                if final:
                    fw.dma("sp", nout.t[r0:r0 + 128, :], nob[:], reads=[nob])
                else:
                    for kc in range(8):
                        fw.op("pe", lambda e: e.transpose(tp[:, kc * 128:(kc + 1) * 128], nob[:, kc * 128:(kc + 1) * 128], ident[:]),
                              reads=[nob, ident], writes=[tp])
                    ns = nstg[ti % 2]
                    fw.op("act", lambda e: e.activation(out=ns[:, :, s * 128:(s + 1) * 128],
                                                        in_=tp[:].rearrange("p (k t) -> p k t", k=8), func=AF.Copy),
                          reads=[tp], writes=[ns])
                    if s == 3:
                        fw.dma("sp", xnS.t.rearrange("kc p t -> p kc t")[:, :, ti * 512:(ti + 1) * 512], ns[:], reads=[ns])
        fw.barrier()
        fw.es = old_es

D = 1024
S = 8192
NH = 8
DH = 64
BLK = 256
NB = S // BLK
BIG = 30000.0
SCALE = 1.0 / math.sqrt(DH)
RMS_EPS = 1e-6


def phase_a(fw, ident, identf, x, gmix, wqkv, cos, sinm, QT, KT, out_fn, n_heads=NH, n_sub=S // 128):
    S_loc = n_sub * 128
    NBL = S_loc // BLK
    nc = fw.nc
    nqt = n_sub // 4
    with ExitStack() as es0:
        old_es = fw.es
        fw.es = es0
        V_sb = fw.sbuf([128, n_sub, NH, 65], BF16, "V_sb")
        ssq_q = fw.sbuf([128, n_sub, NH], F32, "ssq_q")
        kmax2 = fw.sbuf([128, NH], F32, "kmax2")
        fw.op("pool", lambda e: e.memset(V_sb[:], 1.0), writes=[V_sb])
        fw.op("pool", lambda e: e.memset(kmax2[:], 0.0), writes=[kmax2])
        with ExitStack() as es1:
            fw.es = es1
            w_sb = fw.sbuf([128, 8, 1536], BF16, "wqkv_sb")
            gB = fw.sbuf([128, 1024], F32, "gB")
            xb = [fw.sbuf([128, 1024], F32, "xb%d" % i) for i in range(2)]
            xn = [fw.sbuf([128, 1024], BF16, "xn%d" % i) for i in range(2)]
            xnT = [fw.sbuf([128, 8, 128], BF16, "xnT%d" % i) for i in range(2)]
            cs = [fw.sbuf([128, 2, 64], F32, "cs%d" % i) for i in range(2)]
            tcb = fw.sbuf([128, NH, 64], F32, "tcb")
            trb = fw.sbuf([128, NH, 64], F32, "trb")
            qk_tok = [fw.sbuf([128, 2, 512], BF16, "qktok%d" % i) for i in range(2)]
            stg = [fw.sbuf([128, 2, 4, 512], BF16, "stg%d" % i) for i in range(2)]
            sqj = fw.sbuf([128, 1024], F32, "sqj")
            sqq = fw.sbuf([128, 512], F32, "sqq")
            sqk = fw.sbuf([128, 512], F32, "sqk")
            ssk = fw.sbuf([128, NH], F32, "ssk")
            stat = [fw.sbuf([128, 4], F32, "stat%d" % i) for i in range(2)]
            tp = fw.psum([128, 1024], BF16, "tp")
            pqs = [fw.psum([128, 512], F32, "pq%d" % i) for i in range(2)]
            pks = [fw.psum([128, 512], F32, "pk%d" % i) for i in range(2)]
            pv = fw.psum([128, 512], F32, "pv")
            tqk = fw.psum([128, 2, 512], BF16, "tqk")

            w_v = wqkv.t.rearrange("(kc p) n -> p kc n", p=128)
            fw.dma("pool", [(w_sb[:, kc:kc + 2, :], w_v[:, kc:kc + 2, :]) for kc in range(0, 8, 2)], writes=[w_sb])
            fw.dma("sp", gB[:], gmix.t.partition_broadcast(128), writes=[gB])
            QT_v = QT.t.rearrange("(pr p) t -> p pr t", p=128)
            KT_v = KT.t.rearrange("(pr p) t -> p pr t", p=128)

            def stage1(st):
                b2 = st % 2
                r0 = st * 128
                fw.dma("sp", xb[b2][:], x.t[r0:r0 + 128, :], writes=[xb[b2]])
                fw.dma("sp", [(cs[b2][:, 0, :], cos.t[r0:r0 + 128, :]), (cs[b2][:, 1, :], sinm.t[r0:r0 + 128, :])], writes=[cs[b2]])
                stb = stat[b2]
                fw.op("dve", lambda e: e.memset(stb[:, 0:1], 0.0), writes=[stb])
                fw.op("act", lambda e: e.activation(out=sqj[:], in_=xb[b2][:], func=AF.Square, accum_out=stb[:, 0:1]),
                      reads=[xb[b2]], writes=[sqj, stb])
                fw.op("act", lambda e: e.activation(out=stb[:, 1:2], in_=stb[:, 0:1], func=AF.Sqrt, scale=1.0 / D, bias=RMS_EPS),
                      reads=[stb], writes=[stb])
                fw.op("dve", lambda e: e.reciprocal(out=stb[:, 2:3], in_=stb[:, 1:2]), reads=[stb], writes=[stb])
                fw.op("dve", lambda e: e.scalar_tensor_tensor(out=xn[b2][:], in0=xb[b2][:], scalar=stb[:, 2:3], in1=gB[:],
                                                              op0=ALU.mult, op1=ALU.mult),
                      reads=[xb[b2], stb, gB], writes=[xn[b2]])
                for kc in range(8):
                    fw.op("pe", lambda e: e.transpose(tp[:, kc * 128:(kc + 1) * 128], xn[b2][:, kc * 128:(kc + 1) * 128], ident[:]),
                          reads=[xn[b2], ident], writes=[tp])
                fw.op("act", lambda e: e.activation(out=xnT[b2][:], in_=tp[:].rearrange("p (k t) -> p k t", k=8), func=AF.Copy),
                      reads=[tp], writes=[xnT[b2]])
            def stage2(st):
                b2 = st % 2
                pq = pqs[st % 2]
                pk = pks[st % 2]
                for (pp, c0) in ((pq, 0), (pk, 512), (pv, 1024)):
                    for kc in range(8):
                        fw.op("pe", lambda e: e.matmul(pp[:], lhsT=xnT[b2][:, kc, :], rhs=w_sb[:, kc, c0:c0 + 512],
                                                       start=(kc == 0), stop=(kc == 7)),
                              reads=[xnT[b2], w_sb], writes=[pp])
                fw.op("act", lambda e: e.activation(out=V_sb[:, st, :, 0:64], in_=pv[:].rearrange("p (h d) -> p h d", h=NH), func=AF.Copy),
                      reads=[pv], writes=[V_sb])
                cosB = cs[b2][:, 0, :].unsqueeze(1).to_broadcast([128, NH, 64])
                sinB = cs[b2][:, 1, :].unsqueeze(1).to_broadcast([128, NH, 64])
                qkb = qk_tok[b2]
                for qi, pp in enumerate((pq, pk)):
                    pv3 = pp[:].rearrange("p (h d) -> p h d", h=NH)
                    sqx = sqq if qi == 0 else sqk
                    fw.op("act", lambda e: e.activation(out=sqx[:], in_=pp[:], func=AF.Square), reads=[pp], writes=[sqx])
                    if qi == 0:
                        fw.op("dve", lambda e: e.tensor_reduce(out=ssq_q[:, st, :], in_=sqx[:].rearrange("p (h d) -> p h d", h=NH),
                                                               axis=AX.X, op=ALU.add), reads=[sqx], writes=[ssq_q])
                    else:
                        fw.op("dve", lambda e: e.tensor_reduce(out=ssk[:], in_=sqx[:].rearrange("p (h d) -> p h d", h=NH),
                                                               axis=AX.X, op=ALU.add), reads=[sqx], writes=[ssk])
                        fw.op("dve", lambda e: e.tensor_tensor(out=kmax2[:], in0=kmax2[:], in1=ssk[:], op=ALU.max),
                              reads=[ssk, kmax2], writes=[kmax2])
                    fw.op("dve", lambda e: e.tensor_tensor(out=tcb[:], in0=pv3, in1=cosB, op=ALU.mult),
                          reads=[pp, cs[b2]], writes=[tcb])
                    fw.op("dve", lambda e: e.tensor_tensor(out=trb[:, :, 0:32], in0=pv3[:, :, 32:64], in1=sinB[:, :, 0:32], op=ALU.mult),
                          reads=[pp, cs[b2]], writes=[trb])
                    fw.op("dve", lambda e: e.tensor_tensor(out=trb[:, :, 32:64], in0=pv3[:, :, 0:32], in1=sinB[:, :, 32:64], op=ALU.mult),
                          reads=[pp, cs[b2]], writes=[trb])
                    fw.op("pool", lambda e: e.tensor_tensor(out=qkb[:, qi, :].rearrange("p (h d) -> p h d", h=NH), in0=tcb[:], in1=trb[:], op=ALU.add),
                          reads=[tcb, trb], writes=[qkb])
            def stage3(st):
                b2 = st % 2
                qkb = qk_tok[b2]
                for qi in range(2):
                    for pr in range(4):
                        fw.op("pe", lambda e: e.transpose(tqk[:, qi, pr * 128:(pr + 1) * 128], qkb[:, qi, pr * 128:(pr + 1) * 128], ident[:]),
                              reads=[qkb, ident], writes=[tqk])
                sg = stg[(st // 4) % 2]
                slot = st % 4
                fw.op("act", lambda e: e.activation(out=sg[:, :, :, slot * 128:(slot + 1) * 128],
                                                    in_=tqk[:].rearrange("p a (r t) -> p a r t", r=4), func=AF.Copy),
                      reads=[tqk], writes=[sg])
                if slot == 3:
                    t0 = (st // 4) * 512
                    fw.dma("sp", [(QT_v[:, :, t0:t0 + 512], sg[:, 0, :, :]), (KT_v[:, :, t0:t0 + 512], sg[:, 1, :, :])], reads=[sg])
            stage1(0)
            for st in range(n_sub):
                if st + 1 < n_sub:
                    stage1(st + 1)
                stage2(st)
                if st >= 1:
                    stage3(st - 1)
            stage3(n_sub - 1)
            fw.barrier()
        with ExitStack() as es2:
            fw.es = es2
            QA = [fw.sbuf([96, S_loc], BF16, "QA%d" % i) for i in range(2)]
            KA = [fw.sbuf([96, S_loc], BF16, "KA%d" % i) for i in range(2)]
            cm = [fw.sbuf([128, 512], BF16, "cm%d" % i) for i in range(4)]
            C2 = fw.sbuf([128, 64], F32, "C2")
            Dc = fw.sbuf([128, 64], F32, "Dc")
            Ec = fw.sbuf([128, 64], F32, "Ec")
            onesf = fw.sbuf([128, 128], F32, "onesf")
            kmf = fw.sbuf([96, NB], F32, "kmf")
            kmT = fw.sbuf([96, NB], BF16, "kmT")
            kmx = fw.sbuf([128, NH], F32, "kmx")
            kmxT = fw.sbuf([NH, 128], F32, "kmxT")
            kmr = fw.sbuf([NH, 2], F32, "kmr")
            kdiag = fw.sbuf([NH, NH], F32, "kdiag")
            stabm = fw.sbuf([128, n_sub, NH], F32, "stabm")
            gm = [fw.sbuf([128, NB], F32, "gm%d" % i) for i in range(2)]
            top8 = [fw.sbuf([128, 8], F32, "top8_%d" % i) for i in range(2)]
            s1 = [fw.sbuf([128, NB], F32, "s1_%d" % i) for i in range(2)]
            nmk = [fw.sbuf([128, 96], BF16, "nmk%d" % i) for i in range(2)]
            PT = [fw.sbuf([128, 512], BF16, "PT%d" % i) for i in range(3)]
            rd = [fw.sbuf([128, 512], F32, "rd%d" % i) for i in range(2)]
            bcs = fw.sbuf([64, 512], F32, "bcs")
            at = [fw.sbuf([64, 512], BF16, "at%d" % i) for i in range(2)]
            ps_s = [fw.psum([128, 512], F32, "ps_s%d" % i) for i in range(3)]
            ps_o = [fw.psum([128, 512], F32, "ps_o%d" % i) for i in range(2)]
            ps_b = fw.psum([128, 512], F32, "ps_b")
            ps_g = fw.psum([128, 512], F32, "ps_g")
            ps_t = fw.psum([128, 1024], BF16, "ps_t")

            for m in range(4):
                fw.op("pool", lambda e: e.memset(cm[m][:], 1.0), writes=[cm[m]])
                fw.op("pool", lambda e: e.affine_select(out=cm[m][:], in_=cm[m][:], pattern=[[1, 512]], compare_op=ALU.is_ge,
                                                        fill=0.0, base=-128 * m, channel_multiplier=-1),
                      reads=[cm[m]], writes=[cm[m]])
            fw.op("dve", lambda e: e.memset(C2[:, 0:32], 0.0), writes=[C2])
            fw.op("dve", lambda e: e.memset(C2[:, 32:64], -BIG), writes=[C2])
            fw.op("dve", lambda e: e.memset(Dc[:, 0:32], -2 * BIG), writes=[Dc])
            fw.op("dve", lambda e: e.memset(Dc[:, 32:64], 0.0), writes=[Dc])
            fw.op("dve", lambda e: e.memset(Ec[:, 0:33], 0.0), writes=[Ec])
            fw.op("dve", lambda e: e.memset(Ec[:, 33:64], -BIG), writes=[Ec])
            fw.op("dve", lambda e: e.memset(onesf[:], 1.0), writes=[onesf])
            fw.op("dve", lambda e: e.memset(kmT[:], 0.0), writes=[kmT])
            for i in range(2):
                fw.op("dve", lambda e: e.memset(nmk[i][:], 0.0), writes=[nmk[i]])
            for i in range(2):
                fw.op("pool", lambda e: e.memset(QA[i][64:96, :], 0.0), writes=[QA[i]])
                fw.op("pool", lambda e: e.memset(KA[i][64:96, :], 1.0), writes=[KA[i]])
                fw.op("pool", lambda e: e.affine_select(out=KA[i][64:96, :], in_=KA[i][64:96, :], pattern=[[1, S_loc]], compare_op=ALU.is_ge,
                                                        fill=0.0, base=0, channel_multiplier=-BLK), reads=[KA[i]], writes=[KA[i]])
                fw.op("pool", lambda e: e.affine_select(out=KA[i][64:96, :], in_=KA[i][64:96, :], pattern=[[-1, S_loc]], compare_op=ALU.is_ge,
                                                        fill=0.0, base=BLK - 1, channel_multiplier=BLK), reads=[KA[i]], writes=[KA[i]])
            fw.op("pe", lambda e: e.transpose(ps_g[0:NH, 0:128], kmax2[:], identf[:]), reads=[kmax2, identf], writes=[ps_g])
            fw.op("dve", lambda e: e.tensor_reduce(out=kmr[:, 0:1], in_=ps_g[0:NH, 0:128], axis=AX.X, op=ALU.max), reads=[ps_g], writes=[kmr])
            fw.op("dve", lambda e: e.tensor_scalar(out=kdiag[:], in0=identf[0:NH, 0:NH], scalar1=kmr[:, 0:1], scalar2=None, op0=ALU.mult),
                  reads=[identf, kmr], writes=[kdiag])
            fw.op("pe", lambda e: e.matmul(ps_b[:, 0:NH], lhsT=onesf[0:NH, :], rhs=kdiag[:], start=True, stop=True),
                  reads=[kdiag, onesf], writes=[ps_b])
            fw.op("dve", lambda e: e.tensor_copy(out=kmx[:], in_=ps_b[:, 0:NH]), reads=[ps_b], writes=[kmx])
            fw.op("dve", lambda e: e.tensor_tensor(out=stabm[:], in0=ssq_q[:], in1=kmx[:].unsqueeze(1).to_broadcast([128, n_sub, NH]), op=ALU.mult),
                  reads=[ssq_q, kmx], writes=[stabm])
            fw.op("act", lambda e: e.activation(out=stabm[:], in_=stabm[:], func=AF.Sqrt), reads=[stabm], writes=[stabm])
            fw.op("dve", lambda e: e.tensor_scalar(out=stabm[:], in0=stabm[:], scalar1=-1.0, scalar2=None, op0=ALU.mult),
                  reads=[stabm], writes=[stabm])

            def load_head(h):
                b = h % 2
                fw.dma("sp", QA[b][0:64, :], QT.t[h * 64:(h + 1) * 64, :], writes=[QA[b]])
                fw.dma("sp", KA[b][0:64, :], KT.t[h * 64:(h + 1) * 64, :], writes=[KA[b]])

            def gating(h):
                b = h % 2
                Qa, Ka = QA[b], KA[b]
                fw.op("dve", lambda e: e.tensor_reduce(out=kmf[0:64, 0:NBL], in_=Ka[0:64, :].rearrange("p (j t) -> p j t", t=BLK),
                                                       axis=AX.X, op=ALU.add), reads=[Ka], writes=[kmf])
                fw.op("act", lambda e: e.activation(out=kmT[0:64, 0:NBL], in_=kmf[0:64, 0:NBL], func=AF.Copy, scale=1.0 / BLK),
                      reads=[kmf], writes=[kmT])
                for qs in range(n_sub):
                    qb = qs // 2
                    g16 = qs % 16
                    if g16 == 0:
                        pass
                    fw.op("pe", lambda e: e.matmul(ps_g[:, g16 * NB:(g16 + 1) * NB], lhsT=Qa[:, qs * 128:(qs + 1) * 128], rhs=kmT[:],
                                                   start=True, stop=True), reads=[Qa, kmT], writes=[ps_g])
                    if g16 == 15 or qs == n_sub - 1:
                        for q2 in range(qs - g16, qs + 1):
                            qb2 = q2 // 2
                            gg = q2 % 16
                            i2 = q2 % 2
                            lo = 32 - qb2
                            fw.op("dve", lambda e: e.tensor_tensor(out=gm[i2][:], in0=ps_g[:, gg * NB:(gg + 1) * NB], in1=C2[:, lo:lo + NB], op=ALU.add),
                                  reads=[ps_g, C2], writes=[gm[i2]])
                            fw.op("dve", lambda e: e.max(out=top8[i2][:], in_=gm[i2][:]), reads=[gm[i2]], writes=[top8[i2]])
                            fw.op("dve", lambda e: e.tensor_scalar(out=s1[i2][:], in0=gm[i2][:], scalar1=top8[i2][:, 2:3], scalar2=BIG,
                                                                   op0=ALU.is_ge, op1=ALU.mult),
                                  reads=[gm[i2], top8[i2]], writes=[s1[i2]])
                            fw.op("dve", lambda e: e.scalar_tensor_tensor(out=s1[i2][:], in0=s1[i2][:], scalar=-BIG, in1=Dc[:, lo:lo + NB],
                                                                          op0=ALU.add, op1=ALU.max),
                                  reads=[s1[i2], Dc], writes=[s1[i2]])
                            fw.op("dve", lambda e: e.scalar_tensor_tensor(out=nmk[i2][:, 64:96], in0=s1[i2][:], scalar=stabm[:, q2, h:h + 1], in1=Ec[:, lo:lo + NB],
                                                                          op0=ALU.add, op1=ALU.add),
                                  reads=[s1[i2], stabm, Ec], writes=[nmk[i2]])
                            t8 = q2 % 8
                            fw.op("pe", lambda e: e.transpose(ps_t[0:96, t8 * 128:(t8 + 1) * 128], nmk[i2][:], ident[:]),
                                  reads=[nmk[i2], ident], writes=[ps_t])
                            if t8 == 7 or q2 == n_sub - 1:
                                c0 = (q2 - t8) * 128
                                fw.op("act", lambda e: e.activation(out=Qa[64:96, c0:c0 + (t8 + 1) * 128], in_=ps_t[64:96, 0:(t8 + 1) * 128], func=AF.Copy),
                                      reads=[ps_t], writes=[Qa])
                            yield

            load_head(0)
            for _ in gating(0):
                pass
            ev = 0
            for h in range(n_heads):
                b = h % 2
                Qa, Ka = QA[b], KA[b]
                if h + 1 < n_heads:
                    load_head(h + 1)
                iters = [(qt, kt) for qt in range(nqt) for kt in range(4 * (qt + 1))]
                LA = 2
                pend = []

                def emit_S(i):
                    qt, kt = iters[i]
                    pss = ps_s[i % 3]
                    ptb = PT[i % 3]
                    fw.op("pe", lambda e: e.matmul(pss[:], lhsT=Ka[:, kt * 128:(kt + 1) * 128], rhs=Qa[:, qt * 512:(qt + 1) * 512],
                                                   start=True, stop=True), reads=[Ka, Qa], writes=[pss])
                    fw.op("act", lambda e: e.activation(out=ptb[:], in_=pss[:], func=AF.Exp, scale=SCALE), reads=[pss], writes=[ptb])
                    m = kt - 4 * qt
                    if m >= 0:
                        fw.op("pool", lambda e: e.tensor_tensor(out=ptb[:], in0=ptb[:], in1=cm[m][:], op=ALU.mult),
                              reads=[ptb, cm[m]], writes=[ptb])

                def emit_PV(i):
                    qt, kt = iters[i]
                    nkt = 4 * (qt + 1)
                    po = ps_o[qt % 2]
                    ptb = PT[i % 3]
                    fw.op("pe", lambda e: e.matmul(po[0:65, :], lhsT=V_sb[:, kt, h, :], rhs=ptb[:], start=(kt == 0), stop=(kt == nkt - 1)),
                          reads=[V_sb, ptb], writes=[po])
                    if kt == nkt - 1:
                        rdb = rd[qt % 2]
                        fw.op("dve", lambda e: e.reciprocal(out=rdb[64:65, :], in_=po[64:65, :]), reads=[po], writes=[rdb])

                        def tail(qt=qt, po=po, rdb=rdb):
                            fw.op("pe", lambda e: e.matmul(ps_b[0:64, :], lhsT=onesf[64:65, 0:64], rhs=rdb[64:65, :], start=True, stop=True),
                                  reads=[rdb, onesf], writes=[ps_b])
                            fw.op("dve", lambda e: e.tensor_copy(out=bcs[:], in_=ps_b[0:64, :]), reads=[ps_b], writes=[bcs])
                            ab = at[qt % 2]
                            fw.op("dve", lambda e: e.tensor_tensor(out=ab[:], in0=po[0:64, :], in1=bcs[:], op=ALU.mult), reads=[po, bcs], writes=[ab])
                            fw.dma("sp", [(dst, ab[:, c_a:c_b]) for (dst, c_a, c_b) in out_fn(h, qt)], reads=[ab])
                        pend.append([3, tail])

                gnext = gating(h + 1) if h + 1 < n_heads else None
                for i in range(len(iters) + LA):
                    if gnext is not None and i % 6 == 5:
                        try:
                            next(gnext)
                        except StopIteration:
                            gnext = None
                    if i < len(iters):
                        emit_S(i)
                    if i - LA >= 0:
                        emit_PV(i - LA)
                    for pe_ in list(pend):
                        pe_[0] -= 1
                        if pe_[0] <= 0:
                            pend.remove(pe_)
                            pe_[1]()
                for pe_ in pend:
                    pe_[1]()
                if gnext is not None:
                    for _ in gnext:
                        pass
            fw.barrier()
        fw.es = old_es

D = 1024
T = 128
C0 = math.exp(-0.5)
LNX_EPS = 64e-5


def phase_r(fw, identb, identf, x_fn, mu, wr, wk, wv, w1, a1, g1, w2, a2, g2, pf, lnxw, lnxb, zout_fn, n_tiles, stop=99):
    nc = fw.nc
    with ExitStack() as es:
        old_es = fw.es
        fw.es = es
        sb = fw.sbuf
        wr_sb = sb([128, 8, 512], BF16, "wr_sb"); wk_sb = sb([128, 8, 512], BF16, "wk_sb"); wv_sb = sb([128, 8, 512], BF16, "wv_sb")
        w1_sb = sb([128, 8, 64], BF16, "w1_sb"); a1_sb = sb([128, 8, 64], BF16, "a1_sb"); g1_sb = sb([128, 8, 128], BF16, "g1_sb")
        w2_sb = sb([64, 512], BF16, "w2_sb"); a2_sb = sb([64, 512], BF16, "a2_sb"); g2_sb = sb([128, 512], BF16, "g2_sb")
        mu_sb = sb([128, 6, 8], F32, "mu_sb"); pf_sb = sb([128, 4, 5], F32, "pf_sb")
        lwB = sb([128, 512], F32, "lwB"); lbB = sb([128, 512], F32, "lbB")
        for (dst, src) in ((wr_sb, wr), (wk_sb, wk), (wv_sb, wv), (w1_sb, w1), (a1_sb, a1), (g1_sb, g1)):
            v = src.t.rearrange("(kc p) n -> p kc n", p=128)
            fw.dma("pool", [(dst[:, 0:4, :], v[:, 0:4, :]), (dst[:, 4:8, :], v[:, 4:8, :])], writes=[dst])
        fw.dma("pool", w2_sb[:], w2.t[:, :], writes=[w2_sb])
        fw.dma("pool", a2_sb[:], a2.t[:, :], writes=[a2_sb])
        fw.dma("pool", g2_sb[:], g2.t[:, :], writes=[g2_sb])
        fw.dma("sp", mu_sb[:], mu.t[:, :, :], writes=[mu_sb])
        fw.dma("sp", pf_sb[:], pf.t[:, :, :], writes=[pf_sb])
        fw.dma("sp", lwB[:], lnxw.t.partition_broadcast(128), writes=[lwB])
        fw.dma("sp", lbB[:], lnxb.t.partition_broadcast(128), writes=[lbB])
        rmask = sb([128, 512], F32, "rmask")
        fw.op("pool", lambda e: e.memset(rmask[:], 1.0), writes=[rmask])
        for ch in range(4):
            fw.op("pool", lambda e: e.memset(rmask[:, ch * T:ch * T + 1], 0.0), writes=[rmask])
        MK_SI = sb([128, 2, 2, 128], BF16, "MK_SI")
        MK_SL = sb([128, 4, 128], BF16, "MK_SL")
        fw.op("pool", lambda e: e.memset(MK_SI[:], 1.0), writes=[MK_SI])
        fw.op("pool", lambda e: e.memset(MK_SL[:], 1.0), writes=[MK_SL])
        for cb in range(2):
            fw.op("pool", lambda e: e.affine_select(out=MK_SI[:, cb, 0, :], in_=MK_SI[:, cb, 0, :], pattern=[[1, 128]], compare_op=ALU.is_gt,
                                                    fill=0.0, base=0, channel_multiplier=-1), reads=[MK_SI], writes=[MK_SI])
            fw.op("pool", lambda e: e.affine_select(out=MK_SI[:, cb, 1, :], in_=MK_SI[:, cb, 1, :], pattern=[[1, 128]], compare_op=ALU.is_ge,
                                                    fill=0.0, base=0, channel_multiplier=-1), reads=[MK_SI], writes=[MK_SI])
        for ch in range(4):
            fw.op("pool", lambda e: e.affine_select(out=MK_SL[:, ch, :], in_=MK_SL[:, ch, :], pattern=[[-1, 128]], compare_op=ALU.is_gt,
                                                    fill=0.0, base=0, channel_multiplier=1), reads=[MK_SL], writes=[MK_SL])
        Ind8 = sb([128, 4, 8], BF16, "Ind8")
        fw.op("pool", lambda e: e.memset(Ind8[:], 0.0), writes=[Ind8])
        for c in range(4):
            fw.op("pool", lambda e: e.memset(Ind8[0:64, c, 2 * c:2 * c + 1], 1.0), writes=[Ind8])
            fw.op("pool", lambda e: e.memset(Ind8[64:128, c, 2 * c + 1:2 * c + 2], 1.0), writes=[Ind8])
        BOnes = sb([128, 128], BF16, "BOnes")
        fw.op("pool", lambda e: e.memset(BOnes[:], 0.0), writes=[BOnes])
        fw.op("pool", lambda e: e.memset(BOnes[0:64, 0:64], 1.0), writes=[BOnes])
        fw.op("pool", lambda e: e.memset(BOnes[64:128, 64:128], 1.0), writes=[BOnes])
        xTb = [sb([128, 8, 514], BF16, "xTb0")] * 2
        xxa = sb([128, 8, 512], BF16, "xxa")
        mix = [sb([128, 8, 512], BF16, "mix%d" % i) for i in range(2)]
        thw = sb([64, 512], BF16, "thw"); tha = sb([64, 512], BF16, "tha"); thg = sb([128, 512], BF16, "thg")
        AR = sb([128, 4, 4, 2, 128], BF16, "AR")
        BT = sb([128, 4, 512], BF16, "BT"); KTt = sb([128, 4, 512], BF16, "KTt")
        bpT = [sb([128, 512], BF16, "bpT%d" % i) for i in range(2)]
        kpT = [sb([128, 512], BF16, "kpT%d" % i) for i in range(2)]
        Atok = sb([128, 4, 512], BF16, "Atok"); Bp = sb([128, 4, 512], BF16, "Bp"); Kp = sb([128, 4, 512], BF16, "Kp")
        Vtoks = [sb([128, 4, 512], BF16, "Vtok%d" % i) for i in range(2)]; gtoks = [sb([128, 4, 512], BF16, "gtok%d" % i) for i in range(2)]
        prodT = sb([128, 4, 512], BF16, "prodT")
        rkss = [sb([128, 4, 8], F32, "rks%d" % i) for i in range(2)]
        Gend = sb([128, 4, 4], F32, "Gend")
        Ysb = sb([128, 4, 512], F32, "Ysb")
        sg = sb([128, 512], F32, "sg"); al = sb([128, 512], F32, "al")
        Ec = sb([128, 512], F32, "Ec"); Em = sb([128, 512], F32, "Em"); Ee = sb([128, 512], F32, "Ee")
        e1 = Em; e2 = sb([128, 512], F32, "e2"); e3 = sb([128, 512], F32, "e3"); e4 = Ee
        rsb = sb([128, 512], F32, "rsb"); ksb = sb([128, 512], F32, "ksb"); kkf = sb([128, 512], F32, "kkf"); sqb = sb([128, 512], BF16, "sqb"); nrm = sb([128, 512], F32, "nrm")
        kkn = sb([128, 512], F32, "kkn"); tm1 = sb([128, 512], F32, "tm1"); kmod = sb([128, 512], F32, "kmod"); ka = sb([128, 512], F32, "ka")
        Zs = [sb([128, 4, 64], BF16, "Zs%d" % i) for i in range(2)]
        for i in range(2):
            fw.op("pool", lambda e: e.memset(Zs[i][:], 0.0), writes=[Zs[i]])
        NM1 = [sb([128, 4, 2, 128], BF16, "NM1_%d" % i) for i in range(2)]
        LM = [sb([128, 4, 2, 128], BF16, "LM_%d" % i) for i in range(2)]
        Npp = [[sb([128, 4, 128], BF16, "N_%d_%d" % (i, j)) for j in range(2)] for i in range(2)]
        Ntp = [[sb([128, 4, 128], BF16, "Nt_%d_%d" % (i, j)) for j in range(2)] for i in range(2)]
        Xp = [[sb([128, 4, 128], BF16, "X_%d_%d" % (i, j)) for j in range(2)] for i in range(2)]
        GT = [sb([128, 4, 128], BF16, "GT%d" % i) for i in range(2)]
        PTm = [sb([128, 4, 64], BF16, "PTm%d" % i) for i in range(2)]
        for i in range(2):
            fw.op("pool", lambda e: e.memset(GT[i][:], 0.0), writes=[GT[i]])
            fw.op("pool", lambda e: e.memset(PTm[i][:], 0.0), writes=[PTm[i]])
        ysqs = [sb([128, 512], BF16, "ysq%d" % i) for i in range(4)]; bvs = [sb([128, 512], BF16, "bv%d" % i) for i in range(4)]
        mtmpb = [ysqs[0], ysqs[1]]
        st8 = [sb([128, 8, 8], F32, "st8_%d" % i) for i in range(4)]
        zbs = [sb([128, 512], BF16, "zb%d" % i) for i in range(4)]
        zstg = [sb([128, 4, 512], BF16, "zstg0")] * 2
        pbA = [fw.psum([128, 512], F32, "pbA%d" % i) for i in range(2)]
        pbB = [fw.psum([128, 512], F32, "pbB%d" % i) for i in range(3)]
        psYt = fw.psum([128, 4, 128], F32, "psY")
        psZt = fw.psum([128, 512], F32, "psZ")
        psYs = [psYt, psYt]
        psZs = [psZt, psZt]
        ptb = fw.psum([128, 1024], BF16, "ptb")
        pbi = [0, 0]

        def nbA():
            b = pbA[pbi[0] % 2]
            pbi[0] += 1
            return b

        def nbB():
            b = pbB[pbi[1] % 3]
            pbi[1] += 1
            return b

        Ysbs = [Buf(Ysb.t, "Ysb_s%d" % s) for s in range(4)]
        ARc = [Buf(AR.t, "AR_c%d" % c) for c in range(4)]
        BTc = [Buf(BT.t, "BT_c%d" % c) for c in range(4)]
        KTc = [Buf(KTt.t, "KT_c%d" % c) for c in range(4)]
        Atc = [Buf(Atok.t, "At_c%d" % c) for c in range(4)]
        Bpc = [Buf(Bp.t, "Bp_c%d" % c) for c in range(4)]
        Kpc = [Buf(Kp.t, "Kp_c%d" % c) for c in range(4)]
        prc = [Buf(prodT.t, "pr_c%d" % c) for c in range(4)]
        Gec = [Buf(Gend.t, "Ge_c%d" % c) for c in range(4)]


        v4 = lambda ap: ap.rearrange("p (c t) -> p c t", t=T)
        st_mix = {}

        def P(ti):
            t0 = ti * 512
            xb = xTb[ti % 2]
            Vtok = Vtoks[ti % 2]; gtok = gtoks[ti % 2]
            if ti == 0:
                fw.op("pool", lambda e: e.memset(xb[:, :, 0:2], 0.0), writes=[xb])
            for (lo, hi, src) in x_fn(t0):
                if hi - lo == 1:
                    fw.dma("sp", xb[:, :, lo + 1:hi + 1], src, writes=[xb], allow_slow_non_contiguous=True)
                else:
                    fw.dma("sp", xb[:, :, lo + 1:hi + 1], src, writes=[xb])
            mixi = [0]

            xxb = [None]

            def make_mix(n):
                m = mix[mixi[0] % 2]
                mixi[0] += 1
                for kc in range(8):
                    tmpm = mtmpb[kc % 2]
                    fw.op("dve", lambda e: e.tensor_scalar(out=tmpm[:], in0=xxa[:, kc, :], scalar1=mu_sb[:, n, kc:kc + 1], scalar2=None, op0=ALU.mult),
                          reads=[xxa, mu_sb], writes=[tmpm])
                    fw.op("dve", lambda e: e.tensor_tensor(out=m[:, kc, :], in0=tmpm[:], in1=xb[:, kc, 2:514], op=ALU.add),
                          reads=[tmpm, xb], writes=[m])
                return m

            for kc in range(8):
                fw.op("dve", lambda e: e.tensor_tensor(out=xxa[:, kc, :], in0=xb[:, kc, 1:513], in1=xb[:, kc, 2:514], op=ALU.subtract),
                      reads=[xb], writes=[xxa])
            m = make_mix(1)
            p = nbA(); proj_fm(m, w1_sb, 0, 64, p)
            fw.op("act", lambda e: e.activation(out=thw[:], in_=p[0:64, :], func=AF.Tanh), reads=[p], writes=[thw])
            yield
            m = make_mix(4)
            p = nbA(); proj_fm(m, a1_sb, 0, 64, p)
            fw.op("act", lambda e: e.activation(out=tha[:], in_=p[0:64, :], func=AF.Copy), reads=[p], writes=[tha])
            yield
            m = make_mix(5)
            p = nbA(); proj_fm(m, g1_sb, 0, 128, p)
            fw.op("act", lambda e: e.activation(out=thg[:], in_=p[:], func=AF.Sigmoid), reads=[p], writes=[thg])
            yield
            for sub in range(4):
                p = nbA()
                fw.op("pe", lambda e: e.matmul(p[:], lhsT=thg[:, sub * 128:(sub + 1) * 128], rhs=g2_sb[:], start=True, stop=True),
                      reads=[thg, g2_sb], writes=[p])
                fw.op("act", lambda e: e.activation(out=gtok[:, sub, :], in_=p[:], func=AF.Copy), reads=[p], writes=[gtok])
            yield
            m = make_mix(3)
            for sub in range(4):
                p = nbA()
                for kc in range(8):
                    fw.op("pe", lambda e: e.matmul(p[:], lhsT=m[:, kc, sub * 128:(sub + 1) * 128], rhs=wv_sb[:, kc, :], start=(kc == 0), stop=(kc == 7)),
                          reads=[m, wv_sb], writes=[p])
                fw.op("act", lambda e: e.activation(out=Vtok[:, sub, :], in_=p[:], func=AF.Copy), reads=[p], writes=[Vtok])
            yield
            mr = make_mix(0)
            yield
            mk = make_mix(2)
            st_mix[ti] = (mr, mk)
            yield

        def proj_fm(m, w_sb, c0, ncol, p):
            for kc in range(8):
                fw.op("pe", lambda e: e.matmul(p[0:ncol, :], lhsT=w_sb[:, kc, c0:c0 + ncol], rhs=m[:, kc, :], start=(kc == 0), stop=(kc == 7)),
                      reads=[w_sb, m], writes=[p])

        def R1a(ti, c):
            mr, mk = st_mix[ti]
            pr = nbA(); proj_fm(mr, wr_sb, c * 128, 128, pr)
            fw.op("act", lambda e: e.activation(out=rsb[:], in_=pr[:], func=AF.Copy), reads=[pr], writes=[rsb])
            yield
            pk = nbA(); proj_fm(mk, wk_sb, c * 128, 128, pk)
            fw.op("act", lambda e: e.activation(out=ksb[:], in_=pk[:], func=AF.Copy), reads=[pk], writes=[ksb])
            yield
            pw = nbA()
            fw.op("pe", lambda e: e.matmul(pw[:], lhsT=w2_sb[:, c * 128:(c + 1) * 128], rhs=thw[:], start=True, stop=True),
                  reads=[w2_sb, thw], writes=[pw])
            fw.op("act", lambda e: e.activation(out=sg[:], in_=pw[:], func=AF.Sigmoid, bias=pf_sb[:, c, 0:1], scale=1.0),
                  reads=[pw, pf_sb], writes=[sg])
            yield
            pa = nbA()
            fw.op("pe", lambda e: e.matmul(pa[:], lhsT=a2_sb[:, c * 128:(c + 1) * 128], rhs=tha[:], start=True, stop=True),
                  reads=[a2_sb, tha], writes=[pa])
            fw.op("act", lambda e: e.activation(out=al[:], in_=pa[:], func=AF.Sigmoid, bias=pf_sb[:, c, 1:2], scale=1.0),
                  reads=[pa, pf_sb], writes=[al])
            yield
            fw.op("dve", lambda e: e.tensor_tensor_scan(out=Ec[:], data0=rmask[:], data1=sg[:], initial=0.0, op0=ALU.mult, op1=ALU.add),
                  reads=[rmask, sg], writes=[Ec])
            fw.op("dve", lambda e: e.tensor_tensor(out=Em[:], in0=Ec[:], in1=sg[:], op=ALU.subtract), reads=[Ec, sg], writes=[Em])
            Ec3 = Ec[:].rearrange("p (c t) -> p c t", t=T)
            fw.op("dve", lambda e: e.tensor_tensor(out=Ee[:].rearrange("p (c t) -> p c t", t=T), in0=Ec3[:, :, T - 1:T].to_broadcast([128, 4, T]),
                                                   in1=Ec3, op=ALU.subtract), reads=[Ec], writes=[Ee])
            yield
            fw.op("act", lambda e: e.activation(out=e1[:], in_=Em[:], func=AF.Exp, scale=-C0), reads=[Em], writes=[e1])
            fw.op("act", lambda e: e.activation(out=e2[:], in_=Ec[:], func=AF.Exp, scale=C0), reads=[Ec], writes=[e2])
            fw.op("act", lambda e: e.activation(out=e3[:], in_=Ec[:], func=AF.Exp, scale=-C0), reads=[Ec], writes=[e3])
            fw.op("act", lambda e: e.activation(out=e4[:], in_=Ee[:], func=AF.Exp, scale=-C0), reads=[Ee], writes=[e4])
            fw.op("act", lambda e: e.activation(out=Gend[:, c, :], in_=e3[:].rearrange("p (c t) -> p c t", t=T)[:, :, T - 1], func=AF.Copy),
                  reads=[e3], writes=[Gec[c]])
            yield
            fw.op("dve", lambda e: e.tensor_scalar(out=kkf[:], in0=ksb[:], scalar1=pf_sb[:, c, 2:3], scalar2=None, op0=ALU.mult),
                  reads=[ksb, pf_sb], writes=[kkf])
            fw.op("act", lambda e: e.activation(out=sqb[:], in_=kkf[:], func=AF.Square), reads=[kkf], writes=[sqb])
            yield
            pn = nbA()
            fw.op("pe", lambda e: e.matmul(pn[:], lhsT=BOnes[:], rhs=sqb[:], start=True, stop=True), reads=[BOnes, sqb], writes=[pn])
            fw.op("act", lambda e: e.activation(out=nrm[:], in_=pn[:], func=AF.Sqrt), reads=[pn], writes=[nrm])
            yield
            fw.op("dve", lambda e: e.tensor_scalar(out=nrm[:], in0=nrm[:], scalar1=1e-12, scalar2=None, op0=ALU.max), reads=[nrm], writes=[nrm])
            fw.op("dve", lambda e: e.reciprocal(out=nrm[:], in_=nrm[:]), reads=[nrm], writes=[nrm])
            fw.op("dve", lambda e: e.tensor_tensor(out=kkn[:], in0=kkf[:], in1=nrm[:], op=ALU.mult), reads=[kkf, nrm], writes=[kkn])
            yield
            fw.op("dve", lambda e: e.tensor_scalar(out=tm1[:], in0=al[:], scalar1=-1.0, scalar2=pf_sb[:, c, 3:4], op0=ALU.add, op1=ALU.mult),
                  reads=[al, pf_sb], writes=[tm1])
            fw.op("dve", lambda e: e.scalar_tensor_tensor(out=kmod[:], in0=tm1[:], scalar=1.0, in1=ksb[:], op0=ALU.add, op1=ALU.mult),
                  reads=[tm1, ksb], writes=[kmod])
            yield
            fw.op("dve", lambda e: e.scalar_tensor_tensor(out=AR[:, c, :, 0, :], in0=v4(kkn[:]), scalar=-1.0, in1=v4(e1[:]), op0=ALU.mult, op1=ALU.mult),
                  reads=[kkn, e1], writes=[ARc[c]])
            fw.op("pool", lambda e: e.tensor_tensor(out=AR[:, c, :, 1, :], in0=v4(rsb[:]), in1=v4(e3[:]), op=ALU.mult), reads=[rsb, e3], writes=[ARc[c]])
            fw.op("pool", lambda e: e.tensor_tensor(out=ka[:], in0=kkn[:], in1=al[:], op=ALU.mult), reads=[kkn, al], writes=[ka])
            fw.op("pool", lambda e: e.tensor_tensor(out=BT[:, c, :], in0=ka[:], in1=e2[:], op=ALU.mult), reads=[ka, e2], writes=[BTc[c]])
            fw.op("pool", lambda e: e.tensor_tensor(out=KTt[:, c, :], in0=kmod[:], in1=e2[:], op=ALU.mult), reads=[kmod, e2], writes=[KTc[c]])
            yield
            bpb = bpT[c % 2]; kpb = kpT[c % 2]
            fw.op("pool", lambda e: e.tensor_tensor(out=bpb[:], in0=ka[:], in1=e4[:], op=ALU.mult), reads=[ka, e4], writes=[bpb])
            fw.op("pool", lambda e: e.tensor_tensor(out=kpb[:], in0=kmod[:], in1=e4[:], op=ALU.mult), reads=[kmod, e4], writes=[kpb])
            fw.op("dve", lambda e: e.scalar_tensor_tensor(out=prodT[:, c, :], in0=rsb[:], scalar=pf_sb[:, c, 4:5], in1=kmod[:], op0=ALU.mult, op1=ALU.mult),
                  reads=[rsb, pf_sb, kmod], writes=[prc[c]])
            yield

        def R1b(ti, c):
            bpb = bpT[c % 2]; kpb = kpT[c % 2]
            for which, (src_fn, dst, dstb, srcbuf) in enumerate(((lambda sub: AR[:, c, sub, 0, :], Atok, Atc[c], ARc[c]),
                                                                 (lambda sub: bpb[:, sub * 128:(sub + 1) * 128], Bp, Bpc[c], bpb),
                                                                 (lambda sub: kpb[:, sub * 128:(sub + 1) * 128], Kp, Kpc[c], kpb))):
                for sub in range(4):
                    fw.op("pe", lambda e: e.transpose(ptb[:, sub * 128:(sub + 1) * 128], src_fn(sub), identb[:]), reads=[srcbuf, identb], writes=[ptb])
                if which != 1:
                    fw.op("act", lambda e: e.activation(out=dst[:, :, c * 128:(c + 1) * 128], in_=ptb[:, 0:512].rearrange("p (s f) -> p s f", s=4), func=AF.Copy),
                          reads=[ptb], writes=[dstb])
                else:
                    fw.op("dve", lambda e: e.tensor_copy(out=dst[:, :, c * 128:(c + 1) * 128], in_=ptb[:, 0:512].rearrange("p (s f) -> p s f", s=4)),
                          reads=[ptb], writes=[dstb])

        def RKS(ti):
            rks = rkss[ti % 2]
            for sub in range(4):
                p = nbA()
                for c in range(4):
                    fw.op("pe", lambda e: e.matmul(p[:, 0:8], lhsT=prodT[:, c, sub * 128:(sub + 1) * 128], rhs=Ind8[:, c, :], start=(c == 0), stop=(c == 3)),
                          reads=[prc[c], Ind8], writes=[p])
                fw.op("act", lambda e: e.activation(out=rks[:, sub, :], in_=p[:, 0:8], func=AF.Copy), reads=[p], writes=[rks])

        def R2a(ti, c):
            Vtok = Vtoks[ti % 2]
            hs = [(2 * c + hb, 64 * hb) for hb in range(2)]
            for hi, (h, r0) in enumerate(hs):
                for (lh, lhb, dstb) in ((BT, BTc[c], NM1[hi]), (KTt, KTc[c], LM[hi])):
                    for half in range(2):
                        p = nbB()
                        for cc in range(2):
                            ch = half * 2 + cc
                            fw.op("pe", lambda e: e.matmul(p[:, cc * 256:(cc + 1) * 256], lhsT=lh[r0:r0 + 64, c, ch * T:(ch + 1) * T],
                                                           rhs=AR[r0:r0 + 64, c, ch, :, :], start=True, stop=True),
                                  reads=[lhb, ARc[c]], writes=[p])
                        fw.op("dve", lambda e: e.tensor_tensor(out=dstb[:, half * 2:half * 2 + 2, :, :], in0=p[:].rearrange("p (a b i) -> p a b i", a=2, b=2),
                                                               in1=MK_SI[:], op=ALU.mult), reads=[p, MK_SI], writes=[dstb])
                p = nbB()
                for ch in range(4):
                    fw.op("pe", lambda e: e.matmul(p[:, ch * T:(ch + 1) * T], lhsT=AR[r0:r0 + 64, c, ch, 0, :], rhs=BT[r0:r0 + 64, c, ch * T:(ch + 1) * T],
                                                   start=True, stop=True), reads=[ARc[c], BTc[c]], writes=[p])
                fw.op("dve", lambda e: e.tensor_tensor(out=Npp[hi][0][:], in0=p[:].rearrange("p (c j) -> p c j", c=4), in1=MK_SL[:], op=ALU.mult),
                      reads=[p, MK_SL], writes=[Npp[hi][0]])
            for hi, (h, r0) in enumerate(hs):
                p = nbB()
                for ch in range(4):
                    fw.op("pe", lambda e: e.matmul(p[:, ch * 64:(ch + 1) * 64], lhsT=LM[hi][:, ch, 0, :], rhs=Vtok[:, ch, h * 64:(h + 1) * 64],
                                                   start=True, stop=True), reads=[LM[hi], Vtok], writes=[p])
                X0 = Xp[hi][0]
                fw.op("act", lambda e: e.activation(out=X0[:, :, 64:128], in_=p[:, 0:256].rearrange("p (c v) -> p c v", c=4), func=AF.Copy),
                      reads=[p], writes=[X0])
                fw.op("pool", lambda e: e.tensor_copy(out=X0[:, :, 0:64], in_=Atok[:, :, c * 128 + r0:c * 128 + r0 + 64]), reads=[Atc[c]], writes=[X0])

        def R2b(ti, c):
            Vtok = Vtoks[ti % 2]
            hs = [(2 * c + hb, 64 * hb) for hb in range(2)]
            for k in range(7):
                for hi, (h, r0) in enumerate(hs):
                    Xc = Xp[hi][k % 2]; Xn = Xp[hi][(k + 1) % 2]
                    Ntk = (lambda ch: NM1[hi][:, ch, 0, :]) if k == 0 else (lambda ch: Ntp[hi][k % 2][:, ch, :])
                    Ntbuf = NM1[hi] if k == 0 else Ntp[hi][k % 2]
                    Nk = Npp[hi][k % 2]
                    p = nbB()
                    for ch in range(4):
                        fw.op("pe", lambda e: e.matmul(p[:, ch * T:(ch + 1) * T], lhsT=identb[:], rhs=Xc[:, ch, :], start=True, stop=False),
                              reads=[identb, Xc], writes=[p])
                        fw.op("pe", lambda e: e.matmul(p[:, ch * T:(ch + 1) * T], lhsT=Ntk(ch), rhs=Xc[:, ch, :], start=False, stop=True),
                              reads=[Ntbuf, Xc], writes=[p])
                    fw.op("act", lambda e: e.activation(out=Xn[:], in_=p[:].rearrange("p (c v) -> p c v", c=4), func=AF.Copy), reads=[p], writes=[Xn])
                    if k < 6:
                        p2 = nbB()
                        for ch in range(4):
                            fw.op("pe", lambda e: e.matmul(p2[:, ch * T:(ch + 1) * T], lhsT=Nk[:, ch, :], rhs=Ntk(ch), start=True, stop=True),
                                  reads=[Nk, Ntbuf], writes=[p2])
                        Ntn = Ntp[hi][(k + 1) % 2]
                        fw.op("act", lambda e: e.activation(out=Ntn[:], in_=p2[:].rearrange("p (c v) -> p c v", c=4), func=AF.Copy), reads=[p2], writes=[Ntn])
                    if k < 5:
                        p3 = nbB()
                        for ch in range(4):
                            fw.op("pe", lambda e: e.matmul(p3[:, ch * T:(ch + 1) * T], lhsT=Ntk(ch), rhs=Nk[:, ch, :], start=True, stop=True),
                                  reads=[Nk, Ntbuf], writes=[p3])
                        Nn = Npp[hi][(k + 1) % 2]
                        fw.op("dve", lambda e: e.tensor_copy(out=Nn[:], in_=p3[:].rearrange("p (c v) -> p c v", c=4)), reads=[p3], writes=[Nn])
                yield
            for hi, (h, r0) in enumerate(hs):
                Xf = Xp[hi][7 % 2]
                p = nbB()
                for ch in range(4):
                    fw.op("pe", lambda e: e.matmul(p[r0:r0 + 64, ch * T:(ch + 1) * T], lhsT=Xf[:, ch, 0:64], rhs=NM1[hi][:, ch, 1, :], start=True, stop=False),
                          reads=[Xf, NM1[hi]], writes=[p])
                    fw.op("pe", lambda e: e.matmul(p[r0:r0 + 64, ch * T:(ch + 1) * T], lhsT=identb[:, r0:r0 + 64], rhs=AR[:, c, ch, 1, :],
                                                   start=False, stop=True), reads=[identb, ARc[c]], writes=[p])
                fw.op("act", lambda e: e.activation(out=GT[hi][r0:r0 + 64, :, :], in_=p[r0:r0 + 64, :].rearrange("p (c v) -> p c v", c=4), func=AF.Copy),
                      reads=[p], writes=[GT[hi]])
                p = nbB()
                for ch in range(4):
                    fw.op("pe", lambda e: e.matmul(p[r0:r0 + 64, ch * 64:(ch + 1) * 64], lhsT=Xf[:, ch, 0:64], rhs=Bp[:, ch, c * 128 + r0:c * 128 + r0 + 64],
                                                   start=True, stop=True), reads=[Xf, Bpc[c]], writes=[p])
                for ch in range(4):
                    fw.op("dve", lambda e: e.scalar_tensor_tensor(out=PTm[hi][r0:r0 + 64, ch, :], in0=identf[r0:r0 + 64, r0:r0 + 64], scalar=Gend[r0:r0 + 64, c, ch:ch + 1],
                                                                  in1=p[r0:r0 + 64, ch * 64:(ch + 1) * 64], op0=ALU.mult, op1=ALU.add),
                          reads=[identf, Gec[c], p], writes=[PTm[hi]])
                yield
            for ch in range(4):
                yield
                assert ti == 0 or (ti - 1) in r3_done, "epilogue of the previous tile must be emitted before Y of this tile is written"
                for hi, (h, r0) in enumerate(hs):
                    Xf = Xp[hi][7 % 2]
                    zi = (ti * 4 + ch) % 2
                    Zc = Zs[zi]; Zn = Zs[1 - zi]
                    psY = psYs[hi]; psZ = psZs[hi]
                    ycol = slice(hi * 64, hi * 64 + 64)
                    fw.op("pe", lambda e: e.matmul(psY[:, ch, ycol], lhsT=NM1[hi][:, ch, 1, :], rhs=Xf[:, ch, 64:128], start=True, stop=False),
                          reads=[NM1[hi], Xf], writes=[psY])
                    fw.op("pe", lambda e: e.matmul(psY[:, ch, ycol], lhsT=LM[hi][:, ch, 1, :], rhs=Vtok[:, ch, h * 64:(h + 1) * 64], start=False, stop=False),
                          reads=[LM[hi], Vtok], writes=[psY])
                    fw.op("pe", lambda e: e.matmul(psY[:, ch, ycol], lhsT=GT[hi][:, ch, :], rhs=Zc[:, c, :], start=False, stop=True),
                          reads=[GT[hi], Zc], writes=[psY])
                    fw.op("act", lambda e: e.activation(out=Ysb[:, ch, h * 64:(h + 1) * 64], in_=psY[:, ch, ycol], func=AF.Copy), reads=[psY], writes=[Ysbs[ch]])
                    fw.op("pe", lambda e: e.matmul(psZ[r0:r0 + 64, 0:64], lhsT=Bp[:, ch, c * 128 + r0:c * 128 + r0 + 64], rhs=Xf[:, ch, 64:128], start=True, stop=False),
                          reads=[Bpc[c], Xf], writes=[psZ])
                    fw.op("pe", lambda e: e.matmul(psZ[r0:r0 + 64, 0:64], lhsT=Kp[:, ch, c * 128 + r0:c * 128 + r0 + 64], rhs=Vtok[:, ch, h * 64:(h + 1) * 64], start=False, stop=False),
                          reads=[Kpc[c], Vtok], writes=[psZ])
                    fw.op("pe", lambda e: e.matmul(psZ[r0:r0 + 64, 0:64], lhsT=PTm[hi][:, ch, :], rhs=Zc[:, c, :], start=False, stop=True),
                          reads=[PTm[hi], Zc], writes=[psZ])
                    fw.op("dve", lambda e: e.tensor_copy(out=Zn[r0:r0 + 64, c, :], in_=psZ[r0:r0 + 64, 0:64]), reads=[psZ], writes=[Zn])

        r3_done = set()

        def R3(ti):
            t0 = ti * 512
            Vtok = Vtoks[ti % 2]; gtok = gtoks[ti % 2]; rks = rkss[ti % 2]
            S4 = range(4)
            Y = lambda sub: Ysb[:, sub, :]
            Y3 = lambda sub: Ysb[:, sub, :].rearrange("p (h v) -> p h v", h=8)
            bc = lambda ap: ap.unsqueeze(2).to_broadcast([128, 8, 64])
            yield
            for sub in S4:
                fw.op("dve", lambda e: e.tensor_reduce(out=st8[sub][:, 0, :], in_=Y3(sub), axis=AX.X, op=ALU.add), reads=[Ysbs[sub]], writes=[st8[sub]])
            yield
            for sub in S4:
                fw.op("act", lambda e: e.activation(out=ysqs[sub][:], in_=Y(sub), func=AF.Square), reads=[Ysbs[sub]], writes=[ysqs[sub]])
            yield
            for sub in S4:
                fw.op("dve", lambda e: e.tensor_reduce(out=st8[sub][:, 1, :], in_=ysqs[sub][:].rearrange("p (h v) -> p h v", h=8), axis=AX.X, op=ALU.add),
                      reads=[ysqs[sub]], writes=[st8[sub]])
            yield
            for sub in S4:
                s8 = st8[sub]
                fw.op("dve", lambda e: e.tensor_scalar(out=s8[:, 2, :], in0=s8[:, 0, :], scalar1=1.0 / 64, scalar2=None, op0=ALU.mult), reads=[s8], writes=[s8])
            yield
            for sub in S4:
                s8 = st8[sub]
                fw.op("dve", lambda e: e.tensor_tensor(out=s8[:, 3, :], in0=s8[:, 2, :], in1=s8[:, 2, :], op=ALU.mult), reads=[s8], writes=[s8])
            yield
            for sub in S4:
                s8 = st8[sub]
                fw.op("dve", lambda e: e.scalar_tensor_tensor(out=s8[:, 4, :], in0=s8[:, 1, :], scalar=1.0 / 64, in1=s8[:, 3, :], op0=ALU.mult, op1=ALU.subtract),
                      reads=[s8], writes=[s8])
            yield
            for sub in S4:
                s8 = st8[sub]
                fw.op("act", lambda e: e.activation(out=s8[:, 5, :], in_=s8[:, 4, :], func=AF.Sqrt, bias=LNX_EPS, scale=1.0), reads=[s8], writes=[s8])
            yield
            for sub in S4:
                s8 = st8[sub]
                fw.op("dve", lambda e: e.reciprocal(out=s8[:, 6, :], in_=s8[:, 5, :]), reads=[s8], writes=[s8])
            yield
            for sub in S4:
                fw.op("dve", lambda e: e.tensor_tensor(out=Y3(sub), in0=Y3(sub), in1=bc(st8[sub][:, 2, :]), op=ALU.subtract),
                      reads=[Ysbs[sub], st8[sub]], writes=[Ysbs[sub]])
            yield
            for sub in S4:
                fw.op("pool", lambda e: e.tensor_tensor(out=bvs[sub][:].rearrange("p (h v) -> p h v", h=8), in0=Vtok[:, sub, :].rearrange("p (h v) -> p h v", h=8),
                                                        in1=bc(rks[:, sub, :]), op=ALU.mult), reads=[Vtok, rks], writes=[bvs[sub]])
            yield
            for sub in S4:
                fw.op("dve", lambda e: e.tensor_tensor(out=Y3(sub), in0=Y3(sub), in1=bc(st8[sub][:, 6, :]), op=ALU.mult),
                      reads=[Ysbs[sub], st8[sub]], writes=[Ysbs[sub]])
            yield
            for sub in S4:
                fw.op("dve", lambda e: e.tensor_tensor(out=Y(sub), in0=Y(sub), in1=lwB[:], op=ALU.mult), reads=[Ysbs[sub], lwB], writes=[Ysbs[sub]])
            yield
            for sub in S4:
                fw.op("dve", lambda e: e.tensor_tensor(out=Y(sub), in0=Y(sub), in1=lbB[:], op=ALU.add), reads=[Ysbs[sub], lbB], writes=[Ysbs[sub]])
            yield
            for sub in S4:
                fw.op("dve", lambda e: e.tensor_tensor(out=Y(sub), in0=Y(sub), in1=bvs[sub][:], op=ALU.add), reads=[Ysbs[sub], bvs[sub]], writes=[Ysbs[sub]])
            yield
            for sub in S4:
                fw.op("dve", lambda e: e.tensor_tensor(out=zbs[sub][:], in0=Y(sub), in1=gtok[:, sub, :], op=ALU.mult), reads=[Ysbs[sub], gtok], writes=[zbs[sub]])
            yield
            zs = zstg[ti % 2]
            yield
            for sub in S4:
                for pr_ in range(4):
                    fw.op("pe", lambda e: e.transpose(ptb[:, pr_ * 128:(pr_ + 1) * 128], zbs[sub][:, pr_ * 128:(pr_ + 1) * 128], identb[:]),
                          reads=[zbs[sub], identb], writes=[ptb])
                fw.op("act", lambda e: e.activation(out=zs[:, :, sub * 128:(sub + 1) * 128], in_=ptb[:, 0:512].rearrange("p (r t) -> p r t", r=4), func=AF.Copy),
                      reads=[ptb], writes=[zs])
            fw.dma("sp", [(dst, zs[:, :, c_a:c_b]) for (dst, c_a, c_b) in zout_fn(t0)], reads=[zs])
            r3_done.add(ti)

        def run(*gens, rate=1):
            gens = [[g, (1 if i == 0 else rate)] for i, g in enumerate(gens) if g is not None]
            while gens:
                for ent in list(gens):
                    for _ in range(ent[1]):
                        try:
                            next(ent[0])
                        except StopIteration:
                            gens.remove(ent)
                            break

        def chain(*gens):
            for g in gens:
                if g is not None:
                    yield from g

        run(P(0))
        for c in range(4):
            run(R1a(0, c))
            R1b(0, c)
        RKS(0)
        for ti in range(n_tiles):
            nxt = ti + 1 < n_tiles
            R2a(ti, 0)
            run(R2b(ti, 0), chain(R3(ti - 1) if ti > 0 else None, R1a(ti, 3) if ti > 0 else None), rate=3)
            if ti > 0:
                R1b(ti, 3)
                RKS(ti)
            R2a(ti, 1)
            run(R2b(ti, 1), P(ti + 1) if nxt else None)
            R2a(ti, 2)
            run(R2b(ti, 2), chain(R1a(ti + 1, 0), R1a(ti + 1, 1)) if nxt else None, rate=2)
            if nxt:
                R1b(ti + 1, 0)
                R1b(ti + 1, 1)
            R2a(ti, 3)
            run(R2b(ti, 3), R1a(ti + 1, 2) if nxt else None)
            if nxt:
                R1b(ti + 1, 2)
        run(R3(n_tiles - 1))
        fw.barrier()
        fw.es = old_es
import ml_dtypes
_bf = ml_dtypes.bfloat16
NCORES = 8
SEQ = 8192
HALF = 4096
PADC = 128
CW = PADC + HALF
CS = CW + 64
RL = 2 * CS
PAIRS = [[0, 1], [2, 3], [4, 5], [6, 7]]


def _din(nc, name, shape, dt=F32):
    return Buf(nc.dram_tensor(name, list(shape), dt, kind="ExternalInput").ap(), name)


def _dout(nc, name, shape, dt=F32):
    return Buf(nc.dram_tensor(name, list(shape), dt, kind="ExternalOutput").ap(), name)


def _dscr(nc, name, shape, dt):
    return Buf(nc.dram_tensor(name, list(shape), dt).ap(), name)


def _lay_wgu(w):
    return np.ascontiguousarray(w.reshape(8, 128, 11, 2, 128).transpose(2, 1, 3, 0, 4))


def build_fused():
    nc = bass.Bass("TRN2", target_bir_lowering=False)
    x_full = _din(nc, "x_full", [SEQ, D]); x_half = _din(nc, "x_half", [128 + HALF, D])
    gmix0 = _din(nc, "gmix0", [D]); wqkv = _din(nc, "wqkv", [D, 1536]); cos = _din(nc, "cos", [SEQ, 64]); sinm = _din(nc, "sinm", [SEQ, 64])
    ffn_in = []
    for l in range(2):
        ffn_in.append(dict(wo=_din(nc, "wo%d" % l, [D, D]), gffn=_din(nc, "gffn%d" % l, [D]), wg=_din(nc, "wg%d" % l, [11, 128, 2, 8, 128]),
                           wu=_din(nc, "wu%d" % l, [11, 128, 2, 8, 128]), wd=_din(nc, "wd%d" % l, [DFF, D]), cw=_din(nc, "cw%d" % l, [128, NFC, 3]),
                           cb=_din(nc, "cb%d" % l, [128, NFC]), gnext=_din(nc, "gnext%d" % l, [D])))
    mu = _din(nc, "mu", [128, 6, 8]); wr = _din(nc, "wr", [D, 512]); wk = _din(nc, "wk", [D, 512]); wv = _din(nc, "wv", [D, 512])
    w1 = _din(nc, "w1", [D, 64]); a1 = _din(nc, "a1", [D, 64]); g1 = _din(nc, "g1", [D, 128]); w2 = _din(nc, "w2", [64, 512]); a2 = _din(nc, "a2", [64, 512]); g2 = _din(nc, "g2", [128, 512])
    pf = _din(nc, "pf", [128, 4, 5]); lnxw = _din(nc, "lnxw", [512]); lnxb = _din(nc, "lnxb", [512])
    out = _dout(nc, "out", [HALF, D], F32)
    QT = _dscr(nc, "QT", [512, SEQ], BF16); KT = _dscr(nc, "KT", [512, SEQ], BF16)
    attnS = _dscr(nc, "attnS", [8, 64, RL], BF16); attnG = _dscr(nc, "attnG", [8, 128, RL], BF16)
    xnS = _dscr(nc, "xnS", [8, 128, HALF], BF16); xnG = _dscr(nc, "xnG", [8, 256, HALF], BF16)
    h2 = _dscr(nc, "h2", [HALF, D], F32); tailS = _dscr(nc, "tailS", [128, D], F32); tailG = _dscr(nc, "tailG", [256, D], F32)
    tail3 = _dscr(nc, "tail3", [256, D], F32)
    mixL = _dscr(nc, "mixL", [D, CW], BF16)
    zS = _dscr(nc, "zS", [8, 64, RL], BF16); zG = _dscr(nc, "zG", [8, 128, RL], BF16)
    with ExitStack() as es:
        fw = FW(nc, es)
        ident, identf = make_ident(fw)
        pid = nc.partition_id(engines=[mybir.EngineType.SP])
        r = pid % 2
        with ExitStack() as ez:
            fw.es = ez
            zt = fw.sbuf([128, 1024], F32, "zeros_f")
            ztb = fw.sbuf([128, 4, 128], BF16, "zeros_b")
            fw.op("pool", lambda e: e.memset(zt[:], 0.0), writes=[zt])
            fw.op("pool", lambda e: e.memset(ztb[:], 0.0), writes=[ztb])
            fw.dma("sp", [(attnS.t.rearrange("(pr jl) q t -> (jl q) pr t", jl=2)[:, :, 0:PADC], ztb[:]),
                          (zS.t.rearrange("(pr jl) q t -> (jl q) pr t", jl=2)[:, :, 0:PADC], ztb[:]),
                          (tail3.t[0:128, :], zt[:])], reads=[zt, ztb])
            for S_ in (attnS, zS):
                Sv = S_.t.rearrange("(pr jl) q t -> (jl q) pr t", jl=2)
                fw.dma("sp", [(Sv[:, :, CW:CS], ztb[:, :, 0:CS - CW]), (Sv[:, :, CS + CW:RL], ztb[:, :, 0:RL - CS - CW])], reads=[ztb])
            fw.barrier()
            fw.es = es
        def half_cols(t0):
            pc = PADC + t0
            if t0 + 512 <= HALF:
                res = [(pc, 0, 512)]
                if t0 + 512 == HALF:
                    res.append((CS, 512 - PADC, 512))
                return res
            return [(CS + pc - HALF, 0, 512)]

        def localize(G, L):
            for rk in range(2):
                fw.dma("sp", L.t[rk * 512:(rk + 1) * 512, :].rearrange("(j q) t -> j q t", q=64),
                       G.t[:, rk * 64:(rk + 1) * 64, bass.ds(r * CS, CW)])

        def local_mix(L):
            Lv = L.t.rearrange("(kc p) t -> p kc t", p=128)
            return lambda c0, n: [(0, 128, 0, 8, Lv[:, :, c0:c0 + n])]

        phase_a(fw, ident, identf, x_full, gmix0, wqkv, cos, sinm, QT, KT,
                lambda h, qt: [(attnS.t[h, :, c:c + (b_ - a_)], a_, b_) for (c, a_, b_) in half_cols(qt * 512)])
        for j in range(8):
            fw.collective("AllGather", attnS.t[j], attnG.t[j], PAIRS)
        fw.barrier()
        localize(attnG, mixL)
        fw.barrier()
        f = ffn_in[0]
        phase_f(fw, ident, lambda s: x_half.t[s * 128:(s + 1) * 128, :], local_mix(mixL),
                f["wo"], f["gffn"], f["wg"], f["wu"], f["wd"], f["cw"], f["cb"], f["gnext"], h2, tailS, None, xnS, HALF // 512, False)
        for j in range(8):
            fw.collective("AllGather", xnS.t[j], xnG.t[j], PAIRS)
        fw.collective("AllGather", tailS.t[:, :], tailG.t[:, :], PAIRS)
        fw.barrier()
        fw.dma("sp", tail3.t[128:256, :], tailG.t[0:128, :])
        fw.barrier()
        xg_v = xnG.t.rearrange("kc (hf p) t -> hf p kc t", hf=2)

        def x_fn(t0):
            hf, tl = t0 // HALF, t0 % HALF
            if t0 == 0:
                return [(1, 513, xg_v[0][:, :, 0:512])]
            if tl == 0:
                return [(0, 1, xg_v[hf - 1][:, :, HALF - 1:HALF]), (1, 513, xg_v[hf][:, :, 0:512])]
            return [(0, 513, xg_v[hf][:, :, tl - 1:tl + 512])]

        zS_v = zS.t.rearrange("(pr jl) q t -> (jl q) pr t", jl=2)
        phase_r(fw, ident, identf, x_fn, mu, wr, wk, wv, w1, a1, g1, w2, a2, g2, pf, lnxw, lnxb,
                lambda t0: [(zS_v[:, :, c:c + (b_ - a_)], a_, b_) for (c, a_, b_) in half_cols(t0)], SEQ // 512)
        for j in range(8):
            fw.collective("AllGather", zS.t[j], zG.t[j], PAIRS)
        fw.barrier()
        localize(zG, mixL)
        fw.barrier()
        f = ffn_in[1]
        phase_f(fw, ident, lambda s: (tail3.t[bass.ds(r * 128, 128), :] if s == 0 else h2.t[(s - 1) * 128:s * 128, :]), local_mix(mixL),
                f["wo"], f["gffn"], f["wg"], f["wu"], f["wd"], f["cw"], f["cb"], f["gnext"], None, None, out, None, HALF // 512, True)
        fw.finish([])
    return nc


def _rope_tables():
    inv = (1.0 / (10000.0 ** (np.arange(0, 64, 2, dtype=np.float32) / np.float32(64)))).astype(np.float32)
    ang = np.arange(SEQ, dtype=np.float32)[:, None] * inv[None, :]
    ang = np.concatenate([ang, ang], -1)
    cos = np.cos(ang).astype(np.float32); sin = np.sin(ang).astype(np.float32)
    sinm = np.concatenate([-sin[:, :32], sin[:, 32:]], -1).astype(np.float32)
    return np.ascontiguousarray(cos), np.ascontiguousarray(sinm)


def kernel(x, norm_mix, norm_ffn, norm_final, attn_w_qkv, attn_w_o,
           rwkv_mu, rwkv_w_rkv, rwkv_w0, rwkv_w1, rwkv_w2, rwkv_a0, rwkv_a1,
           rwkv_a2, rwkv_g1, rwkv_g2, rwkv_k_k, rwkv_k_a, rwkv_r_k,
           rwkv_lnx_w, rwkv_lnx_b, rwkv_w_o,
           ffn_w_gate, ffn_w_up, ffn_conv_w, ffn_conv_b, ffn_w_down):
    f32 = lambda a: np.ascontiguousarray(np.asarray(a, dtype=np.float32))
    x = f32(x)
    cores = list(range(NCORES))
    cos, sinm = _rope_tables()
    wqkv = f32(attn_w_qkv)[0]
    wrkv = f32(rwkv_w_rkv)[0]
    mu_l = np.ascontiguousarray(f32(rwkv_mu)[0].reshape(6, 8, 128).transpose(2, 0, 1))
    shared = {"gmix0": f32(norm_mix)[0], "cos": cos, "sinm": sinm, "mu": mu_l,
              "w1": f32(rwkv_w1)[0], "a1": f32(rwkv_a1)[0], "g1": f32(rwkv_g1)[0]}
    wos = [f32(attn_w_o)[0], f32(rwkv_w_o)[0]]
    gnexts = [f32(norm_mix)[1], f32(norm_final)]
    for l in range(2):
        cw = f32(ffn_conv_w)[l]; cb = f32(ffn_conv_b)[l]
        shared.update({"wo%d" % l: wos[l], "gffn%d" % l: f32(norm_ffn)[l], "wg%d" % l: _lay_wgu(f32(ffn_w_gate)[l]), "wu%d" % l: _lay_wgu(f32(ffn_w_up)[l]),
                       "wd%d" % l: f32(ffn_w_down)[l], "cw%d" % l: np.ascontiguousarray(cw.T.reshape(NFC, 128, 3).transpose(1, 0, 2)),
                       "cb%d" % l: np.ascontiguousarray(cb.reshape(NFC, 128).T), "gnext%d" % l: gnexts[l]})
    maps = []
    for c in cores:
        b, r = c // 2, c % 2
        own = slice(512 * r, 512 * r + 512)
        wc = np.ascontiguousarray(np.concatenate([wqkv[:, 0:1024][:, own], wqkv[:, 1024:2048][:, own], wqkv[:, 2048:3072][:, own]], 1))
        xh = np.zeros((128 + HALF, D), np.float32)
        if r == 0:
            xh[128:] = x[b][0:HALF]
        else:
            xh[:] = x[b][HALF - 128:SEQ]
        pfl = np.stack([f32(rwkv_w0)[0][own], f32(rwkv_a0)[0][own], f32(rwkv_k_k)[0][own], f32(rwkv_k_a)[0][own], f32(rwkv_r_k)[0].reshape(-1)[own]], -1)
        m = dict(shared)
        m.update({"x_full": x[b], "x_half": xh, "wqkv": wc,
                  "wr": np.ascontiguousarray(wrkv[0][:, own]), "wk": np.ascontiguousarray(wrkv[1][:, own]), "wv": np.ascontiguousarray(wrkv[2][:, own]),
                  "w2": np.ascontiguousarray(f32(rwkv_w2)[0][:, own]), "a2": np.ascontiguousarray(f32(rwkv_a2)[0][:, own]),
                  "g2": np.ascontiguousarray(f32(rwkv_g2)[0][:, own]),
                  "pf": np.ascontiguousarray(pfl.reshape(4, 128, 5).transpose(1, 0, 2)),
                  "lnxw": np.ascontiguousarray(f32(rwkv_lnx_w)[0][own]), "lnxb": np.ascontiguousarray(f32(rwkv_lnx_b)[0][own])})
        maps.append(m)
    res = run_bass_kernel_spmd(build_fused(), maps, core_ids=cores)
    out = np.stack([np.concatenate([np.asarray(res.results[2 * b]["out"]), np.asarray(res.results[2 * b + 1]["out"])], 0) for b in range(4)], 0)
    return out.astype(np.float32)
```

```python
import math
import numpy as np
from contextlib import ExitStack
import concourse.bass as bass
import concourse.mybir as mybir
from concourse.bass_utils import run_bass_kernel_spmd

F32 = mybir.dt.float32
BF16 = mybir.dt.bfloat16
AF = mybir.ActivationFunctionType
ALU = mybir.AluOpType
AX = mybir.AxisListType


class Buf:
    def __init__(self, t, name):
        self.t = t
        self.name = name
        self.w = None
        self.r = {}

    def __getitem__(self, k):
        return self.t[k]


class FW:
    ENG = ("pe", "act", "dve", "pool", "sp")

    def __init__(self, nc, es, n_dma_sems=24):
        self.nc = nc
        self.es = es
        self.es0 = es
        self.eng = {"pe": nc.tensor, "act": nc.scalar, "dve": nc.vector,
                    "pool": nc.gpsimd, "sp": nc.sync}
        self.sem = {}
        self.cnt = {}
        for e in self.ENG:
            self.sem[e] = es.enter_context(nc.semaphore("s_" + e))
            self.cnt[e] = 0
        self.dsem = []
        for i in range(n_dma_sems):
            k = "d%d" % i
            self.sem[k] = es.enter_context(nc.semaphore("s_" + k))
            self.cnt[k] = 0
            self.dsem.append(k)
        self.dnext = {"hw": 0, "sw": 0}
        nsw = n_dma_sems // 3
        self.dpool = {"sw": self.dsem[:nsw], "hw": self.dsem[nsw:]}
        self.seen = {e: {} for e in self.ENG}
        self.nbuf = 0
        self.ninst = 0

    def sbuf(self, shape, dtype, name=None):
        self.nbuf += 1
        name = "%s_%d" % (name or "b", self.nbuf)
        t = self.es.enter_context(self.nc.sbuf_tensor(name, list(shape), dtype))
        return Buf(t, name)

    def psum(self, shape, dtype=F32, name=None):
        self.nbuf += 1
        name = "%s_%d" % (name or "p", self.nbuf)
        t = self.es.enter_context(self.nc.psum_tensor(name, list(shape), dtype))
        return Buf(t, name)

    def dram(self, name, shape, dtype, kind="Internal"):
        t = self.nc.dram_tensor(name, list(shape), dtype, kind=kind)
        return Buf(t.ap(), name)

    def _wait(self, e, ev):
        if ev is None:
            return
        k, v = ev
        if self.seen[e].get(k, 0) >= v:
            return
        if k == e and e == "pe":
            return
        self.eng[e].wait_ge(self.sem[k], v)
        self.seen[e][k] = v
        self.ninst += 1

    def _deps(self, e, reads, writes):
        for b in reads:
            self._wait(e, b.w)
        for b in writes:
            self._wait(e, b.w)
            for k, v in b.r.items():
                self._wait(e, (k, v))

    def _mark(self, ev, reads, writes):
        k, v = ev
        for b in reads:
            b.r[k] = v
        for b in writes:
            b.w = ev
            b.r = {}

    def op(self, e, fn, reads=(), writes=()):
        self._deps(e, reads, writes)
        inst = fn(self.eng[e])
        self.cnt[e] += 1
        inst.then_inc(self.sem[e], 1)
        self.ninst += 1
        self._mark((e, self.cnt[e]), reads, writes)
        return inst

    def dma(self, q, out, in_=None, reads=(), writes=(), **kw):
        pairs = out if in_ is None else [(out, in_)]
        self._deps(q, reads, writes)
        kind = "sw" if q == "pool" else "hw"
        pool_ = self.dpool[kind]
        k = pool_[self.dnext[kind]]
        self.dnext[kind] = (self.dnext[kind] + 1) % len(pool_)
        if self.cnt[k] > 0:
            self._wait(q, (k, self.cnt[k]))
        for (o, i) in pairs:
            inst = self.eng[q].dma_start(out=o, in_=i, **kw)
            self.cnt[k] += 16
            inst.then_inc(self.sem[k], 16)
            self.ninst += 1
        self._mark((k, self.cnt[k]), reads, writes)
        return inst

    def finish(self, bufs, e="sp"):
        for b in bufs:
            self._wait(e, b.w)
        for k in self.dsem:
            if self.cnt[k] > 0:
                self._wait(e, (k, self.cnt[k]))


def _barrier(self):
    for e in self.ENG:
        for k in list(self.ENG) + self.dsem + (["cc"] if "cc" in self.sem else []):
            if k == e:
                continue
            if self.cnt[k] > 0:
                self._wait(e, (k, self.cnt[k]))


FW.barrier = _barrier


def _collective(self, kind, in_ap, out_ap, groups, reads=(), writes=()):
    q = "pool"
    self._deps(q, reads, writes)
    if "cc" not in self.sem:
        self.sem["cc"] = self.es0.enter_context(self.nc.semaphore("s_cc"))
        self.cnt["cc"] = 0
    inst = self.nc.gpsimd.collective_compute(kind, ALU.bypass, replica_groups=groups, ins=[in_ap], outs=[out_ap])
    self.cnt["cc"] += 1
    inst.then_inc(self.sem["cc"], 1)
    self.ninst += 1
    self._mark(("cc", self.cnt["cc"]), reads, writes)
    return inst


FW.collective = _collective

D = 1024
DFF = 2816
NFC = 22
RMS_EPS = 1e-6


def make_ident(fw, dtype=BF16):
    identf = fw.sbuf([128, 128], F32, "identf")
    fw.op("pool", lambda e: e.memset(identf[:], 0.0), writes=[identf])
    fw.op("pool", lambda e: e.affine_select(out=identf[:], in_=identf[:], pattern=[[-1, 128]],
                                            compare_op=ALU.not_equal, fill=1.0, base=0,
                                            channel_multiplier=1), reads=[identf], writes=[identf])
    ident = fw.sbuf([128, 128], BF16, "ident")
    fw.op("dve", lambda e: e.tensor_copy(out=ident[:], in_=identf[:]), reads=[identf], writes=[ident])
    return ident, identf


def phase_f(fw, ident, hin_fn, mix_fn, wo, gffn, wg, wu, wd, cw, cb, gnext, hout, tail, nout, xnS, n_tiles, final):
    nc = fw.nc
    with ExitStack() as es:
        old_es = fw.es
        fw.es = es
        wo_sb = fw.sbuf([128, 8, 1024], BF16, "wo_sb")
        wd_sb = fw.sbuf([128, NFC, 1024], BF16, "wd_sb")
        gB = fw.sbuf([128, 1024], F32, "gB")
        gnB = fw.sbuf([128, 1024], F32, "gnB")
        cw_sb = fw.sbuf([128, NFC, 3], F32, "cw_sb")
        cb_sb = fw.sbuf([128, NFC], F32, "cb_sb")
        wgs = [fw.sbuf([128, 2, 8, 128], BF16, "wgs%d" % i) for i in range(2)]
        wus = [fw.sbuf([128, 2, 8, 128], BF16, "wus%d" % i) for i in range(2)]
        hx = [fw.sbuf([128, 1024], F32, "hx%d" % i) for i in range(2)]
        h1 = [fw.sbuf([128, 1024], F32, "h1_%d" % i) for i in range(4)]
        xn = [fw.sbuf([128, 1024], BF16, "xn%d" % i) for i in range(2)]
        xnT = fw.sbuf([128, 8, 512], BF16, "xnT")
        mx = fw.sbuf([128, 8, 512], BF16, "mx")
        aT = [fw.sbuf([128, 512], BF16, "aT%d" % i) for i in range(NFC)]
        G = [fw.sbuf([128, 514], F32, "G%d" % i) for i in range(2)]
        t1 = [fw.sbuf([128, 512], F32, "t1_%d" % i) for i in range(2)]
        sl = [fw.sbuf([128, 512], F32, "sl%d" % i) for i in range(2)]
        H = fw.sbuf([128, NFC, 2], F32, "H")
        nout_dtype = F32 if final else BF16
        no = [fw.sbuf([128, 1024], nout_dtype, "no%d" % i) for i in range(2)]
        nstg = [fw.sbuf([128, 8, 512], BF16, "nstg%d" % i) for i in range(2)] if not final else None
        stat = [fw.sbuf([128, 4], F32, "stat%d" % i) for i in range(2)]
        stat2 = [fw.sbuf([128, 4], F32, "stat2_%d" % i) for i in range(2)]
        sq = fw.sbuf([128, 1024], F32, "sqjunk")
        pj = [fw.psum([128, 512], F32, "pj%d" % i) for i in range(2)]
        tp = fw.psum([128, 1024], BF16, "tp")
        pg = [fw.psum([128, 512], F32, "pg%d" % i) for i in range(2)]
        pu = [fw.psum([128, 512], F32, "pu%d" % i) for i in range(2)]

        wo_v = wo.t.rearrange("(kc p) n -> p kc n", p=128)
        fw.dma("pool", [(wo_sb[:, kc:kc + 2, :], wo_v[:, kc:kc + 2, :]) for kc in range(0, 8, 2)], writes=[wo_sb])
        wd_v = wd.t.rearrange("(fc p) n -> p fc n", p=128)
        fw.dma("pool", [(wd_sb[:, fc:fc + 2, :], wd_v[:, fc:fc + 2, :]) for fc in range(0, NFC, 2)], writes=[wd_sb])
        fw.dma("sp", gB[:], gffn.t.partition_broadcast(128), writes=[gB])
        fw.dma("sp", gnB[:], gnext.t.partition_broadcast(128), writes=[gnB])
        fw.dma("sp", cw_sb[:], cw.t[:, :, :], writes=[cw_sb])
        fw.dma("sp", cb_sb[:], cb.t[:, :], writes=[cb_sb])

        nsub_total = 1 + 4 * n_tiles
        wcount = [0]

        def front(s_glob, slot):
            hxb = hx[s_glob % 2]
            h1b = h1[slot]
            xnb = xn[s_glob % 2]
            stb = stat[s_glob % 2]
            fw.dma("sp", hxb[:], hin_fn(s_glob), writes=[hxb])
            for half in range(2):
                for kc in range(8):
                    fw.op("pe", lambda e: e.matmul(pj[half][:], lhsT=mx[:, kc, slot * 128:(slot + 1) * 128],
                                                   rhs=wo_sb[:, kc, half * 512:(half + 1) * 512],
                                                   start=(kc == 0), stop=(kc == 7)),
                          reads=[mx, wo_sb], writes=[pj[half]])
            for half in range(2):
                fw.op("dve", lambda e: e.tensor_tensor(out=h1b[:, half * 512:(half + 1) * 512],
                                                       in0=hxb[:, half * 512:(half + 1) * 512],
                                                       in1=pj[half][:], op=ALU.add),
                      reads=[hxb, pj[half]], writes=[h1b])
            rms(h1b, stb)
            fw.op("dve", lambda e: e.scalar_tensor_tensor(out=xnb[:], in0=h1b[:], scalar=stb[:, 2:3], in1=gB[:],
                                                          op0=ALU.mult, op1=ALU.mult),
                  reads=[h1b, stb, gB], writes=[xnb])

        def front_b(s_glob, slot):
            xnb = xn[s_glob % 2]
            for kc in range(8):
                fw.op("pe", lambda e: e.transpose(tp[:, kc * 128:(kc + 1) * 128], xnb[:, kc * 128:(kc + 1) * 128], ident[:]),
                      reads=[xnb, ident], writes=[tp])
            fw.op("act", lambda e: e.activation(out=xnT[:, :, slot * 128:(slot + 1) * 128],
                                                in_=tp[:].rearrange("p (k t) -> p k t", k=8), func=AF.Copy),
                  reads=[tp], writes=[xnT])

        def rms(hb, stb):
            fw.op("dve", lambda e: e.memset(stb[:, 0:1], 0.0), writes=[stb])
            fw.op("act", lambda e: e.activation(out=sq[:], in_=hb[:], func=AF.Square, accum_out=stb[:, 0:1]),
                  reads=[hb], writes=[sq, stb])
            fw.op("act", lambda e: e.activation(out=stb[:, 1:2], in_=stb[:, 0:1], func=AF.Sqrt, scale=1.0 / D, bias=RMS_EPS),
                  reads=[stb], writes=[stb])
            fw.op("dve", lambda e: e.reciprocal(out=stb[:, 2:3], in_=stb[:, 1:2]), reads=[stb], writes=[stb])

        def load_w(j):
            sl_ = wcount[0] % 2
            wcount[0] += 1
            fw.dma("pool", wgs[sl_][:], wg.t[j], writes=[wgs[sl_]])
            fw.dma("pool", wus[sl_][:], wu.t[j], writes=[wus[sl_]])
            return sl_

        fw.dma("sp", [(mx[p0:p1, k0:k1, 0:128], src) for (p0, p1, k0, k1, src) in mix_fn(0, 128)], writes=[mx])
        front(0, 0)
        front_b(0, 0)
        for j in range(NFC // 2):
            ws = load_w(j)
            for jj in range(2):
                fc = 2 * j + jj
                pgb = pg[fc % 2]
                for kc in range(8):
                    fw.op("pe", lambda e: e.matmul(pgb[:, 0:2], lhsT=wgs[ws][:, jj, kc, :], rhs=xnT[:, kc, 126:128],
                                                   start=(kc == 0), stop=(kc == 7)),
                          reads=[wgs[ws], xnT], writes=[pgb])
                fw.op("act", lambda e: e.activation(out=H[:, fc, :], in_=pgb[:, 0:2], func=AF.Copy),
                      reads=[pgb], writes=[H])

        for ti in range(n_tiles):
            c0 = 128 + ti * 512
            fw.dma("sp", [(mx[p0:p1, k0:k1, :], src) for (p0, p1, k0, k1, src) in mix_fn(c0, 512)], writes=[mx])
            for s in range(4):
                front(1 + ti * 4 + s, s)
                if s > 0:
                    front_b(1 + ti * 4 + s - 1, s - 1)
            front_b(1 + ti * 4 + 3, 3)
            for j in range(NFC // 2):
                ws = load_w(j)
                for jj in range(2):
                    fc = 2 * j + jj
                    pgb = pg[fc % 2]
                    pub = pu[fc % 2]
                    Gb = G[fc % 2]
                    t1b = t1[fc % 2]
                    slb = sl[fc % 2]
                    for kc in range(8):
                        fw.op("pe", lambda e: e.matmul(pgb[:], lhsT=wgs[ws][:, jj, kc, :], rhs=xnT[:, kc, :],
                                                       start=(kc == 0), stop=(kc == 7)),
                              reads=[wgs[ws], xnT], writes=[pgb])
                    for kc in range(8):
                        fw.op("pe", lambda e: e.matmul(pub[:], lhsT=wus[ws][:, jj, kc, :], rhs=xnT[:, kc, :],
                                                       start=(kc == 0), stop=(kc == 7)),
                              reads=[wus[ws], xnT], writes=[pub])
                    fw.op("act", lambda e: e.activation(out=Gb[:, 0:2], in_=H[:, fc, :], func=AF.Copy),
                          reads=[H], writes=[Gb])
                    fw.op("act", lambda e: e.activation(out=Gb[:, 2:514], in_=pgb[:], func=AF.Copy), reads=[pgb], writes=[Gb])
                    fw.op("act", lambda e: e.activation(out=H[:, fc, :], in_=Gb[:, 512:514], func=AF.Copy),
                          reads=[Gb], writes=[H])
                    fw.op("act", lambda e: e.activation(out=t1b[:], in_=Gb[:, 0:512], func=AF.Copy, scale=cw_sb[:, fc, 0:1]),
                          reads=[Gb, cw_sb], writes=[t1b])
                    fw.op("dve", lambda e: e.scalar_tensor_tensor(out=t1b[:], in0=Gb[:, 1:513], scalar=cw_sb[:, fc, 1:2],
                                                                  in1=t1b[:], op0=ALU.mult, op1=ALU.add),
                          reads=[Gb, cw_sb, t1b], writes=[t1b])
                    fw.op("dve", lambda e: e.scalar_tensor_tensor(out=t1b[:], in0=Gb[:, 2:514], scalar=cw_sb[:, fc, 2:3],
                                                                  in1=t1b[:], op0=ALU.mult, op1=ALU.add),
                          reads=[Gb, cw_sb, t1b], writes=[t1b])
                    fw.op("act", lambda e: e.activation(out=slb[:], in_=t1b[:], func=AF.Silu, bias=cb_sb[:, fc:fc + 1], scale=1.0),
                          reads=[t1b, cb_sb], writes=[slb])
                    fw.op("dve", lambda e: e.tensor_tensor(out=aT[fc][:], in0=slb[:], in1=pub[:], op=ALU.mult),
                          reads=[slb, pub], writes=[aT[fc]])
            for s in range(4):
                h1b = h1[s]
                r0 = ti * 512 + s * 128
                sidx = ti * 4 + s
                for half in range(2):
                    for fc in range(NFC):
                        fw.op("pe", lambda e: e.matmul(pj[half][:], lhsT=aT[fc][:, s * 128:(s + 1) * 128],
                                                       rhs=wd_sb[:, fc, half * 512:(half + 1) * 512],
                                                       start=(fc == 0), stop=(fc == NFC - 1)),
                              reads=[aT[fc], wd_sb], writes=[pj[half]])
                for half in range(2):
                    fw.op("dve", lambda e: e.tensor_tensor(out=h1b[:, half * 512:(half + 1) * 512],
                                                           in0=h1b[:, half * 512:(half + 1) * 512],
                                                           in1=pj[half][:], op=ALU.add),
                          reads=[h1b, pj[half]], writes=[h1b])
                if hout is not None:
                    fw.dma("sp", hout.t[r0:r0 + 128, :], h1b[:], reads=[h1b])
                    if tail is not None and ti == n_tiles - 1 and s == 3:
                        fw.dma("sp", tail.t[:, :], h1b[:], reads=[h1b])
                stb = stat2[sidx % 2]
                nob = no[sidx % 2]
                rms(h1b, stb)
                fw.op("dve", lambda e: e.scalar_tensor_tensor(out=nob[:], in0=h1b[:], scalar=stb[:, 2:3], in1=gnB[:],
                                                              op0=ALU.mult, op1=ALU.mult),
                      reads=[h1b, stb, gnB], writes=[nob])
                if final:
                    fw.dma("sp", nout.t[r0:r0 + 128, :], nob[:], reads=[nob])
                else:
                    for kc in range(8):
                        fw.op("pe", lambda e: e.transpose(tp[:, kc * 128:(kc + 1) * 128], nob[:, kc * 128:(kc + 1) * 128], ident[:]),
                              reads=[nob, ident], writes=[tp])
                    ns = nstg[ti % 2]
                    fw.op("act", lambda e: e.activation(out=ns[:, :, s * 128:(s + 1) * 128],
                                                        in_=tp[:].rearrange("p (k t) -> p k t", k=8), func=AF.Copy),
                          reads=[tp], writes=[ns])
                    if s == 3:
                        fw.dma("sp", xnS.t.rearrange("kc p t -> p kc t")[:, :, ti * 512:(ti + 1) * 512], ns[:], reads=[ns])
        fw.barrier()
        fw.es = old_es

D = 1024
S = 8192
NH = 8
DH = 64
BLK = 256
NB = S // BLK
BIG = 30000.0
SCALE = 1.0 / math.sqrt(DH)
RMS_EPS = 1e-6


def phase_a(fw, ident, identf, x, gmix, wqkv, cos, sinm, QT, KT, out_fn, n_heads=NH, n_sub=S // 128):
    S_loc = n_sub * 128
    NBL = S_loc // BLK
    nc = fw.nc
    nqt = n_sub // 4
    with ExitStack() as es0:
        old_es = fw.es
        fw.es = es0
        V_sb = fw.sbuf([128, n_sub, NH, 65], BF16, "V_sb")
        ssq_q = fw.sbuf([128, n_sub, NH], F32, "ssq_q")
        kmax2 = fw.sbuf([128, NH], F32, "kmax2")
        fw.op("pool", lambda e: e.memset(V_sb[:], 1.0), writes=[V_sb])
        fw.op("pool", lambda e: e.memset(kmax2[:], 0.0), writes=[kmax2])
        with ExitStack() as es1:
            fw.es = es1
            w_sb = fw.sbuf([128, 8, 1536], BF16, "wqkv_sb")
            gB = fw.sbuf([128, 1024], F32, "gB")
            xb = [fw.sbuf([128, 1024], F32, "xb%d" % i) for i in range(2)]
            xn = [fw.sbuf([128, 1024], BF16, "xn%d" % i) for i in range(2)]
            xnT = [fw.sbuf([128, 8, 128], BF16, "xnT%d" % i) for i in range(2)]
            cs = [fw.sbuf([128, 2, 64], F32, "cs%d" % i) for i in range(2)]
            tcb = fw.sbuf([128, NH, 64], F32, "tcb")
            trb = fw.sbuf([128, NH, 64], F32, "trb")
            qk_tok = [fw.sbuf([128, 2, 512], BF16, "qktok%d" % i) for i in range(2)]
            stg = [fw.sbuf([128, 2, 4, 512], BF16, "stg%d" % i) for i in range(2)]
            sqj = fw.sbuf([128, 1024], F32, "sqj")
            sqq = fw.sbuf([128, 512], F32, "sqq")
            sqk = fw.sbuf([128, 512], F32, "sqk")
            ssk = fw.sbuf([128, NH], F32, "ssk")
            stat = [fw.sbuf([128, 4], F32, "stat%d" % i) for i in range(2)]
            tp = fw.psum([128, 1024], BF16, "tp")
            pqs = [fw.psum([128, 512], F32, "pq%d" % i) for i in range(2)]
            pks = [fw.psum([128, 512], F32, "pk%d" % i) for i in range(2)]
            pv = fw.psum([128, 512], F32, "pv")
            tqk = fw.psum([128, 2, 512], BF16, "tqk")

            w_v = wqkv.t.rearrange("(kc p) n -> p kc n", p=128)
            fw.dma("pool", [(w_sb[:, kc:kc + 2, :], w_v[:, kc:kc + 2, :]) for kc in range(0, 8, 2)], writes=[w_sb])
            fw.dma("sp", gB[:], gmix.t.partition_broadcast(128), writes=[gB])
            QT_v = QT.t.rearrange("(pr p) t -> p pr t", p=128)
            KT_v = KT.t.rearrange("(pr p) t -> p pr t", p=128)

            def stage1(st):
                b2 = st % 2
                r0 = st * 128
                fw.dma("sp", xb[b2][:], x.t[r0:r0 + 128, :], writes=[xb[b2]])
                fw.dma("sp", [(cs[b2][:, 0, :], cos.t[r0:r0 + 128, :]), (cs[b2][:, 1, :], sinm.t[r0:r0 + 128, :])], writes=[cs[b2]])
                stb = stat[b2]
                fw.op("dve", lambda e: e.memset(stb[:, 0:1], 0.0), writes=[stb])
                fw.op("act", lambda e: e.activation(out=sqj[:], in_=xb[b2][:], func=AF.Square, accum_out=stb[:, 0:1]),
                      reads=[xb[b2]], writes=[sqj, stb])
                fw.op("act", lambda e: e.activation(out=stb[:, 1:2], in_=stb[:, 0:1], func=AF.Sqrt, scale=1.0 / D, bias=RMS_EPS),
                      reads=[stb], writes=[stb])
                fw.op("dve", lambda e: e.reciprocal(out=stb[:, 2:3], in_=stb[:, 1:2]), reads=[stb], writes=[stb])
                fw.op("dve", lambda e: e.scalar_tensor_tensor(out=xn[b2][:], in0=xb[b2][:], scalar=stb[:, 2:3], in1=gB[:],
                                                              op0=ALU.mult, op1=ALU.mult),
                      reads=[xb[b2], stb, gB], writes=[xn[b2]])
                for kc in range(8):
                    fw.op("pe", lambda e: e.transpose(tp[:, kc * 128:(kc + 1) * 128], xn[b2][:, kc * 128:(kc + 1) * 128], ident[:]),
                          reads=[xn[b2], ident], writes=[tp])
                fw.op("act", lambda e: e.activation(out=xnT[b2][:], in_=tp[:].rearrange("p (k t) -> p k t", k=8), func=AF.Copy),
                      reads=[tp], writes=[xnT[b2]])
            def stage2(st):
                b2 = st % 2
                pq = pqs[st % 2]
                pk = pks[st % 2]
                for (pp, c0) in ((pq, 0), (pk, 512), (pv, 1024)):
                    for kc in range(8):
                        fw.op("pe", lambda e: e.matmul(pp[:], lhsT=xnT[b2][:, kc, :], rhs=w_sb[:, kc, c0:c0 + 512],
                                                       start=(kc == 0), stop=(kc == 7)),
                              reads=[xnT[b2], w_sb], writes=[pp])
                fw.op("act", lambda e: e.activation(out=V_sb[:, st, :, 0:64], in_=pv[:].rearrange("p (h d) -> p h d", h=NH), func=AF.Copy),
                      reads=[pv], writes=[V_sb])
                cosB = cs[b2][:, 0, :].unsqueeze(1).to_broadcast([128, NH, 64])
                sinB = cs[b2][:, 1, :].unsqueeze(1).to_broadcast([128, NH, 64])
                qkb = qk_tok[b2]
                for qi, pp in enumerate((pq, pk)):
                    pv3 = pp[:].rearrange("p (h d) -> p h d", h=NH)
                    sqx = sqq if qi == 0 else sqk
                    fw.op("act", lambda e: e.activation(out=sqx[:], in_=pp[:], func=AF.Square), reads=[pp], writes=[sqx])
                    if qi == 0:
                        fw.op("dve", lambda e: e.tensor_reduce(out=ssq_q[:, st, :], in_=sqx[:].rearrange("p (h d) -> p h d", h=NH),
                                                               axis=AX.X, op=ALU.add), reads=[sqx], writes=[ssq_q])
                    else:
                        fw.op("dve", lambda e: e.tensor_reduce(out=ssk[:], in_=sqx[:].rearrange("p (h d) -> p h d", h=NH),
                                                               axis=AX.X, op=ALU.add), reads=[sqx], writes=[ssk])
                        fw.op("dve", lambda e: e.tensor_tensor(out=kmax2[:], in0=kmax2[:], in1=ssk[:], op=ALU.max),
                              reads=[ssk, kmax2], writes=[kmax2])
                    fw.op("dve", lambda e: e.tensor_tensor(out=tcb[:], in0=pv3, in1=cosB, op=ALU.mult),
                          reads=[pp, cs[b2]], writes=[tcb])
                    fw.op("dve", lambda e: e.tensor_tensor(out=trb[:, :, 0:32], in0=pv3[:, :, 32:64], in1=sinB[:, :, 0:32], op=ALU.mult),
                          reads=[pp, cs[b2]], writes=[trb])
                    fw.op("dve", lambda e: e.tensor_tensor(out=trb[:, :, 32:64], in0=pv3[:, :, 0:32], in1=sinB[:, :, 32:64], op=ALU.mult),
                          reads=[pp, cs[b2]], writes=[trb])
                    fw.op("pool", lambda e: e.tensor_tensor(out=qkb[:, qi, :].rearrange("p (h d) -> p h d", h=NH), in0=tcb[:], in1=trb[:], op=ALU.add),
                          reads=[tcb, trb], writes=[qkb])
            def stage3(st):
                b2 = st % 2
                qkb = qk_tok[b2]
                for qi in range(2):
                    for pr in range(4):
                        fw.op("pe", lambda e: e.transpose(tqk[:, qi, pr * 128:(pr + 1) * 128], qkb[:, qi, pr * 128:(pr + 1) * 128], ident[:]),
                              reads=[qkb, ident], writes=[tqk])
                sg = stg[(st // 4) % 2]
                slot = st % 4
                fw.op("act", lambda e: e.activation(out=sg[:, :, :, slot * 128:(slot + 1) * 128],
                                                    in_=tqk[:].rearrange("p a (r t) -> p a r t", r=4), func=AF.Copy),
                      reads=[tqk], writes=[sg])
                if slot == 3:
                    t0 = (st // 4) * 512
                    fw.dma("sp", [(QT_v[:, :, t0:t0 + 512], sg[:, 0, :, :]), (KT_v[:, :, t0:t0 + 512], sg[:, 1, :, :])], reads=[sg])
            stage1(0)
            for st in range(n_sub):
                if st + 1 < n_sub:
                    stage1(st + 1)
                stage2(st)
                if st >= 1:
                    stage3(st - 1)
            stage3(n_sub - 1)
            fw.barrier()
        with ExitStack() as es2:
            fw.es = es2
            QA = [fw.sbuf([96, S_loc], BF16, "QA%d" % i) for i in range(2)]
            KA = [fw.sbuf([96, S_loc], BF16, "KA%d" % i) for i in range(2)]
            cm = [fw.sbuf([128, 512], BF16, "cm%d" % i) for i in range(4)]
            C2 = fw.sbuf([128, 64], F32, "C2")
            Dc = fw.sbuf([128, 64], F32, "Dc")
            Ec = fw.sbuf([128, 64], F32, "Ec")
            onesf = fw.sbuf([128, 128], F32, "onesf")
            kmf = fw.sbuf([96, NB], F32, "kmf")
            kmT = fw.sbuf([96, NB], BF16, "kmT")
            kmx = fw.sbuf([128, NH], F32, "kmx")
            kmxT = fw.sbuf([NH, 128], F32, "kmxT")
            kmr = fw.sbuf([NH, 2], F32, "kmr")
            kdiag = fw.sbuf([NH, NH], F32, "kdiag")
            stabm = fw.sbuf([128, n_sub, NH], F32, "stabm")
            gm = [fw.sbuf([128, NB], F32, "gm%d" % i) for i in range(2)]
            top8 = [fw.sbuf([128, 8], F32, "top8_%d" % i) for i in range(2)]
            s1 = [fw.sbuf([128, NB], F32, "s1_%d" % i) for i in range(2)]
            nmk = [fw.sbuf([128, 96], BF16, "nmk%d" % i) for i in range(2)]
            PT = [fw.sbuf([128, 512], BF16, "PT%d" % i) for i in range(3)]
            rd = [fw.sbuf([128, 512], F32, "rd%d" % i) for i in range(2)]
            bcs = fw.sbuf([64, 512], F32, "bcs")
            at = [fw.sbuf([64, 512], BF16, "at%d" % i) for i in range(2)]
            ps_s = [fw.psum([128, 512], F32, "ps_s%d" % i) for i in range(3)]
            ps_o = [fw.psum([128, 512], F32, "ps_o%d" % i) for i in range(2)]
            ps_b = fw.psum([128, 512], F32, "ps_b")
            ps_g = fw.psum([128, 512], F32, "ps_g")
            ps_t = fw.psum([128, 1024], BF16, "ps_t")

            for m in range(4):
                fw.op("pool", lambda e: e.memset(cm[m][:], 1.0), writes=[cm[m]])
                fw.op("pool", lambda e: e.affine_select(out=cm[m][:], in_=cm[m][:], pattern=[[1, 512]], compare_op=ALU.is_ge,
                                                        fill=0.0, base=-128 * m, channel_multiplier=-1),
                      reads=[cm[m]], writes=[cm[m]])
            fw.op("dve", lambda e: e.memset(C2[:, 0:32], 0.0), writes=[C2])
            fw.op("dve", lambda e: e.memset(C2[:, 32:64], -BIG), writes=[C2])
            fw.op("dve", lambda e: e.memset(Dc[:, 0:32], -2 * BIG), writes=[Dc])
            fw.op("dve", lambda e: e.memset(Dc[:, 32:64], 0.0), writes=[Dc])
            fw.op("dve", lambda e: e.memset(Ec[:, 0:33], 0.0), writes=[Ec])
            fw.op("dve", lambda e: e.memset(Ec[:, 33:64], -BIG), writes=[Ec])
            fw.op("dve", lambda e: e.memset(onesf[:], 1.0), writes=[onesf])
            fw.op("dve", lambda e: e.memset(kmT[:], 0.0), writes=[kmT])
            for i in range(2):
                fw.op("dve", lambda e: e.memset(nmk[i][:], 0.0), writes=[nmk[i]])
            for i in range(2):
                fw.op("pool", lambda e: e.memset(QA[i][64:96, :], 0.0), writes=[QA[i]])
                fw.op("pool", lambda e: e.memset(KA[i][64:96, :], 1.0), writes=[KA[i]])
                fw.op("pool", lambda e: e.affine_select(out=KA[i][64:96, :], in_=KA[i][64:96, :], pattern=[[1, S_loc]], compare_op=ALU.is_ge,
                                                        fill=0.0, base=0, channel_multiplier=-BLK), reads=[KA[i]], writes=[KA[i]])
                fw.op("pool", lambda e: e.affine_select(out=KA[i][64:96, :], in_=KA[i][64:96, :], pattern=[[-1, S_loc]], compare_op=ALU.is_ge,
                                                        fill=0.0, base=BLK - 1, channel_multiplier=BLK), reads=[KA[i]], writes=[KA[i]])
            fw.op("pe", lambda e: e.transpose(ps_g[0:NH, 0:128], kmax2[:], identf[:]), reads=[kmax2, identf], writes=[ps_g])
            fw.op("dve", lambda e: e.tensor_reduce(out=kmr[:, 0:1], in_=ps_g[0:NH, 0:128], axis=AX.X, op=ALU.max), reads=[ps_g], writes=[kmr])
            fw.op("dve", lambda e: e.tensor_scalar(out=kdiag[:], in0=identf[0:NH, 0:NH], scalar1=kmr[:, 0:1], scalar2=None, op0=ALU.mult),
                  reads=[identf, kmr], writes=[kdiag])
            fw.op("pe", lambda e: e.matmul(ps_b[:, 0:NH], lhsT=onesf[0:NH, :], rhs=kdiag[:], start=True, stop=True),
                  reads=[kdiag, onesf], writes=[ps_b])
            fw.op("dve", lambda e: e.tensor_copy(out=kmx[:], in_=ps_b[:, 0:NH]), reads=[ps_b], writes=[kmx])
            fw.op("dve", lambda e: e.tensor_tensor(out=stabm[:], in0=ssq_q[:], in1=kmx[:].unsqueeze(1).to_broadcast([128, n_sub, NH]), op=ALU.mult),
                  reads=[ssq_q, kmx], writes=[stabm])
            fw.op("act", lambda e: e.activation(out=stabm[:], in_=stabm[:], func=AF.Sqrt), reads=[stabm], writes=[stabm])
            fw.op("dve", lambda e: e.tensor_scalar(out=stabm[:], in0=stabm[:], scalar1=-1.0, scalar2=None, op0=ALU.mult),
                  reads=[stabm], writes=[stabm])

            def load_head(h):
                b = h % 2
                fw.dma("sp", QA[b][0:64, :], QT.t[h * 64:(h + 1) * 64, :], writes=[QA[b]])
                fw.dma("sp", KA[b][0:64, :], KT.t[h * 64:(h + 1) * 64, :], writes=[KA[b]])

            def gating(h):
                b = h % 2
                Qa, Ka = QA[b], KA[b]
                fw.op("dve", lambda e: e.tensor_reduce(out=kmf[0:64, 0:NBL], in_=Ka[0:64, :].rearrange("p (j t) -> p j t", t=BLK),
                                                       axis=AX.X, op=ALU.add), reads=[Ka], writes=[kmf])
                fw.op("act", lambda e: e.activation(out=kmT[0:64, 0:NBL], in_=kmf[0:64, 0:NBL], func=AF.Copy, scale=1.0 / BLK),
                      reads=[kmf], writes=[kmT])
                for qs in range(n_sub):
                    qb = qs // 2
                    g16 = qs % 16
                    if g16 == 0:
                        pass
                    fw.op("pe", lambda e: e.matmul(ps_g[:, g16 * NB:(g16 + 1) * NB], lhsT=Qa[:, qs * 128:(qs + 1) * 128], rhs=kmT[:],
                                                   start=True, stop=True), reads=[Qa, kmT], writes=[ps_g])
                    if g16 == 15 or qs == n_sub - 1:
                        for q2 in range(qs - g16, qs + 1):
                            qb2 = q2 // 2
                            gg = q2 % 16
                            i2 = q2 % 2
                            lo = 32 - qb2
                            fw.op("dve", lambda e: e.tensor_tensor(out=gm[i2][:], in0=ps_g[:, gg * NB:(gg + 1) * NB], in1=C2[:, lo:lo + NB], op=ALU.add),
                                  reads=[ps_g, C2], writes=[gm[i2]])
                            fw.op("dve", lambda e: e.max(out=top8[i2][:], in_=gm[i2][:]), reads=[gm[i2]], writes=[top8[i2]])
                            fw.op("dve", lambda e: e.tensor_scalar(out=s1[i2][:], in0=gm[i2][:], scalar1=top8[i2][:, 2:3], scalar2=BIG,
                                                                   op0=ALU.is_ge, op1=ALU.mult),
                                  reads=[gm[i2], top8[i2]], writes=[s1[i2]])
                            fw.op("dve", lambda e: e.scalar_tensor_tensor(out=s1[i2][:], in0=s1[i2][:], scalar=-BIG, in1=Dc[:, lo:lo + NB],
                                                                          op0=ALU.add, op1=ALU.max),
                                  reads=[s1[i2], Dc], writes=[s1[i2]])
                            fw.op("dve", lambda e: e.scalar_tensor_tensor(out=nmk[i2][:, 64:96], in0=s1[i2][:], scalar=stabm[:, q2, h:h + 1], in1=Ec[:, lo:lo + NB],
                                                                          op0=ALU.add, op1=ALU.add),
                                  reads=[s1[i2], stabm, Ec], writes=[nmk[i2]])
                            t8 = q2 % 8
                            fw.op("pe", lambda e: e.transpose(ps_t[0:96, t8 * 128:(t8 + 1) * 128], nmk[i2][:], ident[:]),
                                  reads=[nmk[i2], ident], writes=[ps_t])
                            if t8 == 7 or q2 == n_sub - 1:
                                c0 = (q2 - t8) * 128
                                fw.op("act", lambda e: e.activation(out=Qa[64:96, c0:c0 + (t8 + 1) * 128], in_=ps_t[64:96, 0:(t8 + 1) * 128], func=AF.Copy),
                                      reads=[ps_t], writes=[Qa])
                            yield

            load_head(0)
            for _ in gating(0):
                pass
            ev = 0
            for h in range(n_heads):
                b = h % 2
                Qa, Ka = QA[b], KA[b]
                if h + 1 < n_heads:
                    load_head(h + 1)
                iters = [(qt, kt) for qt in range(nqt) for kt in range(4 * (qt + 1))]
                LA = 2
                pend = []

                def emit_S(i):
                    qt, kt = iters[i]
                    pss = ps_s[i % 3]
                    ptb = PT[i % 3]
                    fw.op("pe", lambda e: e.matmul(pss[:], lhsT=Ka[:, kt * 128:(kt + 1) * 128], rhs=Qa[:, qt * 512:(qt + 1) * 512],
                                                   start=True, stop=True), reads=[Ka, Qa], writes=[pss])
                    fw.op("act", lambda e: e.activation(out=ptb[:], in_=pss[:], func=AF.Exp, scale=SCALE), reads=[pss], writes=[ptb])
                    m = kt - 4 * qt
                    if m >= 0:
                        fw.op("dve", lambda e: e.tensor_tensor(out=ptb[:], in0=ptb[:], in1=cm[m][:], op=ALU.mult),
                              reads=[ptb, cm[m]], writes=[ptb])

                def emit_PV(i):
                    qt, kt = iters[i]
                    nkt = 4 * (qt + 1)
                    po = ps_o[qt % 2]
                    ptb = PT[i % 3]
                    fw.op("pe", lambda e: e.matmul(po[0:65, :], lhsT=V_sb[:, kt, h, :], rhs=ptb[:], start=(kt == 0), stop=(kt == nkt - 1)),
                          reads=[V_sb, ptb], writes=[po])
                    if kt == nkt - 1:
                        rdb = rd[qt % 2]
                        fw.op("dve", lambda e: e.reciprocal(out=rdb[64:65, :], in_=po[64:65, :]), reads=[po], writes=[rdb])

                        def tail(qt=qt, po=po, rdb=rdb):
                            fw.op("pe", lambda e: e.matmul(ps_b[0:64, :], lhsT=onesf[64:65, 0:64], rhs=rdb[64:65, :], start=True, stop=True),
                                  reads=[rdb, onesf], writes=[ps_b])
                            fw.op("dve", lambda e: e.tensor_copy(out=bcs[:], in_=ps_b[0:64, :]), reads=[ps_b], writes=[bcs])
                            ab = at[qt % 2]
                            fw.op("dve", lambda e: e.tensor_tensor(out=ab[:], in0=po[0:64, :], in1=bcs[:], op=ALU.mult), reads=[po, bcs], writes=[ab])
                            fw.dma("sp", [(dst, ab[:, c_a:c_b]) for (dst, c_a, c_b) in out_fn(h, qt)], reads=[ab])
                        pend.append([3, tail])

                gnext = gating(h + 1) if h + 1 < n_heads else None
                for i in range(len(iters) + LA):
                    if gnext is not None and i % 6 == 5:
                        try:
                            next(gnext)
                        except StopIteration:
                            gnext = None
                    if i < len(iters):
                        emit_S(i)
                    if i - LA >= 0:
                        emit_PV(i - LA)
                    for pe_ in list(pend):
                        pe_[0] -= 1
                        if pe_[0] <= 0:
                            pend.remove(pe_)
                            pe_[1]()
                for pe_ in pend:
                    pe_[1]()
                if gnext is not None:
                    for _ in gnext:
                        pass
            fw.barrier()
        fw.es = old_es

D = 1024
T = 128
C0 = math.exp(-0.5)
LNX_EPS = 64e-5


def phase_r(fw, identb, identf, x_fn, mu, wr, wk, wv, w1, a1, g1, w2, a2, g2, pf, lnxw, lnxb, zout_fn, n_tiles, stop=99):
    nc = fw.nc
    with ExitStack() as es:
        old_es = fw.es
        fw.es = es
        sb = fw.sbuf
        wr_sb = sb([128, 8, 512], BF16, "wr_sb"); wk_sb = sb([128, 8, 512], BF16, "wk_sb"); wv_sb = sb([128, 8, 512], BF16, "wv_sb")
        w1_sb = sb([128, 8, 64], BF16, "w1_sb"); a1_sb = sb([128, 8, 64], BF16, "a1_sb"); g1_sb = sb([128, 8, 128], BF16, "g1_sb")
        w2_sb = sb([64, 512], BF16, "w2_sb"); a2_sb = sb([64, 512], BF16, "a2_sb"); g2_sb = sb([128, 512], BF16, "g2_sb")
        mu_sb = sb([128, 6, 8], F32, "mu_sb"); pf_sb = sb([128, 4, 5], F32, "pf_sb")
        lwB = sb([128, 512], F32, "lwB"); lbB = sb([128, 512], F32, "lbB")
        for (dst, src) in ((wr_sb, wr), (wk_sb, wk), (wv_sb, wv), (w1_sb, w1), (a1_sb, a1), (g1_sb, g1)):
            v = src.t.rearrange("(kc p) n -> p kc n", p=128)
            fw.dma("pool", [(dst[:, 0:4, :], v[:, 0:4, :]), (dst[:, 4:8, :], v[:, 4:8, :])], writes=[dst])
        fw.dma("pool", w2_sb[:], w2.t[:, :], writes=[w2_sb])
        fw.dma("pool", a2_sb[:], a2.t[:, :], writes=[a2_sb])
        fw.dma("pool", g2_sb[:], g2.t[:, :], writes=[g2_sb])
        fw.dma("sp", mu_sb[:], mu.t[:, :, :], writes=[mu_sb])
        fw.dma("sp", pf_sb[:], pf.t[:, :, :], writes=[pf_sb])
        fw.dma("sp", lwB[:], lnxw.t.partition_broadcast(128), writes=[lwB])
        fw.dma("sp", lbB[:], lnxb.t.partition_broadcast(128), writes=[lbB])
        rmask = sb([128, 512], F32, "rmask")
        fw.op("pool", lambda e: e.memset(rmask[:], 1.0), writes=[rmask])
        for ch in range(4):
            fw.op("pool", lambda e: e.memset(rmask[:, ch * T:ch * T + 1], 0.0), writes=[rmask])
        MK_SI = sb([128, 2, 2, 128], BF16, "MK_SI")
        MK_SL = sb([128, 4, 128], BF16, "MK_SL")
        fw.op("pool", lambda e: e.memset(MK_SI[:], 1.0), writes=[MK_SI])
        fw.op("pool", lambda e: e.memset(MK_SL[:], 1.0), writes=[MK_SL])
        for cb in range(2):
            fw.op("pool", lambda e: e.affine_select(out=MK_SI[:, cb, 0, :], in_=MK_SI[:, cb, 0, :], pattern=[[1, 128]], compare_op=ALU.is_gt,
                                                    fill=0.0, base=0, channel_multiplier=-1), reads=[MK_SI], writes=[MK_SI])
            fw.op("pool", lambda e: e.affine_select(out=MK_SI[:, cb, 1, :], in_=MK_SI[:, cb, 1, :], pattern=[[1, 128]], compare_op=ALU.is_ge,
                                                    fill=0.0, base=0, channel_multiplier=-1), reads=[MK_SI], writes=[MK_SI])
        for ch in range(4):
            fw.op("pool", lambda e: e.affine_select(out=MK_SL[:, ch, :], in_=MK_SL[:, ch, :], pattern=[[-1, 128]], compare_op=ALU.is_gt,
                                                    fill=0.0, base=0, channel_multiplier=1), reads=[MK_SL], writes=[MK_SL])
        Ind8 = sb([128, 4, 8], BF16, "Ind8")
        fw.op("pool", lambda e: e.memset(Ind8[:], 0.0), writes=[Ind8])
        for c in range(4):
            fw.op("pool", lambda e: e.memset(Ind8[0:64, c, 2 * c:2 * c + 1], 1.0), writes=[Ind8])
            fw.op("pool", lambda e: e.memset(Ind8[64:128, c, 2 * c + 1:2 * c + 2], 1.0), writes=[Ind8])
        BOnes = sb([128, 128], BF16, "BOnes")
        fw.op("pool", lambda e: e.memset(BOnes[:], 0.0), writes=[BOnes])
        fw.op("pool", lambda e: e.memset(BOnes[0:64, 0:64], 1.0), writes=[BOnes])
        fw.op("pool", lambda e: e.memset(BOnes[64:128, 64:128], 1.0), writes=[BOnes])
        xTb = [sb([128, 8, 514], BF16, "xTb0")] * 2
        xxa = sb([128, 8, 512], BF16, "xxa")
        mix = [sb([128, 8, 512], BF16, "mix%d" % i) for i in range(2)]
        thw = sb([64, 512], BF16, "thw"); tha = sb([64, 512], BF16, "tha"); thg = sb([128, 512], BF16, "thg")
        AR = sb([128, 4, 4, 2, 128], BF16, "AR")
        BT = sb([128, 4, 512], BF16, "BT"); KTt = sb([128, 4, 512], BF16, "KTt")
        bpT = [sb([128, 512], BF16, "bpT%d" % i) for i in range(2)]
        kpT = [sb([128, 512], BF16, "kpT%d" % i) for i in range(2)]
        Atok = sb([128, 4, 512], BF16, "Atok"); Bp = sb([128, 4, 512], BF16, "Bp"); Kp = sb([128, 4, 512], BF16, "Kp")
        Vtoks = [sb([128, 4, 512], BF16, "Vtok%d" % i) for i in range(2)]; gtoks = [sb([128, 4, 512], BF16, "gtok%d" % i) for i in range(2)]
        prodT = sb([128, 4, 512], BF16, "prodT")
        rkss = [sb([128, 4, 8], F32, "rks%d" % i) for i in range(2)]
        Gend = sb([128, 4, 4], F32, "Gend")
        Ysb = sb([128, 4, 512], F32, "Ysb")
        sg = sb([128, 512], F32, "sg"); al = sb([128, 512], F32, "al")
        Ec = sb([128, 512], F32, "Ec"); Em = sb([128, 512], F32, "Em"); Ee = sb([128, 512], F32, "Ee")
        e1 = Em; e2 = sb([128, 512], F32, "e2"); e3 = sb([128, 512], F32, "e3"); e4 = Ee
        rsb = sb([128, 512], F32, "rsb"); ksb = sb([128, 512], F32, "ksb"); kkf = sb([128, 512], F32, "kkf"); sqb = sb([128, 512], BF16, "sqb"); nrm = sb([128, 512], F32, "nrm")
        kkn = sb([128, 512], F32, "kkn"); tm1 = sb([128, 512], F32, "tm1"); kmod = sb([128, 512], F32, "kmod"); ka = sb([128, 512], F32, "ka")
        Zs = [sb([128, 4, 64], BF16, "Zs%d" % i) for i in range(2)]
        for i in range(2):
            fw.op("pool", lambda e: e.memset(Zs[i][:], 0.0), writes=[Zs[i]])
        NM1 = [sb([128, 4, 2, 128], BF16, "NM1_%d" % i) for i in range(2)]
        LM = [sb([128, 4, 2, 128], BF16, "LM_%d" % i) for i in range(2)]
        Npp = [[sb([128, 4, 128], BF16, "N_%d_%d" % (i, j)) for j in range(2)] for i in range(2)]
        Ntp = [[sb([128, 4, 128], BF16, "Nt_%d_%d" % (i, j)) for j in range(2)] for i in range(2)]
        Xp = [[sb([128, 4, 128], BF16, "X_%d_%d" % (i, j)) for j in range(2)] for i in range(2)]
        GT = [sb([128, 4, 128], BF16, "GT%d" % i) for i in range(2)]
        PTm = [sb([128, 4, 64], BF16, "PTm%d" % i) for i in range(2)]
        for i in range(2):
            fw.op("pool", lambda e: e.memset(GT[i][:], 0.0), writes=[GT[i]])
            fw.op("pool", lambda e: e.memset(PTm[i][:], 0.0), writes=[PTm[i]])
        ysqs = [sb([128, 512], BF16, "ysq%d" % i) for i in range(4)]; bvs = [sb([128, 512], BF16, "bv%d" % i) for i in range(4)]
        mtmpb = [ysqs[0], ysqs[1]]
        st8 = [sb([128, 8, 8], F32, "st8_%d" % i) for i in range(4)]
        zbs = [sb([128, 512], BF16, "zb%d" % i) for i in range(4)]
        zstg = [sb([128, 4, 512], BF16, "zstg0")] * 2
        pbA = [fw.psum([128, 512], F32, "pbA%d" % i) for i in range(2)]
        pbB = [fw.psum([128, 512], F32, "pbB%d" % i) for i in range(3)]
        psYt = fw.psum([128, 4, 128], F32, "psY")
        psZt = fw.psum([128, 512], F32, "psZ")
        psYs = [psYt, psYt]
        psZs = [psZt, psZt]
        ptb = fw.psum([128, 1024], BF16, "ptb")
        pbi = [0, 0]

        def nbA():
            b = pbA[pbi[0] % 2]
            pbi[0] += 1
            return b

        def nbB():
            b = pbB[pbi[1] % 3]
            pbi[1] += 1
            return b

        Ysbs = [Buf(Ysb.t, "Ysb_s%d" % s) for s in range(4)]
        ARc = [Buf(AR.t, "AR_c%d" % c) for c in range(4)]
        BTc = [Buf(BT.t, "BT_c%d" % c) for c in range(4)]
        KTc = [Buf(KTt.t, "KT_c%d" % c) for c in range(4)]
        Atc = [Buf(Atok.t, "At_c%d" % c) for c in range(4)]
        Bpc = [Buf(Bp.t, "Bp_c%d" % c) for c in range(4)]
        Kpc = [Buf(Kp.t, "Kp_c%d" % c) for c in range(4)]
        prc = [Buf(prodT.t, "pr_c%d" % c) for c in range(4)]
        Gec = [Buf(Gend.t, "Ge_c%d" % c) for c in range(4)]


        v4 = lambda ap: ap.rearrange("p (c t) -> p c t", t=T)
        st_mix = {}

        def P(ti):
            t0 = ti * 512
            xb = xTb[ti % 2]
            Vtok = Vtoks[ti % 2]; gtok = gtoks[ti % 2]
            if ti == 0:
                fw.op("pool", lambda e: e.memset(xb[:, :, 0:2], 0.0), writes=[xb])
            for (lo, hi, src) in x_fn(t0):
                if hi - lo == 1:
                    fw.dma("sp", xb[:, :, lo + 1:hi + 1], src, writes=[xb], allow_slow_non_contiguous=True)
                else:
                    fw.dma("sp", xb[:, :, lo + 1:hi + 1], src, writes=[xb])
            mixi = [0]

            xxb = [None]

            def make_mix(n):
                m = mix[mixi[0] % 2]
                mixi[0] += 1
                for kc in range(8):
                    tmpm = mtmpb[kc % 2]
                    fw.op("dve", lambda e: e.tensor_scalar(out=tmpm[:], in0=xxa[:, kc, :], scalar1=mu_sb[:, n, kc:kc + 1], scalar2=None, op0=ALU.mult),
                          reads=[xxa, mu_sb], writes=[tmpm])
                    fw.op("dve", lambda e: e.tensor_tensor(out=m[:, kc, :], in0=tmpm[:], in1=xb[:, kc, 2:514], op=ALU.add),
                          reads=[tmpm, xb], writes=[m])
                return m

            for kc in range(8):
                fw.op("dve", lambda e: e.tensor_tensor(out=xxa[:, kc, :], in0=xb[:, kc, 1:513], in1=xb[:, kc, 2:514], op=ALU.subtract),
                      reads=[xb], writes=[xxa])
            m = make_mix(1)
            p = nbA(); proj_fm(m, w1_sb, 0, 64, p)
            fw.op("act", lambda e: e.activation(out=thw[:], in_=p[0:64, :], func=AF.Tanh), reads=[p], writes=[thw])
            yield
            m = make_mix(4)
            p = nbA(); proj_fm(m, a1_sb, 0, 64, p)
            fw.op("act", lambda e: e.activation(out=tha[:], in_=p[0:64, :], func=AF.Copy), reads=[p], writes=[tha])
            yield
            m = make_mix(5)
            p = nbA(); proj_fm(m, g1_sb, 0, 128, p)
            fw.op("act", lambda e: e.activation(out=thg[:], in_=p[:], func=AF.Sigmoid), reads=[p], writes=[thg])
            yield
            for sub in range(4):
                p = nbA()
                fw.op("pe", lambda e: e.matmul(p[:], lhsT=thg[:, sub * 128:(sub + 1) * 128], rhs=g2_sb[:], start=True, stop=True),
                      reads=[thg, g2_sb], writes=[p])
                fw.op("act", lambda e: e.activation(out=gtok[:, sub, :], in_=p[:], func=AF.Copy), reads=[p], writes=[gtok])
            yield
            m = make_mix(3)
            for sub in range(4):
                p = nbA()
                for kc in range(8):
                    fw.op("pe", lambda e: e.matmul(p[:], lhsT=m[:, kc, sub * 128:(sub + 1) * 128], rhs=wv_sb[:, kc, :], start=(kc == 0), stop=(kc == 7)),
                          reads=[m, wv_sb], writes=[p])
                fw.op("act", lambda e: e.activation(out=Vtok[:, sub, :], in_=p[:], func=AF.Copy), reads=[p], writes=[Vtok])
            yield
            mr = make_mix(0)
            yield
            mk = make_mix(2)
            st_mix[ti] = (mr, mk)
            yield

        def proj_fm(m, w_sb, c0, ncol, p):
            for kc in range(8):
                fw.op("pe", lambda e: e.matmul(p[0:ncol, :], lhsT=w_sb[:, kc, c0:c0 + ncol], rhs=m[:, kc, :], start=(kc == 0), stop=(kc == 7)),
                      reads=[w_sb, m], writes=[p])

        def R1a(ti, c):
            mr, mk = st_mix[ti]
            pr = nbA(); proj_fm(mr, wr_sb, c * 128, 128, pr)
            fw.op("act", lambda e: e.activation(out=rsb[:], in_=pr[:], func=AF.Copy), reads=[pr], writes=[rsb])
            yield
            pk = nbA(); proj_fm(mk, wk_sb, c * 128, 128, pk)
            fw.op("act", lambda e: e.activation(out=ksb[:], in_=pk[:], func=AF.Copy), reads=[pk], writes=[ksb])
            yield
            pw = nbA()
            fw.op("pe", lambda e: e.matmul(pw[:], lhsT=w2_sb[:, c * 128:(c + 1) * 128], rhs=thw[:], start=True, stop=True),
                  reads=[w2_sb, thw], writes=[pw])
            fw.op("act", lambda e: e.activation(out=sg[:], in_=pw[:], func=AF.Sigmoid, bias=pf_sb[:, c, 0:1], scale=1.0),
                  reads=[pw, pf_sb], writes=[sg])
            yield
            pa = nbA()
            fw.op("pe", lambda e: e.matmul(pa[:], lhsT=a2_sb[:, c * 128:(c + 1) * 128], rhs=tha[:], start=True, stop=True),
                  reads=[a2_sb, tha], writes=[pa])
            fw.op("act", lambda e: e.activation(out=al[:], in_=pa[:], func=AF.Sigmoid, bias=pf_sb[:, c, 1:2], scale=1.0),
                  reads=[pa, pf_sb], writes=[al])
            yield
            fw.op("dve", lambda e: e.tensor_tensor_scan(out=Ec[:], data0=rmask[:], data1=sg[:], initial=0.0, op0=ALU.mult, op1=ALU.add),
                  reads=[rmask, sg], writes=[Ec])
            fw.op("dve", lambda e: e.tensor_tensor(out=Em[:], in0=Ec[:], in1=sg[:], op=ALU.subtract), reads=[Ec, sg], writes=[Em])
            Ec3 = Ec[:].rearrange("p (c t) -> p c t", t=T)
            fw.op("dve", lambda e: e.tensor_tensor(out=Ee[:].rearrange("p (c t) -> p c t", t=T), in0=Ec3[:, :, T - 1:T].to_broadcast([128, 4, T]),
                                                   in1=Ec3, op=ALU.subtract), reads=[Ec], writes=[Ee])
            yield
            fw.op("act", lambda e: e.activation(out=e1[:], in_=Em[:], func=AF.Exp, scale=-C0), reads=[Em], writes=[e1])
            fw.op("act", lambda e: e.activation(out=e2[:], in_=Ec[:], func=AF.Exp, scale=C0), reads=[Ec], writes=[e2])
            fw.op("act", lambda e: e.activation(out=e3[:], in_=Ec[:], func=AF.Exp, scale=-C0), reads=[Ec], writes=[e3])
            fw.op("act", lambda e: e.activation(out=e4[:], in_=Ee[:], func=AF.Exp, scale=-C0), reads=[Ee], writes=[e4])
            fw.op("act", lambda e: e.activation(out=Gend[:, c, :], in_=e3[:].rearrange("p (c t) -> p c t", t=T)[:, :, T - 1], func=AF.Copy),
                  reads=[e3], writes=[Gec[c]])
            yield
            fw.op("dve", lambda e: e.tensor_scalar(out=kkf[:], in0=ksb[:], scalar1=pf_sb[:, c, 2:3], scalar2=None, op0=ALU.mult),
                  reads=[ksb, pf_sb], writes=[kkf])
            fw.op("act", lambda e: e.activation(out=sqb[:], in_=kkf[:], func=AF.Square), reads=[kkf], writes=[sqb])
            yield
            pn = nbA()
            fw.op("pe", lambda e: e.matmul(pn[:], lhsT=BOnes[:], rhs=sqb[:], start=True, stop=True), reads=[BOnes, sqb], writes=[pn])
            fw.op("act", lambda e: e.activation(out=nrm[:], in_=pn[:], func=AF.Sqrt), reads=[pn], writes=[nrm])
            yield
            fw.op("dve", lambda e: e.tensor_scalar(out=nrm[:], in0=nrm[:], scalar1=1e-12, scalar2=None, op0=ALU.max), reads=[nrm], writes=[nrm])
            fw.op("dve", lambda e: e.reciprocal(out=nrm[:], in_=nrm[:]), reads=[nrm], writes=[nrm])
            fw.op("dve", lambda e: e.tensor_tensor(out=kkn[:], in0=kkf[:], in1=nrm[:], op=ALU.mult), reads=[kkf, nrm], writes=[kkn])
            yield
            fw.op("dve", lambda e: e.tensor_scalar(out=tm1[:], in0=al[:], scalar1=-1.0, scalar2=pf_sb[:, c, 3:4], op0=ALU.add, op1=ALU.mult),
                  reads=[al, pf_sb], writes=[tm1])
            fw.op("dve", lambda e: e.scalar_tensor_tensor(out=kmod[:], in0=tm1[:], scalar=1.0, in1=ksb[:], op0=ALU.add, op1=ALU.mult),
                  reads=[tm1, ksb], writes=[kmod])
            yield
            fw.op("dve", lambda e: e.scalar_tensor_tensor(out=AR[:, c, :, 0, :], in0=v4(kkn[:]), scalar=-1.0, in1=v4(e1[:]), op0=ALU.mult, op1=ALU.mult),
                  reads=[kkn, e1], writes=[ARc[c]])
            fw.op("pool", lambda e: e.tensor_tensor(out=AR[:, c, :, 1, :], in0=v4(rsb[:]), in1=v4(e3[:]), op=ALU.mult), reads=[rsb, e3], writes=[ARc[c]])
            fw.op("pool", lambda e: e.tensor_tensor(out=ka[:], in0=kkn[:], in1=al[:], op=ALU.mult), reads=[kkn, al], writes=[ka])
            fw.op("pool", lambda e: e.tensor_tensor(out=BT[:, c, :], in0=ka[:], in1=e2[:], op=ALU.mult), reads=[ka, e2], writes=[BTc[c]])
            fw.op("pool", lambda e: e.tensor_tensor(out=KTt[:, c, :], in0=kmod[:], in1=e2[:], op=ALU.mult), reads=[kmod, e2], writes=[KTc[c]])
            yield
            bpb = bpT[c % 2]; kpb = kpT[c % 2]
            fw.op("pool", lambda e: e.tensor_tensor(out=bpb[:], in0=ka[:], in1=e4[:], op=ALU.mult), reads=[ka, e4], writes=[bpb])
            fw.op("pool", lambda e: e.tensor_tensor(out=kpb[:], in0=kmod[:], in1=e4[:], op=ALU.mult), reads=[kmod, e4], writes=[kpb])
            fw.op("dve", lambda e: e.scalar_tensor_tensor(out=prodT[:, c, :], in0=rsb[:], scalar=pf_sb[:, c, 4:5], in1=kmod[:], op0=ALU.mult, op1=ALU.mult),
                  reads=[rsb, pf_sb, kmod], writes=[prc[c]])
            yield

        def R1b(ti, c):
            bpb = bpT[c % 2]; kpb = kpT[c % 2]
            for which, (src_fn, dst, dstb, srcbuf) in enumerate(((lambda sub: AR[:, c, sub, 0, :], Atok, Atc[c], ARc[c]),
                                                                 (lambda sub: bpb[:, sub * 128:(sub + 1) * 128], Bp, Bpc[c], bpb),
                                                                 (lambda sub: kpb[:, sub * 128:(sub + 1) * 128], Kp, Kpc[c], kpb))):
                for sub in range(4):
                    fw.op("pe", lambda e: e.transpose(ptb[:, sub * 128:(sub + 1) * 128], src_fn(sub), identb[:]), reads=[srcbuf, identb], writes=[ptb])
                if which != 1:
                    fw.op("act", lambda e: e.activation(out=dst[:, :, c * 128:(c + 1) * 128], in_=ptb[:, 0:512].rearrange("p (s f) -> p s f", s=4), func=AF.Copy),
                          reads=[ptb], writes=[dstb])
                else:
                    fw.op("dve", lambda e: e.tensor_copy(out=dst[:, :, c * 128:(c + 1) * 128], in_=ptb[:, 0:512].rearrange("p (s f) -> p s f", s=4)),
                          reads=[ptb], writes=[dstb])

        def RKS(ti):
            rks = rkss[ti % 2]
            for sub in range(4):
                p = nbA()
                for c in range(4):
                    fw.op("pe", lambda e: e.matmul(p[:, 0:8], lhsT=prodT[:, c, sub * 128:(sub + 1) * 128], rhs=Ind8[:, c, :], start=(c == 0), stop=(c == 3)),
                          reads=[prc[c], Ind8], writes=[p])
                fw.op("act", lambda e: e.activation(out=rks[:, sub, :], in_=p[:, 0:8], func=AF.Copy), reads=[p], writes=[rks])

        def R2a(ti, c):
            Vtok = Vtoks[ti % 2]
            hs = [(2 * c + hb, 64 * hb) for hb in range(2)]
            for hi, (h, r0) in enumerate(hs):
                for (lh, lhb, dstb) in ((BT, BTc[c], NM1[hi]), (KTt, KTc[c], LM[hi])):
                    for half in range(2):
                        p = nbB()
                        for cc in range(2):
                            ch = half * 2 + cc
                            fw.op("pe", lambda e: e.matmul(p[:, cc * 256:(cc + 1) * 256], lhsT=lh[r0:r0 + 64, c, ch * T:(ch + 1) * T],
                                                           rhs=AR[r0:r0 + 64, c, ch, :, :], start=True, stop=True),
                                  reads=[lhb, ARc[c]], writes=[p])
                        fw.op("dve", lambda e: e.tensor_tensor(out=dstb[:, half * 2:half * 2 + 2, :, :], in0=p[:].rearrange("p (a b i) -> p a b i", a=2, b=2),
                                                               in1=MK_SI[:], op=ALU.mult), reads=[p, MK_SI], writes=[dstb])
                p = nbB()
                for ch in range(4):
                    fw.op("pe", lambda e: e.matmul(p[:, ch * T:(ch + 1) * T], lhsT=AR[r0:r0 + 64, c, ch, 0, :], rhs=BT[r0:r0 + 64, c, ch * T:(ch + 1) * T],
                                                   start=True, stop=True), reads=[ARc[c], BTc[c]], writes=[p])
                fw.op("dve", lambda e: e.tensor_tensor(out=Npp[hi][0][:], in0=p[:].rearrange("p (c j) -> p c j", c=4), in1=MK_SL[:], op=ALU.mult),
                      reads=[p, MK_SL], writes=[Npp[hi][0]])
            for hi, (h, r0) in enumerate(hs):
                p = nbB()
                for ch in range(4):
                    fw.op("pe", lambda e: e.matmul(p[:, ch * 64:(ch + 1) * 64], lhsT=LM[hi][:, ch, 0, :], rhs=Vtok[:, ch, h * 64:(h + 1) * 64],
                                                   start=True, stop=True), reads=[LM[hi], Vtok], writes=[p])
                X0 = Xp[hi][0]
                fw.op("act", lambda e: e.activation(out=X0[:, :, 64:128], in_=p[:, 0:256].rearrange("p (c v) -> p c v", c=4), func=AF.Copy),
                      reads=[p], writes=[X0])
                fw.op("pool", lambda e: e.tensor_copy(out=X0[:, :, 0:64], in_=Atok[:, :, c * 128 + r0:c * 128 + r0 + 64]), reads=[Atc[c]], writes=[X0])

        def R2b(ti, c):
            Vtok = Vtoks[ti % 2]
            hs = [(2 * c + hb, 64 * hb) for hb in range(2)]
            for k in range(7):
                for hi, (h, r0) in enumerate(hs):
                    Xc = Xp[hi][k % 2]; Xn = Xp[hi][(k + 1) % 2]
                    Ntk = (lambda ch: NM1[hi][:, ch, 0, :]) if k == 0 else (lambda ch: Ntp[hi][k % 2][:, ch, :])
                    Ntbuf = NM1[hi] if k == 0 else Ntp[hi][k % 2]
                    Nk = Npp[hi][k % 2]
                    p = nbB()
                    for ch in range(4):
                        fw.op("pe", lambda e: e.matmul(p[:, ch * T:(ch + 1) * T], lhsT=identb[:], rhs=Xc[:, ch, :], start=True, stop=False),
                              reads=[identb, Xc], writes=[p])
                        fw.op("pe", lambda e: e.matmul(p[:, ch * T:(ch + 1) * T], lhsT=Ntk(ch), rhs=Xc[:, ch, :], start=False, stop=True),
                              reads=[Ntbuf, Xc], writes=[p])
                    fw.op("act", lambda e: e.activation(out=Xn[:], in_=p[:].rearrange("p (c v) -> p c v", c=4), func=AF.Copy), reads=[p], writes=[Xn])
                    if k < 6:
                        p2 = nbB()
                        for ch in range(4):
                            fw.op("pe", lambda e: e.matmul(p2[:, ch * T:(ch + 1) * T], lhsT=Nk[:, ch, :], rhs=Ntk(ch), start=True, stop=True),
                                  reads=[Nk, Ntbuf], writes=[p2])
                        Ntn = Ntp[hi][(k + 1) % 2]
                        fw.op("act", lambda e: e.activation(out=Ntn[:], in_=p2[:].rearrange("p (c v) -> p c v", c=4), func=AF.Copy), reads=[p2], writes=[Ntn])
                    if k < 5:
                        p3 = nbB()
                        for ch in range(4):
                            fw.op("pe", lambda e: e.matmul(p3[:, ch * T:(ch + 1) * T], lhsT=Ntk(ch), rhs=Nk[:, ch, :], start=True, stop=True),
                                  reads=[Nk, Ntbuf], writes=[p3])
                        Nn = Npp[hi][(k + 1) % 2]
                        fw.op("dve", lambda e: e.tensor_copy(out=Nn[:], in_=p3[:].rearrange("p (c v) -> p c v", c=4)), reads=[p3], writes=[Nn])
                yield
            for hi, (h, r0) in enumerate(hs):
                Xf = Xp[hi][7 % 2]
                p = nbB()
                for ch in range(4):
                    fw.op("pe", lambda e: e.matmul(p[r0:r0 + 64, ch * T:(ch + 1) * T], lhsT=Xf[:, ch, 0:64], rhs=NM1[hi][:, ch, 1, :], start=True, stop=False),
                          reads=[Xf, NM1[hi]], writes=[p])
                    fw.op("pe", lambda e: e.matmul(p[r0:r0 + 64, ch * T:(ch + 1) * T], lhsT=identb[:, r0:r0 + 64], rhs=AR[:, c, ch, 1, :],
                                                   start=False, stop=True), reads=[identb, ARc[c]], writes=[p])
                fw.op("act", lambda e: e.activation(out=GT[hi][r0:r0 + 64, :, :], in_=p[r0:r0 + 64, :].rearrange("p (c v) -> p c v", c=4), func=AF.Copy),
                      reads=[p], writes=[GT[hi]])
                p = nbB()
                for ch in range(4):
                    fw.op("pe", lambda e: e.matmul(p[r0:r0 + 64, ch * 64:(ch + 1) * 64], lhsT=Xf[:, ch, 0:64], rhs=Bp[:, ch, c * 128 + r0:c * 128 + r0 + 64],
                                                   start=True, stop=True), reads=[Xf, Bpc[c]], writes=[p])
                for ch in range(4):
                    fw.op("dve", lambda e: e.scalar_tensor_tensor(out=PTm[hi][r0:r0 + 64, ch, :], in0=identf[r0:r0 + 64, r0:r0 + 64], scalar=Gend[r0:r0 + 64, c, ch:ch + 1],
                                                                  in1=p[r0:r0 + 64, ch * 64:(ch + 1) * 64], op0=ALU.mult, op1=ALU.add),
                          reads=[identf, Gec[c], p], writes=[PTm[hi]])
                yield
            for ch in range(4):
                yield
                assert ti == 0 or (ti - 1) in r3_done, "epilogue of the previous tile must be emitted before Y of this tile is written"
                for hi, (h, r0) in enumerate(hs):
                    Xf = Xp[hi][7 % 2]
                    zi = (ti * 4 + ch) % 2
                    Zc = Zs[zi]; Zn = Zs[1 - zi]
                    psY = psYs[hi]; psZ = psZs[hi]
                    ycol = slice(hi * 64, hi * 64 + 64)
                    fw.op("pe", lambda e: e.matmul(psY[:, ch, ycol], lhsT=NM1[hi][:, ch, 1, :], rhs=Xf[:, ch, 64:128], start=True, stop=False),
                          reads=[NM1[hi], Xf], writes=[psY])
                    fw.op("pe", lambda e: e.matmul(psY[:, ch, ycol], lhsT=LM[hi][:, ch, 1, :], rhs=Vtok[:, ch, h * 64:(h + 1) * 64], start=False, stop=False),
                          reads=[LM[hi], Vtok], writes=[psY])
                    fw.op("pe", lambda e: e.matmul(psY[:, ch, ycol], lhsT=GT[hi][:, ch, :], rhs=Zc[:, c, :], start=False, stop=True),
                          reads=[GT[hi], Zc], writes=[psY])
                    fw.op("act", lambda e: e.activation(out=Ysb[:, ch, h * 64:(h + 1) * 64], in_=psY[:, ch, ycol], func=AF.Copy), reads=[psY], writes=[Ysbs[ch]])
                    fw.op("pe", lambda e: e.matmul(psZ[r0:r0 + 64, 0:64], lhsT=Bp[:, ch, c * 128 + r0:c * 128 + r0 + 64], rhs=Xf[:, ch, 64:128], start=True, stop=False),
                          reads=[Bpc[c], Xf], writes=[psZ])
                    fw.op("pe", lambda e: e.matmul(psZ[r0:r0 + 64, 0:64], lhsT=Kp[:, ch, c * 128 + r0:c * 128 + r0 + 64], rhs=Vtok[:, ch, h * 64:(h + 1) * 64], start=False, stop=False),
                          reads=[Kpc[c], Vtok], writes=[psZ])
                    fw.op("pe", lambda e: e.matmul(psZ[r0:r0 + 64, 0:64], lhsT=PTm[hi][:, ch, :], rhs=Zc[:, c, :], start=False, stop=True),
                          reads=[PTm[hi], Zc], writes=[psZ])
                    fw.op("dve", lambda e: e.tensor_copy(out=Zn[r0:r0 + 64, c, :], in_=psZ[r0:r0 + 64, 0:64]), reads=[psZ], writes=[Zn])

        r3_done = set()

        def R3(ti):
            t0 = ti * 512
            Vtok = Vtoks[ti % 2]; gtok = gtoks[ti % 2]; rks = rkss[ti % 2]
            S4 = range(4)
            Y = lambda sub: Ysb[:, sub, :]
            Y3 = lambda sub: Ysb[:, sub, :].rearrange("p (h v) -> p h v", h=8)
            bc = lambda ap: ap.unsqueeze(2).to_broadcast([128, 8, 64])
            yield
            for sub in S4:
                fw.op("dve", lambda e: e.tensor_reduce(out=st8[sub][:, 0, :], in_=Y3(sub), axis=AX.X, op=ALU.add), reads=[Ysbs[sub]], writes=[st8[sub]])
            yield
            for sub in S4:
                fw.op("act", lambda e: e.activation(out=ysqs[sub][:], in_=Y(sub), func=AF.Square), reads=[Ysbs[sub]], writes=[ysqs[sub]])
            yield
            for sub in S4:
                fw.op("dve", lambda e: e.tensor_reduce(out=st8[sub][:, 1, :], in_=ysqs[sub][:].rearrange("p (h v) -> p h v", h=8), axis=AX.X, op=ALU.add),
                      reads=[ysqs[sub]], writes=[st8[sub]])
            yield
            for sub in S4:
                s8 = st8[sub]
                fw.op("dve", lambda e: e.tensor_scalar(out=s8[:, 2, :], in0=s8[:, 0, :], scalar1=1.0 / 64, scalar2=None, op0=ALU.mult), reads=[s8], writes=[s8])
            yield
            for sub in S4:
                s8 = st8[sub]
                fw.op("dve", lambda e: e.tensor_tensor(out=s8[:, 3, :], in0=s8[:, 2, :], in1=s8[:, 2, :], op=ALU.mult), reads=[s8], writes=[s8])
            yield
            for sub in S4:
                s8 = st8[sub]
                fw.op("dve", lambda e: e.scalar_tensor_tensor(out=s8[:, 4, :], in0=s8[:, 1, :], scalar=1.0 / 64, in1=s8[:, 3, :], op0=ALU.mult, op1=ALU.subtract),
                      reads=[s8], writes=[s8])
            yield
            for sub in S4:
                s8 = st8[sub]
                fw.op("act", lambda e: e.activation(out=s8[:, 5, :], in_=s8[:, 4, :], func=AF.Sqrt, bias=LNX_EPS, scale=1.0), reads=[s8], writes=[s8])
            yield
            for sub in S4:
                s8 = st8[sub]
                fw.op("dve", lambda e: e.reciprocal(out=s8[:, 6, :], in_=s8[:, 5, :]), reads=[s8], writes=[s8])
            yield
            for sub in S4:
                fw.op("dve", lambda e: e.tensor_tensor(out=Y3(sub), in0=Y3(sub), in1=bc(st8[sub][:, 2, :]), op=ALU.subtract),
                      reads=[Ysbs[sub], st8[sub]], writes=[Ysbs[sub]])
            yield
            for sub in S4:
                fw.op("pool", lambda e: e.tensor_tensor(out=bvs[sub][:].rearrange("p (h v) -> p h v", h=8), in0=Vtok[:, sub, :].rearrange("p (h v) -> p h v", h=8),
                                                        in1=bc(rks[:, sub, :]), op=ALU.mult), reads=[Vtok, rks], writes=[bvs[sub]])
            yield
            for sub in S4:
                fw.op("dve", lambda e: e.tensor_tensor(out=Y3(sub), in0=Y3(sub), in1=bc(st8[sub][:, 6, :]), op=ALU.mult),
                      reads=[Ysbs[sub], st8[sub]], writes=[Ysbs[sub]])
            yield
            for sub in S4:
                fw.op("dve", lambda e: e.tensor_tensor(out=Y(sub), in0=Y(sub), in1=lwB[:], op=ALU.mult), reads=[Ysbs[sub], lwB], writes=[Ysbs[sub]])
            yield
            for sub in S4:
                fw.op("dve", lambda e: e.tensor_tensor(out=Y(sub), in0=Y(sub), in1=lbB[:], op=ALU.add), reads=[Ysbs[sub], lbB], writes=[Ysbs[sub]])
            yield
            for sub in S4:
                fw.op("dve", lambda e: e.tensor_tensor(out=Y(sub), in0=Y(sub), in1=bvs[sub][:], op=ALU.add), reads=[Ysbs[sub], bvs[sub]], writes=[Ysbs[sub]])
            yield
            for sub in S4:
                fw.op("dve", lambda e: e.tensor_tensor(out=zbs[sub][:], in0=Y(sub), in1=gtok[:, sub, :], op=ALU.mult), reads=[Ysbs[sub], gtok], writes=[zbs[sub]])
            yield
            zs = zstg[ti % 2]
            yield
            for sub in S4:
                for pr_ in range(4):
                    fw.op("pe", lambda e: e.transpose(ptb[:, pr_ * 128:(pr_ + 1) * 128], zbs[sub][:, pr_ * 128:(pr_ + 1) * 128], identb[:]),
                          reads=[zbs[sub], identb], writes=[ptb])
                fw.op("act", lambda e: e.activation(out=zs[:, :, sub * 128:(sub + 1) * 128], in_=ptb[:, 0:512].rearrange("p (r t) -> p r t", r=4), func=AF.Copy),
                      reads=[ptb], writes=[zs])
            fw.dma("sp", [(dst, zs[:, :, c_a:c_b]) for (dst, c_a, c_b) in zout_fn(t0)], reads=[zs])
            r3_done.add(ti)

        def run(*gens, rate=1):
            gens = [[g, (1 if i == 0 else rate)] for i, g in enumerate(gens) if g is not None]
            while gens:
                for ent in list(gens):
                    for _ in range(ent[1]):
                        try:
                            next(ent[0])
                        except StopIteration:
                            gens.remove(ent)
                            break

        def chain(*gens):
            for g in gens:
                if g is not None:
                    yield from g

        run(P(0))
        for c in range(4):
            run(R1a(0, c))
            R1b(0, c)
        RKS(0)
        for ti in range(n_tiles):
            nxt = ti + 1 < n_tiles
            R2a(ti, 0)
            run(R2b(ti, 0), chain(R3(ti - 1) if ti > 0 else None, R1a(ti, 3) if ti > 0 else None), rate=3)
            if ti > 0:
                R1b(ti, 3)
                RKS(ti)
            R2a(ti, 1)
            run(R2b(ti, 1), P(ti + 1) if nxt else None)
            R2a(ti, 2)
            run(R2b(ti, 2), chain(R1a(ti + 1, 0), R1a(ti + 1, 1)) if nxt else None, rate=2)
            if nxt:
                R1b(ti + 1, 0)
                R1b(ti + 1, 1)
            R2a(ti, 3)
            run(R2b(ti, 3), R1a(ti + 1, 2) if nxt else None)
            if nxt:
                R1b(ti + 1, 2)
        run(R3(n_tiles - 1))
        fw.barrier()
        fw.es = old_es
import ml_dtypes
_bf = ml_dtypes.bfloat16
NCORES = 8
SEQ = 8192
HALF = 4096
PADC = 128
CW = PADC + HALF
CS = CW + 64
RL = 2 * CS
PAIRS = [[0, 1], [2, 3], [4, 5], [6, 7]]


def _din(nc, name, shape, dt=F32):
    return Buf(nc.dram_tensor(name, list(shape), dt, kind="ExternalInput").ap(), name)


def _dout(nc, name, shape, dt=F32):
    return Buf(nc.dram_tensor(name, list(shape), dt, kind="ExternalOutput").ap(), name)


def _dscr(nc, name, shape, dt):
    return Buf(nc.dram_tensor(name, list(shape), dt).ap(), name)


def _lay_wgu(w):
    return np.ascontiguousarray(w.reshape(8, 128, 11, 2, 128).transpose(2, 1, 3, 0, 4))


def build_fused():
    nc = bass.Bass("TRN2", target_bir_lowering=False)
    x_full = _din(nc, "x_full", [SEQ, D]); x_half = _din(nc, "x_half", [128 + HALF, D])
    gmix0 = _din(nc, "gmix0", [D]); wqkv = _din(nc, "wqkv", [D, 1536]); cos = _din(nc, "cos", [SEQ, 64]); sinm = _din(nc, "sinm", [SEQ, 64])
    ffn_in = []
    for l in range(2):
        ffn_in.append(dict(wo=_din(nc, "wo%d" % l, [D, D]), gffn=_din(nc, "gffn%d" % l, [D]), wg=_din(nc, "wg%d" % l, [11, 128, 2, 8, 128]),
                           wu=_din(nc, "wu%d" % l, [11, 128, 2, 8, 128]), wd=_din(nc, "wd%d" % l, [DFF, D]), cw=_din(nc, "cw%d" % l, [128, NFC, 3]),
                           cb=_din(nc, "cb%d" % l, [128, NFC]), gnext=_din(nc, "gnext%d" % l, [D])))
    mu = _din(nc, "mu", [128, 6, 8]); wr = _din(nc, "wr", [D, 512]); wk = _din(nc, "wk", [D, 512]); wv = _din(nc, "wv", [D, 512])
    w1 = _din(nc, "w1", [D, 64]); a1 = _din(nc, "a1", [D, 64]); g1 = _din(nc, "g1", [D, 128]); w2 = _din(nc, "w2", [64, 512]); a2 = _din(nc, "a2", [64, 512]); g2 = _din(nc, "g2", [128, 512])
    pf = _din(nc, "pf", [128, 4, 5]); lnxw = _din(nc, "lnxw", [512]); lnxb = _din(nc, "lnxb", [512])
    out = _dout(nc, "out", [HALF, D], F32)
    QT = _dscr(nc, "QT", [512, SEQ], BF16); KT = _dscr(nc, "KT", [512, SEQ], BF16)
    attnS = _dscr(nc, "attnS", [8, 64, RL], BF16); attnG = _dscr(nc, "attnG", [8, 128, RL], BF16)
    xnS = _dscr(nc, "xnS", [8, 128, HALF], BF16); xnG = _dscr(nc, "xnG", [8, 256, HALF], BF16)
    h2 = _dscr(nc, "h2", [HALF, D], F32); tailS = _dscr(nc, "tailS", [128, D], F32); tailG = _dscr(nc, "tailG", [256, D], F32)
    tail3 = _dscr(nc, "tail3", [256, D], F32)
    mixL = _dscr(nc, "mixL", [D, CW], BF16)
    zS = _dscr(nc, "zS", [8, 64, RL], BF16); zG = _dscr(nc, "zG", [8, 128, RL], BF16)
    with ExitStack() as es:
        fw = FW(nc, es)
        ident, identf = make_ident(fw)
        pid = nc.partition_id(engines=[mybir.EngineType.SP])
        r = pid % 2
        with ExitStack() as ez:
            fw.es = ez
            zt = fw.sbuf([128, 1024], F32, "zeros_f")
            ztb = fw.sbuf([128, 4, 128], BF16, "zeros_b")
            fw.op("pool", lambda e: e.memset(zt[:], 0.0), writes=[zt])
            fw.op("pool", lambda e: e.memset(ztb[:], 0.0), writes=[ztb])
            fw.dma("sp", [(attnS.t.rearrange("(pr jl) q t -> (jl q) pr t", jl=2)[:, :, 0:PADC], ztb[:]),
                          (zS.t.rearrange("(pr jl) q t -> (jl q) pr t", jl=2)[:, :, 0:PADC], ztb[:]),
                          (tail3.t[0:128, :], zt[:])], reads=[zt, ztb])
            for S_ in (attnS, zS):
                Sv = S_.t.rearrange("(pr jl) q t -> (jl q) pr t", jl=2)
                fw.dma("sp", [(Sv[:, :, CW:CS], ztb[:, :, 0:CS - CW]), (Sv[:, :, CS + CW:RL], ztb[:, :, 0:RL - CS - CW])], reads=[ztb])
            fw.barrier()
            fw.es = es
        def half_cols(t0):
            pc = PADC + t0
            if t0 + 512 <= HALF:
                res = [(pc, 0, 512)]
                if t0 + 512 == HALF:
                    res.append((CS, 512 - PADC, 512))
                return res
            return [(CS + pc - HALF, 0, 512)]

        def localize(G, L):
            for rk in range(2):
                fw.dma("sp", L.t[rk * 512:(rk + 1) * 512, :].rearrange("(j q) t -> j q t", q=64),
                       G.t[:, rk * 64:(rk + 1) * 64, bass.ds(r * CS, CW)])

        def local_mix(L):
            Lv = L.t.rearrange("(kc p) t -> p kc t", p=128)
            return lambda c0, n: [(0, 128, 0, 8, Lv[:, :, c0:c0 + n])]

        phase_a(fw, ident, identf, x_full, gmix0, wqkv, cos, sinm, QT, KT,
                lambda h, qt: [(attnS.t[h, :, c:c + (b_ - a_)], a_, b_) for (c, a_, b_) in half_cols(qt * 512)])
        for j in range(8):
            fw.collective("AllGather", attnS.t[j], attnG.t[j], PAIRS)
        fw.barrier()
        localize(attnG, mixL)
        fw.barrier()
        f = ffn_in[0]
        phase_f(fw, ident, lambda s: x_half.t[s * 128:(s + 1) * 128, :], local_mix(mixL),
                f["wo"], f["gffn"], f["wg"], f["wu"], f["wd"], f["cw"], f["cb"], f["gnext"], h2, tailS, None, xnS, HALF // 512, False)
        for j in range(8):
            fw.collective("AllGather", xnS.t[j], xnG.t[j], PAIRS)
        fw.collective("AllGather", tailS.t[:, :], tailG.t[:, :], PAIRS)
        fw.barrier()
        fw.dma("sp", tail3.t[128:256, :], tailG.t[0:128, :])
        fw.barrier()
        xg_v = xnG.t.rearrange("kc (hf p) t -> hf p kc t", hf=2)

        def x_fn(t0):
            hf, tl = t0 // HALF, t0 % HALF
            if t0 == 0:
                return [(1, 513, xg_v[0][:, :, 0:512])]
            if tl == 0:
                return [(0, 1, xg_v[hf - 1][:, :, HALF - 1:HALF]), (1, 513, xg_v[hf][:, :, 0:512])]
            return [(0, 513, xg_v[hf][:, :, tl - 1:tl + 512])]

        zS_v = zS.t.rearrange("(pr jl) q t -> (jl q) pr t", jl=2)
        phase_r(fw, ident, identf, x_fn, mu, wr, wk, wv, w1, a1, g1, w2, a2, g2, pf, lnxw, lnxb,
                lambda t0: [(zS_v[:, :, c:c + (b_ - a_)], a_, b_) for (c, a_, b_) in half_cols(t0)], SEQ // 512)
        for j in range(8):
            fw.collective("AllGather", zS.t[j], zG.t[j], PAIRS)
        fw.barrier()
        localize(zG, mixL)
        fw.barrier()
        f = ffn_in[1]
        phase_f(fw, ident, lambda s: (tail3.t[bass.ds(r * 128, 128), :] if s == 0 else h2.t[(s - 1) * 128:s * 128, :]), local_mix(mixL),
                f["wo"], f["gffn"], f["wg"], f["wu"], f["wd"], f["cw"], f["cb"], f["gnext"], None, None, out, None, HALF // 512, True)
        fw.finish([])
    return nc


def _rope_tables():
    inv = (1.0 / (10000.0 ** (np.arange(0, 64, 2, dtype=np.float32) / np.float32(64)))).astype(np.float32)
    ang = np.arange(SEQ, dtype=np.float32)[:, None] * inv[None, :]
    ang = np.concatenate([ang, ang], -1)
    cos = np.cos(ang).astype(np.float32); sin = np.sin(ang).astype(np.float32)
    sinm = np.concatenate([-sin[:, :32], sin[:, 32:]], -1).astype(np.float32)
    return np.ascontiguousarray(cos), np.ascontiguousarray(sinm)


def kernel(x, norm_mix, norm_ffn, norm_final, attn_w_qkv, attn_w_o,
           rwkv_mu, rwkv_w_rkv, rwkv_w0, rwkv_w1, rwkv_w2, rwkv_a0, rwkv_a1,
           rwkv_a2, rwkv_g1, rwkv_g2, rwkv_k_k, rwkv_k_a, rwkv_r_k,
           rwkv_lnx_w, rwkv_lnx_b, rwkv_w_o,
           ffn_w_gate, ffn_w_up, ffn_conv_w, ffn_conv_b, ffn_w_down):
    f32 = lambda a: np.ascontiguousarray(np.asarray(a, dtype=np.float32))
    x = f32(x)
    cores = list(range(NCORES))
    cos, sinm = _rope_tables()
    wqkv = f32(attn_w_qkv)[0]
    wrkv = f32(rwkv_w_rkv)[0]
    mu_l = np.ascontiguousarray(f32(rwkv_mu)[0].reshape(6, 8, 128).transpose(2, 0, 1))
    shared = {"gmix0": f32(norm_mix)[0], "cos": cos, "sinm": sinm, "mu": mu_l,
              "w1": f32(rwkv_w1)[0], "a1": f32(rwkv_a1)[0], "g1": f32(rwkv_g1)[0]}
    wos = [f32(attn_w_o)[0], f32(rwkv_w_o)[0]]
    gnexts = [f32(norm_mix)[1], f32(norm_final)]
    for l in range(2):
        cw = f32(ffn_conv_w)[l]; cb = f32(ffn_conv_b)[l]
        shared.update({"wo%d" % l: wos[l], "gffn%d" % l: f32(norm_ffn)[l], "wg%d" % l: _lay_wgu(f32(ffn_w_gate)[l]), "wu%d" % l: _lay_wgu(f32(ffn_w_up)[l]),
                       "wd%d" % l: f32(ffn_w_down)[l], "cw%d" % l: np.ascontiguousarray(cw.T.reshape(NFC, 128, 3).transpose(1, 0, 2)),
                       "cb%d" % l: np.ascontiguousarray(cb.reshape(NFC, 128).T), "gnext%d" % l: gnexts[l]})
    maps = []
    for c in cores:
        b, r = c // 2, c % 2
        own = slice(512 * r, 512 * r + 512)
        wc = np.ascontiguousarray(np.concatenate([wqkv[:, 0:1024][:, own], wqkv[:, 1024:2048][:, own], wqkv[:, 2048:3072][:, own]], 1))
        xh = np.zeros((128 + HALF, D), np.float32)
        if r == 0:
            xh[128:] = x[b][0:HALF]
        else:
            xh[:] = x[b][HALF - 128:SEQ]
        pfl = np.stack([f32(rwkv_w0)[0][own], f32(rwkv_a0)[0][own], f32(rwkv_k_k)[0][own], f32(rwkv_k_a)[0][own], f32(rwkv_r_k)[0].reshape(-1)[own]], -1)
        m = dict(shared)
        m.update({"x_full": x[b], "x_half": xh, "wqkv": wc,
                  "wr": np.ascontiguousarray(wrkv[0][:, own]), "wk": np.ascontiguousarray(wrkv[1][:, own]), "wv": np.ascontiguousarray(wrkv[2][:, own]),
                  "w2": np.ascontiguousarray(f32(rwkv_w2)[0][:, own]), "a2": np.ascontiguousarray(f32(rwkv_a2)[0][:, own]),
                  "g2": np.ascontiguousarray(f32(rwkv_g2)[0][:, own]),
                  "pf": np.ascontiguousarray(pfl.reshape(4, 128, 5).transpose(1, 0, 2)),
                  "lnxw": np.ascontiguousarray(f32(rwkv_lnx_w)[0][own]), "lnxb": np.ascontiguousarray(f32(rwkv_lnx_b)[0][own])})
        maps.append(m)
    res = run_bass_kernel_spmd(build_fused(), maps, core_ids=cores)
    out = np.stack([np.concatenate([np.asarray(res.results[2 * b]["out"]), np.asarray(res.results[2 * b + 1]["out"])], 0) for b in range(4)], 0)
    return out.astype(np.float32)
```

```python
import math
import numpy as np
from contextlib import ExitStack
import concourse.bass as bass
import concourse.mybir as mybir
from concourse.bass_utils import run_bass_kernel_spmd

F32 = mybir.dt.float32
BF16 = mybir.dt.bfloat16
AF = mybir.ActivationFunctionType
ALU = mybir.AluOpType
AX = mybir.AxisListType


class Buf:
    def __init__(self, t, name):
        self.t = t
        self.name = name
        self.w = None
        self.r = {}

    def __getitem__(self, k):
        return self.t[k]


class FW:
    ENG = ("pe", "act", "dve", "pool", "sp")

    def __init__(self, nc, es, n_dma_sems=24):
        self.nc = nc
        self.es = es
        self.es0 = es
        self.eng = {"pe": nc.tensor, "act": nc.scalar, "dve": nc.vector,
                    "pool": nc.gpsimd, "sp": nc.sync}
        self.sem = {}
        self.cnt = {}
        for e in self.ENG:
            self.sem[e] = es.enter_context(nc.semaphore("s_" + e))
            self.cnt[e] = 0
        self.dsem = []
        for i in range(n_dma_sems):
            k = "d%d" % i
            self.sem[k] = es.enter_context(nc.semaphore("s_" + k))
            self.cnt[k] = 0
            self.dsem.append(k)
        self.dnext = {"hw": 0, "sw": 0}
        nsw = n_dma_sems // 3
        self.dpool = {"sw": self.dsem[:nsw], "hw": self.dsem[nsw:]}
        self.seen = {e: {} for e in self.ENG}
        self.nbuf = 0
        self.ninst = 0

    def sbuf(self, shape, dtype, name=None):
        self.nbuf += 1
        name = "%s_%d" % (name or "b", self.nbuf)
        t = self.es.enter_context(self.nc.sbuf_tensor(name, list(shape), dtype))
        return Buf(t, name)

    def psum(self, shape, dtype=F32, name=None):
        self.nbuf += 1
        name = "%s_%d" % (name or "p", self.nbuf)
        t = self.es.enter_context(self.nc.psum_tensor(name, list(shape), dtype))
        return Buf(t, name)

    def dram(self, name, shape, dtype, kind="Internal"):
        t = self.nc.dram_tensor(name, list(shape), dtype, kind=kind)
        return Buf(t.ap(), name)

    def _wait(self, e, ev):
        if ev is None:
            return
        k, v = ev
        if self.seen[e].get(k, 0) >= v:
            return
        if k == e and e == "pe":
            return
        self.eng[e].wait_ge(self.sem[k], v)
        self.seen[e][k] = v
        self.ninst += 1

    def _deps(self, e, reads, writes):
        for b in reads:
            self._wait(e, b.w)
        for b in writes:
            self._wait(e, b.w)
            for k, v in b.r.items():
                self._wait(e, (k, v))

    def _mark(self, ev, reads, writes):
        k, v = ev
        for b in reads:
            b.r[k] = v
        for b in writes:
            b.w = ev
            b.r = {}

    def op(self, e, fn, reads=(), writes=()):
        self._deps(e, reads, writes)
        inst = fn(self.eng[e])
        self.cnt[e] += 1
        inst.then_inc(self.sem[e], 1)
        self.ninst += 1
        self._mark((e, self.cnt[e]), reads, writes)
        return inst

    def dma(self, q, out, in_=None, reads=(), writes=(), **kw):
        pairs = out if in_ is None else [(out, in_)]
        self._deps(q, reads, writes)
        kind = "sw" if q == "pool" else "hw"
        pool_ = self.dpool[kind]
        k = pool_[self.dnext[kind]]
        self.dnext[kind] = (self.dnext[kind] + 1) % len(pool_)
        if self.cnt[k] > 0:
            self._wait(q, (k, self.cnt[k]))
        for (o, i) in pairs:
            inst = self.eng[q].dma_start(out=o, in_=i, **kw)
            self.cnt[k] += 16
            inst.then_inc(self.sem[k], 16)
            self.ninst += 1
        self._mark((k, self.cnt[k]), reads, writes)
        return inst

    def finish(self, bufs, e="sp"):
        for b in bufs:
            self._wait(e, b.w)
        for k in self.dsem:
            if self.cnt[k] > 0:
                self._wait(e, (k, self.cnt[k]))


def _barrier(self):
    for e in self.ENG:
        for k in list(self.ENG) + self.dsem + (["cc"] if "cc" in self.sem else []):
            if k == e:
                continue
            if self.cnt[k] > 0:
                self._wait(e, (k, self.cnt[k]))


FW.barrier = _barrier


def _collective(self, kind, in_ap, out_ap, groups, reads=(), writes=()):
    q = "pool"
    self._deps(q, reads, writes)
    if "cc" not in self.sem:
        self.sem["cc"] = self.es0.enter_context(self.nc.semaphore("s_cc"))
        self.cnt["cc"] = 0
    inst = self.nc.gpsimd.collective_compute(kind, ALU.bypass, replica_groups=groups, ins=[in_ap], outs=[out_ap])
    self.cnt["cc"] += 1
    inst.then_inc(self.sem["cc"], 1)
    self.ninst += 1
    self._mark(("cc", self.cnt["cc"]), reads, writes)
    return inst


FW.collective = _collective

D = 1024
DFF = 2816
NFC = 22
RMS_EPS = 1e-6


def make_ident(fw, dtype=BF16):
    identf = fw.sbuf([128, 128], F32, "identf")
    fw.op("pool", lambda e: e.memset(identf[:], 0.0), writes=[identf])
    fw.op("pool", lambda e: e.affine_select(out=identf[:], in_=identf[:], pattern=[[-1, 128]],
                                            compare_op=ALU.not_equal, fill=1.0, base=0,
                                            channel_multiplier=1), reads=[identf], writes=[identf])
    ident = fw.sbuf([128, 128], BF16, "ident")
    fw.op("dve", lambda e: e.tensor_copy(out=ident[:], in_=identf[:]), reads=[identf], writes=[ident])
    return ident, identf


def phase_f(fw, ident, hin_fn, mix_fn, wo, gffn, wg, wu, wd, cw, cb, gnext, hout, tail, nout, xnS, n_tiles, final):
    nc = fw.nc
    with ExitStack() as es:
        old_es = fw.es
        fw.es = es
        wo_sb = fw.sbuf([128, 8, 1024], BF16, "wo_sb")
        wd_sb = fw.sbuf([128, NFC, 1024], BF16, "wd_sb")
        gB = fw.sbuf([128, 1024], F32, "gB")
        gnB = fw.sbuf([128, 1024], F32, "gnB")
        cw_sb = fw.sbuf([128, NFC, 3], F32, "cw_sb")
        cb_sb = fw.sbuf([128, NFC], F32, "cb_sb")
        wgs = [fw.sbuf([128, 2, 8, 128], BF16, "wgs%d" % i) for i in range(2)]
        wus = [fw.sbuf([128, 2, 8, 128], BF16, "wus%d" % i) for i in range(2)]
        hx = [fw.sbuf([128, 1024], F32, "hx%d" % i) for i in range(2)]
        h1 = [fw.sbuf([128, 1024], F32, "h1_%d" % i) for i in range(4)]
        xn = [fw.sbuf([128, 1024], BF16, "xn%d" % i) for i in range(2)]
        xnT = fw.sbuf([128, 8, 512], BF16, "xnT")
        mx = fw.sbuf([128, 8, 512], BF16, "mx")
        aT = [fw.sbuf([128, 512], BF16, "aT%d" % i) for i in range(NFC)]
        G = [fw.sbuf([128, 514], F32, "G%d" % i) for i in range(2)]
        t1 = [fw.sbuf([128, 512], F32, "t1_%d" % i) for i in range(2)]
        sl = [fw.sbuf([128, 512], F32, "sl%d" % i) for i in range(2)]
        H = fw.sbuf([128, NFC, 2], F32, "H")
        nout_dtype = F32 if final else BF16
        no = [fw.sbuf([128, 1024], nout_dtype, "no%d" % i) for i in range(2)]
        nstg = [fw.sbuf([128, 8, 512], BF16, "nstg%d" % i) for i in range(2)] if not final else None
        stat = [fw.sbuf([128, 4], F32, "stat%d" % i) for i in range(2)]
        stat2 = [fw.sbuf([128, 4], F32, "stat2_%d" % i) for i in range(2)]
        sq = fw.sbuf([128, 1024], F32, "sqjunk")
        pj = [fw.psum([128, 512], F32, "pj%d" % i) for i in range(2)]
        tp = fw.psum([128, 1024], BF16, "tp")
        pg = [fw.psum([128, 512], F32, "pg%d" % i) for i in range(2)]
        pu = [fw.psum([128, 512], F32, "pu%d" % i) for i in range(2)]

        wo_v = wo.t.rearrange("(kc p) n -> p kc n", p=128)
        fw.dma("pool", [(wo_sb[:, kc:kc + 2, :], wo_v[:, kc:kc + 2, :]) for kc in range(0, 8, 2)], writes=[wo_sb])
        wd_v = wd.t.rearrange("(fc p) n -> p fc n", p=128)
        fw.dma("pool", [(wd_sb[:, fc:fc + 2, :], wd_v[:, fc:fc + 2, :]) for fc in range(0, NFC, 2)], writes=[wd_sb])
        fw.dma("sp", gB[:], gffn.t.partition_broadcast(128), writes=[gB])
        fw.dma("sp", gnB[:], gnext.t.partition_broadcast(128), writes=[gnB])
        fw.dma("sp", cw_sb[:], cw.t[:, :, :], writes=[cw_sb])
        fw.dma("sp", cb_sb[:], cb.t[:, :], writes=[cb_sb])

        nsub_total = 1 + 4 * n_tiles
        wcount = [0]

        def front(s_glob, slot):
            hxb = hx[s_glob % 2]
            h1b = h1[slot]
            xnb = xn[s_glob % 2]
            stb = stat[s_glob % 2]
            fw.dma("sp", hxb[:], hin_fn(s_glob), writes=[hxb])
            for half in range(2):
                for kc in range(8):
                    fw.op("pe", lambda e: e.matmul(pj[half][:], lhsT=mx[:, kc, slot * 128:(slot + 1) * 128],
                                                   rhs=wo_sb[:, kc, half * 512:(half + 1) * 512],
                                                   start=(kc == 0), stop=(kc == 7)),
                          reads=[mx, wo_sb], writes=[pj[half]])
            for half in range(2):
                fw.op("dve", lambda e: e.tensor_tensor(out=h1b[:, half * 512:(half + 1) * 512],
                                                       in0=hxb[:, half * 512:(half + 1) * 512],
                                                       in1=pj[half][:], op=ALU.add),
                      reads=[hxb, pj[half]], writes=[h1b])
            rms(h1b, stb)
            fw.op("dve", lambda e: e.scalar_tensor_tensor(out=xnb[:], in0=h1b[:], scalar=stb[:, 2:3], in1=gB[:],
                                                          op0=ALU.mult, op1=ALU.mult),
                  reads=[h1b, stb, gB], writes=[xnb])

        def front_b(s_glob, slot):
            xnb = xn[s_glob % 2]
            for kc in range(8):
                fw.op("pe", lambda e: e.transpose(tp[:, kc * 128:(kc + 1) * 128], xnb[:, kc * 128:(kc + 1) * 128], ident[:]),
                      reads=[xnb, ident], writes=[tp])
            fw.op("act", lambda e: e.activation(out=xnT[:, :, slot * 128:(slot + 1) * 128],
                                                in_=tp[:].rearrange("p (k t) -> p k t", k=8), func=AF.Copy),
                  reads=[tp], writes=[xnT])

        def rms(hb, stb):
            fw.op("dve", lambda e: e.memset(stb[:, 0:1], 0.0), writes=[stb])
            fw.op("act", lambda e: e.activation(out=sq[:], in_=hb[:], func=AF.Square, accum_out=stb[:, 0:1]),
                  reads=[hb], writes=[sq, stb])
            fw.op("act", lambda e: e.activation(out=stb[:, 1:2], in_=stb[:, 0:1], func=AF.Sqrt, scale=1.0 / D, bias=RMS_EPS),
                  reads=[stb], writes=[stb])
            fw.op("dve", lambda e: e.reciprocal(out=stb[:, 2:3], in_=stb[:, 1:2]), reads=[stb], writes=[stb])

        def load_w(j):
            sl_ = wcount[0] % 2
            wcount[0] += 1
            fw.dma("pool", wgs[sl_][:], wg.t[j], writes=[wgs[sl_]])
            fw.dma("pool", wus[sl_][:], wu.t[j], writes=[wus[sl_]])
            return sl_

        fw.dma("sp", [(mx[p0:p1, k0:k1, 0:128], src) for (p0, p1, k0, k1, src) in mix_fn(0, 128)], writes=[mx])
        front(0, 0)
        front_b(0, 0)
        for j in range(NFC // 2):
            ws = load_w(j)
            for jj in range(2):
                fc = 2 * j + jj
                pgb = pg[fc % 2]
                for kc in range(8):
                    fw.op("pe", lambda e: e.matmul(pgb[:, 0:2], lhsT=wgs[ws][:, jj, kc, :], rhs=xnT[:, kc, 126:128],
                                                   start=(kc == 0), stop=(kc == 7)),
                          reads=[wgs[ws], xnT], writes=[pgb])
                fw.op("act", lambda e: e.activation(out=H[:, fc, :], in_=pgb[:, 0:2], func=AF.Copy),
                      reads=[pgb], writes=[H])

        for ti in range(n_tiles):
            c0 = 128 + ti * 512
            fw.dma("sp", [(mx[p0:p1, k0:k1, :], src) for (p0, p1, k0, k1, src) in mix_fn(c0, 512)], writes=[mx])
            for s in range(4):
                front(1 + ti * 4 + s, s)
                if s > 0:
                    front_b(1 + ti * 4 + s - 1, s - 1)
            front_b(1 + ti * 4 + 3, 3)
            for j in range(NFC // 2):
                ws = load_w(j)
                for jj in range(2):
                    fc = 2 * j + jj
                    pgb = pg[fc % 2]
                    pub = pu[fc % 2]
                    Gb = G[fc % 2]
                    t1b = t1[fc % 2]
                    slb = sl[fc % 2]
                    for kc in range(8):
                        fw.op("pe", lambda e: e.matmul(pgb[:], lhsT=wgs[ws][:, jj, kc, :], rhs=xnT[:, kc, :],
                                                       start=(kc == 0), stop=(kc == 7)),
                              reads=[wgs[ws], xnT], writes=[pgb])
                    for kc in range(8):
                        fw.op("pe", lambda e: e.matmul(pub[:], lhsT=wus[ws][:, jj, kc, :], rhs=xnT[:, kc, :],
                                                       start=(kc == 0), stop=(kc == 7)),
                              reads=[wus[ws], xnT], writes=[pub])
                    fw.op("act", lambda e: e.activation(out=Gb[:, 0:2], in_=H[:, fc, :], func=AF.Copy),
                          reads=[H], writes=[Gb])
                    fw.op("act", lambda e: e.activation(out=Gb[:, 2:514], in_=pgb[:], func=AF.Copy), reads=[pgb], writes=[Gb])
                    fw.op("act", lambda e: e.activation(out=H[:, fc, :], in_=Gb[:, 512:514], func=AF.Copy),
                          reads=[Gb], writes=[H])
                    fw.op("act", lambda e: e.activation(out=t1b[:], in_=Gb[:, 0:512], func=AF.Copy, scale=cw_sb[:, fc, 0:1]),
                          reads=[Gb, cw_sb], writes=[t1b])
                    fw.op("dve", lambda e: e.scalar_tensor_tensor(out=t1b[:], in0=Gb[:, 1:513], scalar=cw_sb[:, fc, 1:2],
                                                                  in1=t1b[:], op0=ALU.mult, op1=ALU.add),
                          reads=[Gb, cw_sb, t1b], writes=[t1b])
                    fw.op("dve", lambda e: e.scalar_tensor_tensor(out=t1b[:], in0=Gb[:, 2:514], scalar=cw_sb[:, fc, 2:3],
                                                                  in1=t1b[:], op0=ALU.mult, op1=ALU.add),
                          reads=[Gb, cw_sb, t1b], writes=[t1b])
                    fw.op("act", lambda e: e.activation(out=slb[:], in_=t1b[:], func=AF.Silu, bias=cb_sb[:, fc:fc + 1], scale=1.0),
                          reads=[t1b, cb_sb], writes=[slb])
                    fw.op("dve", lambda e: e.tensor_tensor(out=aT[fc][:], in0=slb[:], in1=pub[:], op=ALU.mult),
                          reads=[slb, pub], writes=[aT[fc]])
            for s in range(4):
                h1b = h1[s]
                r0 = ti * 512 + s * 128
                sidx = ti * 4 + s
                for half in range(2):
                    for fc in range(NFC):
                        fw.op("pe", lambda e: e.matmul(pj[half][:], lhsT=aT[fc][:, s * 128:(s + 1) * 128],
                                                       rhs=wd_sb[:, fc, half * 512:(half + 1) * 512],
                                                       start=(fc == 0), stop=(fc == NFC - 1)),
                              reads=[aT[fc], wd_sb], writes=[pj[half]])
                for half in range(2):
                    fw.op("dve", lambda e: e.tensor_tensor(out=h1b[:, half * 512:(half + 1) * 512],
                                                           in0=h1b[:, half * 512:(half + 1) * 512],
                                                           in1=pj[half][:], op=ALU.add),
                          reads=[h1b, pj[half]], writes=[h1b])
                if hout is not None:
                    fw.dma("sp", hout.t[r0:r0 + 128, :], h1b[:], reads=[h1b])
                    if tail is not None and ti == n_tiles - 1 and s == 3:
                        fw.dma("sp", tail.t[:, :], h1b[:], reads=[h1b])
                stb = stat2[sidx % 2]
                nob = no[sidx % 2]
                rms(h1b, stb)
                fw.op("dve", lambda e: e.scalar_tensor_tensor(out=nob[:], in0=h1b[:], scalar=stb[:, 2:3], in1=gnB[:],
                                                              op0=ALU.mult, op1=ALU.mult),
                      reads=[h1b, stb, gnB], writes=[nob])
                if final:
                    fw.dma("sp", nout.t[r0:r0 + 128, :], nob[:], reads=[nob])
                else:
                    for kc in range(8):
                        fw.op("pe", lambda e: e.transpose(tp[:, kc * 128:(kc + 1) * 128], nob[:, kc * 128:(kc + 1) * 128], ident[:]),
                              reads=[nob, ident], writes=[tp])
                    ns = nstg[ti % 2]
                    fw.op("act", lambda e: e.activation(out=ns[:, :, s * 128:(s + 1) * 128],
                                                        in_=tp[:].rearrange("p (k t) -> p k t", k=8), func=AF.Copy),
                          reads=[tp], writes=[ns])
                    if s == 3:
                        fw.dma("sp", xnS.t.rearrange("kc p t -> p kc t")[:, :, ti * 512:(ti + 1) * 512], ns[:], reads=[ns])
        fw.barrier()
        fw.es = old_es

D = 1024
S = 8192
NH = 8
DH = 64
BLK = 256
NB = S // BLK
BIG = 30000.0
SCALE = 1.0 / math.sqrt(DH)
RMS_EPS = 1e-6


def phase_a(fw, ident, identf, x, gmix, wqkv, cos, sinm, QT, KT, out_fn, n_heads=NH, n_sub=S // 128):
    S_loc = n_sub * 128
    NBL = S_loc // BLK
    nc = fw.nc
    nqt = n_sub // 4
    with ExitStack() as es0:
        old_es = fw.es
        fw.es = es0
        V_sb = fw.sbuf([128, n_sub, NH, 65], BF16, "V_sb")
        ssq_q = fw.sbuf([128, n_sub, NH], F32, "ssq_q")
        kmax2 = fw.sbuf([128, NH], F32, "kmax2")
        fw.op("pool", lambda e: e.memset(V_sb[:], 1.0), writes=[V_sb])
        fw.op("pool", lambda e: e.memset(kmax2[:], 0.0), writes=[kmax2])
        with ExitStack() as es1:
            fw.es = es1
            w_sb = fw.sbuf([128, 8, 1536], BF16, "wqkv_sb")
            gB = fw.sbuf([128, 1024], F32, "gB")
            xb = [fw.sbuf([128, 1024], F32, "xb%d" % i) for i in range(2)]
            xn = [fw.sbuf([128, 1024], BF16, "xn%d" % i) for i in range(2)]
            xnT = [fw.sbuf([128, 8, 128], BF16, "xnT%d" % i) for i in range(2)]
            cs = [fw.sbuf([128, 2, 64], F32, "cs%d" % i) for i in range(2)]
            tcb = fw.sbuf([128, NH, 64], F32, "tcb")
            trb = fw.sbuf([128, NH, 64], F32, "trb")
            qk_tok = [fw.sbuf([128, 2, 512], BF16, "qktok%d" % i) for i in range(2)]
            stg = [fw.sbuf([128, 2, 4, 512], BF16, "stg%d" % i) for i in range(2)]
            sqj = fw.sbuf([128, 1024], F32, "sqj")
            sqq = fw.sbuf([128, 512], F32, "sqq")
            sqk = fw.sbuf([128, 512], F32, "sqk")
            ssk = fw.sbuf([128, NH], F32, "ssk")
            stat = [fw.sbuf([128, 4], F32, "stat%d" % i) for i in range(2)]
            tp = fw.psum([128, 1024], BF16, "tp")
            pqs = [fw.psum([128, 512], F32, "pq%d" % i) for i in range(2)]
            pks = [fw.psum([128, 512], F32, "pk%d" % i) for i in range(2)]
            pv = fw.psum([128, 512], F32, "pv")
            tqk = fw.psum([128, 2, 512], BF16, "tqk")

            w_v = wqkv.t.rearrange("(kc p) n -> p kc n", p=128)
            fw.dma("pool", [(w_sb[:, kc:kc + 2, :], w_v[:, kc:kc + 2, :]) for kc in range(0, 8, 2)], writes=[w_sb])
            fw.dma("sp", gB[:], gmix.t.partition_broadcast(128), writes=[gB])
            QT_v = QT.t.rearrange("(pr p) t -> p pr t", p=128)
            KT_v = KT.t.rearrange("(pr p) t -> p pr t", p=128)

            def stage1(st):
                b2 = st % 2
                r0 = st * 128
                fw.dma("sp", xb[b2][:], x.t[r0:r0 + 128, :], writes=[xb[b2]])
                fw.dma("sp", [(cs[b2][:, 0, :], cos.t[r0:r0 + 128, :]), (cs[b2][:, 1, :], sinm.t[r0:r0 + 128, :])], writes=[cs[b2]])
                stb = stat[b2]
                fw.op("dve", lambda e: e.memset(stb[:, 0:1], 0.0), writes=[stb])
                fw.op("act", lambda e: e.activation(out=sqj[:], in_=xb[b2][:], func=AF.Square, accum_out=stb[:, 0:1]),
                      reads=[xb[b2]], writes=[sqj, stb])
                fw.op("act", lambda e: e.activation(out=stb[:, 1:2], in_=stb[:, 0:1], func=AF.Sqrt, scale=1.0 / D, bias=RMS_EPS),
                      reads=[stb], writes=[stb])
                fw.op("dve", lambda e: e.reciprocal(out=stb[:, 2:3], in_=stb[:, 1:2]), reads=[stb], writes=[stb])
                fw.op("dve", lambda e: e.scalar_tensor_tensor(out=xn[b2][:], in0=xb[b2][:], scalar=stb[:, 2:3], in1=gB[:],
                                                              op0=ALU.mult, op1=ALU.mult),
                      reads=[xb[b2], stb, gB], writes=[xn[b2]])
                for kc in range(8):
                    fw.op("pe", lambda e: e.transpose(tp[:, kc * 128:(kc + 1) * 128], xn[b2][:, kc * 128:(kc + 1) * 128], ident[:]),
                          reads=[xn[b2], ident], writes=[tp])
                fw.op("act", lambda e: e.activation(out=xnT[b2][:], in_=tp[:].rearrange("p (k t) -> p k t", k=8), func=AF.Copy),
                      reads=[tp], writes=[xnT[b2]])
            def stage2(st):
                b2 = st % 2
                pq = pqs[st % 2]
                pk = pks[st % 2]
                for (pp, c0) in ((pq, 0), (pk, 512), (pv, 1024)):
                    for kc in range(8):
                        fw.op("pe", lambda e: e.matmul(pp[:], lhsT=xnT[b2][:, kc, :], rhs=w_sb[:, kc, c0:c0 + 512],
                                                       start=(kc == 0), stop=(kc == 7)),
                              reads=[xnT[b2], w_sb], writes=[pp])
                fw.op("act", lambda e: e.activation(out=V_sb[:, st, :, 0:64], in_=pv[:].rearrange("p (h d) -> p h d", h=NH), func=AF.Copy),
                      reads=[pv], writes=[V_sb])
                cosB = cs[b2][:, 0, :].unsqueeze(1).to_broadcast([128, NH, 64])
                sinB = cs[b2][:, 1, :].unsqueeze(1).to_broadcast([128, NH, 64])
                qkb = qk_tok[b2]
                for qi, pp in enumerate((pq, pk)):
                    pv3 = pp[:].rearrange("p (h d) -> p h d", h=NH)
                    sqx = sqq if qi == 0 else sqk
                    fw.op("act", lambda e: e.activation(out=sqx[:], in_=pp[:], func=AF.Square), reads=[pp], writes=[sqx])
                    if qi == 0:
                        fw.op("dve", lambda e: e.tensor_reduce(out=ssq_q[:, st, :], in_=sqx[:].rearrange("p (h d) -> p h d", h=NH),
                                                               axis=AX.X, op=ALU.add), reads=[sqx], writes=[ssq_q])
                    else:
                        fw.op("dve", lambda e: e.tensor_reduce(out=ssk[:], in_=sqx[:].rearrange("p (h d) -> p h d", h=NH),
                                                               axis=AX.X, op=ALU.add), reads=[sqx], writes=[ssk])
                        fw.op("dve", lambda e: e.tensor_tensor(out=kmax2[:], in0=kmax2[:], in1=ssk[:], op=ALU.max),
                              reads=[ssk, kmax2], writes=[kmax2])
                    fw.op("dve", lambda e: e.tensor_tensor(out=tcb[:], in0=pv3, in1=cosB, op=ALU.mult),
                          reads=[pp, cs[b2]], writes=[tcb])
                    fw.op("dve", lambda e: e.tensor_tensor(out=trb[:, :, 0:32], in0=pv3[:, :, 32:64], in1=sinB[:, :, 0:32], op=ALU.mult),
                          reads=[pp, cs[b2]], writes=[trb])
                    fw.op("dve", lambda e: e.tensor_tensor(out=trb[:, :, 32:64], in0=pv3[:, :, 0:32], in1=sinB[:, :, 32:64], op=ALU.mult),
                          reads=[pp, cs[b2]], writes=[trb])
                    fw.op("pool", lambda e: e.tensor_tensor(out=qkb[:, qi, :].rearrange("p (h d) -> p h d", h=NH), in0=tcb[:], in1=trb[:], op=ALU.add),
                          reads=[tcb, trb], writes=[qkb])
            def stage3(st):
                b2 = st % 2
                qkb = qk_tok[b2]
                for qi in range(2):
                    for pr in range(4):
                        fw.op("pe", lambda e: e.transpose(tqk[:, qi, pr * 128:(pr + 1) * 128], qkb[:, qi, pr * 128:(pr + 1) * 128], ident[:]),
                              reads=[qkb, ident], writes=[tqk])
                sg = stg[(st // 4) % 2]
                slot = st % 4
                fw.op("act", lambda e: e.activation(out=sg[:, :, :, slot * 128:(slot + 1) * 128],
                                                    in_=tqk[:].rearrange("p a (r t) -> p a r t", r=4), func=AF.Copy),
                      reads=[tqk], writes=[sg])
                if slot == 3:
                    t0 = (st // 4) * 512
                    fw.dma("sp", [(QT_v[:, :, t0:t0 + 512], sg[:, 0, :, :]), (KT_v[:, :, t0:t0 + 512], sg[:, 1, :, :])], reads=[sg])
            stage1(0)
            for st in range(n_sub):
                if st + 1 < n_sub:
                    stage1(st + 1)
                stage2(st)
                if st >= 1:
                    stage3(st - 1)
            stage3(n_sub - 1)
            fw.barrier()
        with ExitStack() as es2:
            fw.es = es2
            QA = [fw.sbuf([96, S_loc], BF16, "QA%d" % i) for i in range(2)]
            KA = [fw.sbuf([96, S_loc], BF16, "KA%d" % i) for i in range(2)]
            cm = [fw.sbuf([128, 512], BF16, "cm%d" % i) for i in range(4)]
            C2 = fw.sbuf([128, 64], F32, "C2")
            Dc = fw.sbuf([128, 64], F32, "Dc")
            Ec = fw.sbuf([128, 64], F32, "Ec")
            onesf = fw.sbuf([128, 128], F32, "onesf")
            kmf = fw.sbuf([96, NB], F32, "kmf")
            kmT = fw.sbuf([96, NB], BF16, "kmT")
            kmx = fw.sbuf([128, NH], F32, "kmx")
            kmxT = fw.sbuf([NH, 128], F32, "kmxT")
            kmr = fw.sbuf([NH, 2], F32, "kmr")
            kdiag = fw.sbuf([NH, NH], F32, "kdiag")
            stabm = fw.sbuf([128, n_sub, NH], F32, "stabm")
            gm = [fw.sbuf([128, NB], F32, "gm%d" % i) for i in range(2)]
            top8 = [fw.sbuf([128, 8], F32, "top8_%d" % i) for i in range(2)]
            s1 = [fw.sbuf([128, NB], F32, "s1_%d" % i) for i in range(2)]
            nmk = [fw.sbuf([128, 96], BF16, "nmk%d" % i) for i in range(2)]
            PT = [fw.sbuf([128, 512], BF16, "PT%d" % i) for i in range(3)]
            rd = [fw.sbuf([128, 512], F32, "rd%d" % i) for i in range(2)]
            bcs = fw.sbuf([64, 512], F32, "bcs")
            at = [fw.sbuf([64, 512], BF16, "at%d" % i) for i in range(2)]
            ps_s = [fw.psum([128, 512], F32, "ps_s%d" % i) for i in range(3)]
            ps_o = [fw.psum([128, 512], F32, "ps_o%d" % i) for i in range(2)]
            ps_b = fw.psum([128, 512], F32, "ps_b")
            ps_g = fw.psum([128, 512], F32, "ps_g")
            ps_t = fw.psum([128, 1024], BF16, "ps_t")

            for m in range(4):
                fw.op("pool", lambda e: e.memset(cm[m][:], 1.0), writes=[cm[m]])
                fw.op("pool", lambda e: e.affine_select(out=cm[m][:], in_=cm[m][:], pattern=[[1, 512]], compare_op=ALU.is_ge,
                                                        fill=0.0, base=-128 * m, channel_multiplier=-1),
                      reads=[cm[m]], writes=[cm[m]])
            fw.op("dve", lambda e: e.memset(C2[:, 0:32], 0.0), writes=[C2])
            fw.op("dve", lambda e: e.memset(C2[:, 32:64], -BIG), writes=[C2])
            fw.op("dve", lambda e: e.memset(Dc[:, 0:32], -2 * BIG), writes=[Dc])
            fw.op("dve", lambda e: e.memset(Dc[:, 32:64], 0.0), writes=[Dc])
            fw.op("dve", lambda e: e.memset(Ec[:, 0:33], 0.0), writes=[Ec])
            fw.op("dve", lambda e: e.memset(Ec[:, 33:64], -BIG), writes=[Ec])
            fw.op("dve", lambda e: e.memset(onesf[:], 1.0), writes=[onesf])
            fw.op("dve", lambda e: e.memset(kmT[:], 0.0), writes=[kmT])
            for i in range(2):
                fw.op("dve", lambda e: e.memset(nmk[i][:], 0.0), writes=[nmk[i]])
            for i in range(2):
                fw.op("pool", lambda e: e.memset(QA[i][64:96, :], 0.0), writes=[QA[i]])
                fw.op("pool", lambda e: e.memset(KA[i][64:96, :], 1.0), writes=[KA[i]])
                fw.op("pool", lambda e: e.affine_select(out=KA[i][64:96, :], in_=KA[i][64:96, :], pattern=[[1, S_loc]], compare_op=ALU.is_ge,
                                                        fill=0.0, base=0, channel_multiplier=-BLK), reads=[KA[i]], writes=[KA[i]])
                fw.op("pool", lambda e: e.affine_select(out=KA[i][64:96, :], in_=KA[i][64:96, :], pattern=[[-1, S_loc]], compare_op=ALU.is_ge,
                                                        fill=0.0, base=BLK - 1, channel_multiplier=BLK), reads=[KA[i]], writes=[KA[i]])
            fw.op("pe", lambda e: e.transpose(ps_g[0:NH, 0:128], kmax2[:], identf[:]), reads=[kmax2, identf], writes=[ps_g])
            fw.op("dve", lambda e: e.tensor_reduce(out=kmr[:, 0:1], in_=ps_g[0:NH, 0:128], axis=AX.X, op=ALU.max), reads=[ps_g], writes=[kmr])
            fw.op("dve", lambda e: e.tensor_scalar(out=kdiag[:], in0=identf[0:NH, 0:NH], scalar1=kmr[:, 0:1], scalar2=None, op0=ALU.mult),
                  reads=[identf, kmr], writes=[kdiag])
            fw.op("pe", lambda e: e.matmul(ps_b[:, 0:NH], lhsT=onesf[0:NH, :], rhs=kdiag[:], start=True, stop=True),
                  reads=[kdiag, onesf], writes=[ps_b])
            fw.op("dve", lambda e: e.tensor_copy(out=kmx[:], in_=ps_b[:, 0:NH]), reads=[ps_b], writes=[kmx])
            fw.op("dve", lambda e: e.tensor_tensor(out=stabm[:], in0=ssq_q[:], in1=kmx[:].unsqueeze(1).to_broadcast([128, n_sub, NH]), op=ALU.mult),
                  reads=[ssq_q, kmx], writes=[stabm])
            fw.op("act", lambda e: e.activation(out=stabm[:], in_=stabm[:], func=AF.Sqrt), reads=[stabm], writes=[stabm])
            fw.op("dve", lambda e: e.tensor_scalar(out=stabm[:], in0=stabm[:], scalar1=-1.0, scalar2=None, op0=ALU.mult),
                  reads=[stabm], writes=[stabm])

            def load_head(h):
                b = h % 2
                fw.dma("sp", QA[b][0:64, :], QT.t[h * 64:(h + 1) * 64, :], writes=[QA[b]])
                fw.dma("sp", KA[b][0:64, :], KT.t[h * 64:(h + 1) * 64, :], writes=[KA[b]])

            def gating(h):
                b = h % 2
                Qa, Ka = QA[b], KA[b]
                fw.op("dve", lambda e: e.tensor_reduce(out=kmf[0:64, 0:NBL], in_=Ka[0:64, :].rearrange("p (j t) -> p j t", t=BLK),
                                                       axis=AX.X, op=ALU.add), reads=[Ka], writes=[kmf])
                fw.op("act", lambda e: e.activation(out=kmT[0:64, 0:NBL], in_=kmf[0:64, 0:NBL], func=AF.Copy, scale=1.0 / BLK),
                      reads=[kmf], writes=[kmT])
                for qs in range(n_sub):
                    qb = qs // 2
                    g16 = qs % 16
                    if g16 == 0:
                        pass
                    fw.op("pe", lambda e: e.matmul(ps_g[:, g16 * NB:(g16 + 1) * NB], lhsT=Qa[:, qs * 128:(qs + 1) * 128], rhs=kmT[:],
                                                   start=True, stop=True), reads=[Qa, kmT], writes=[ps_g])
                    if g16 == 15 or qs == n_sub - 1:
                        for q2 in range(qs - g16, qs + 1):
                            qb2 = q2 // 2
                            gg = q2 % 16
                            i2 = q2 % 2
                            lo = 32 - qb2
                            fw.op("dve", lambda e: e.tensor_tensor(out=gm[i2][:], in0=ps_g[:, gg * NB:(gg + 1) * NB], in1=C2[:, lo:lo + NB], op=ALU.add),
                                  reads=[ps_g, C2], writes=[gm[i2]])
                            fw.op("dve", lambda e: e.max(out=top8[i2][:], in_=gm[i2][:]), reads=[gm[i2]], writes=[top8[i2]])
                            fw.op("dve", lambda e: e.tensor_scalar(out=s1[i2][:], in0=gm[i2][:], scalar1=top8[i2][:, 2:3], scalar2=BIG,
                                                                   op0=ALU.is_ge, op1=ALU.mult),
                                  reads=[gm[i2], top8[i2]], writes=[s1[i2]])
                            fw.op("dve", lambda e: e.scalar_tensor_tensor(out=s1[i2][:], in0=s1[i2][:], scalar=-BIG, in1=Dc[:, lo:lo + NB],
                                                                          op0=ALU.add, op1=ALU.max),
                                  reads=[s1[i2], Dc], writes=[s1[i2]])
                            fw.op("dve", lambda e: e.scalar_tensor_tensor(out=nmk[i2][:, 64:96], in0=s1[i2][:], scalar=stabm[:, q2, h:h + 1], in1=Ec[:, lo:lo + NB],
                                                                          op0=ALU.add, op1=ALU.add),
                                  reads=[s1[i2], stabm, Ec], writes=[nmk[i2]])
                            t8 = q2 % 8
                            fw.op("pe", lambda e: e.transpose(ps_t[0:96, t8 * 128:(t8 + 1) * 128], nmk[i2][:], ident[:]),
                                  reads=[nmk[i2], ident], writes=[ps_t])
                            if t8 == 7 or q2 == n_sub - 1:
                                c0 = (q2 - t8) * 128
                                fw.op("act", lambda e: e.activation(out=Qa[64:96, c0:c0 + (t8 + 1) * 128], in_=ps_t[64:96, 0:(t8 + 1) * 128], func=AF.Copy),
                                      reads=[ps_t], writes=[Qa])
                            yield

            load_head(0)
            for _ in gating(0):
                pass
            ev = 0
            for h in range(n_heads):
                b = h % 2
                Qa, Ka = QA[b], KA[b]
                if h + 1 < n_heads:
                    load_head(h + 1)
                iters = [(qt, kt) for qt in range(nqt) for kt in range(4 * (qt + 1))]
                LA = 2
                pend = []

                def emit_S(i):
                    qt, kt = iters[i]
                    pss = ps_s[i % 3]
                    ptb = PT[i % 3]
                    fw.op("pe", lambda e: e.matmul(pss[:], lhsT=Ka[:, kt * 128:(kt + 1) * 128], rhs=Qa[:, qt * 512:(qt + 1) * 512],
                                                   start=True, stop=True), reads=[Ka, Qa], writes=[pss])
                    fw.op("act", lambda e: e.activation(out=ptb[:], in_=pss[:], func=AF.Exp, scale=SCALE), reads=[pss], writes=[ptb])
                    m = kt - 4 * qt
                    if m >= 0:
                        fw.op("dve", lambda e: e.tensor_tensor(out=ptb[:], in0=ptb[:], in1=cm[m][:], op=ALU.mult),
                              reads=[ptb, cm[m]], writes=[ptb])

                def emit_PV(i):
                    qt, kt = iters[i]
                    nkt = 4 * (qt + 1)
                    po = ps_o[qt % 2]
                    ptb = PT[i % 3]
                    fw.op("pe", lambda e: e.matmul(po[0:65, :], lhsT=V_sb[:, kt, h, :], rhs=ptb[:], start=(kt == 0), stop=(kt == nkt - 1)),
                          reads=[V_sb, ptb], writes=[po])
                    if kt == nkt - 1:
                        rdb = rd[qt % 2]
                        fw.op("dve", lambda e: e.reciprocal(out=rdb[64:65, :], in_=po[64:65, :]), reads=[po], writes=[rdb])

                        def tail(qt=qt, po=po, rdb=rdb):
                            fw.op("pe", lambda e: e.matmul(ps_b[0:64, :], lhsT=onesf[64:65, 0:64], rhs=rdb[64:65, :], start=True, stop=True),
                                  reads=[rdb, onesf], writes=[ps_b])
                            fw.op("dve", lambda e: e.tensor_copy(out=bcs[:], in_=ps_b[0:64, :]), reads=[ps_b], writes=[bcs])
                            ab = at[qt % 2]
                            fw.op("dve", lambda e: e.tensor_tensor(out=ab[:], in0=po[0:64, :], in1=bcs[:], op=ALU.mult), reads=[po, bcs], writes=[ab])
                            fw.dma("sp", [(dst, ab[:, c_a:c_b]) for (dst, c_a, c_b) in out_fn(h, qt)], reads=[ab])
                        pend.append([3, tail])

                gnext = gating(h + 1) if h + 1 < n_heads else None
                for i in range(len(iters) + LA):
                    if gnext is not None and i % 6 == 5:
                        try:
                            next(gnext)
                        except StopIteration:
                            gnext = None
                    if i < len(iters):
                        emit_S(i)
                    if i - LA >= 0:
                        emit_PV(i - LA)
                    for pe_ in list(pend):
                        pe_[0] -= 1
                        if pe_[0] <= 0:
                            pend.remove(pe_)
                            pe_[1]()
                for pe_ in pend:
                    pe_[1]()
                if gnext is not None:
                    for _ in gnext:
                        pass
            fw.barrier()
        fw.es = old_es

D = 1024
T = 128
C0 = math.exp(-0.5)
LNX_EPS = 64e-5


def phase_r(fw, identb, identf, x_fn, mu, wr, wk, wv, w1, a1, g1, w2, a2, g2, pf, lnxw, lnxb, zout_fn, n_tiles, stop=99):
    nc = fw.nc
    with ExitStack() as es:
        old_es = fw.es
        fw.es = es
        sb = fw.sbuf
        wr_sb = sb([128, 8, 512], BF16, "wr_sb"); wk_sb = sb([128, 8, 512], BF16, "wk_sb"); wv_sb = sb([128, 8, 512], BF16, "wv_sb")
        w1_sb = sb([128, 8, 64], BF16, "w1_sb"); a1_sb = sb([128, 8, 64], BF16, "a1_sb"); g1_sb = sb([128, 8, 128], BF16, "g1_sb")
        w2_sb = sb([64, 512], BF16, "w2_sb"); a2_sb = sb([64, 512], BF16, "a2_sb"); g2_sb = sb([128, 512], BF16, "g2_sb")
        mu_sb = sb([128, 6, 8], F32, "mu_sb"); pf_sb = sb([128, 4, 5], F32, "pf_sb")
        lwB = sb([128, 512], F32, "lwB"); lbB = sb([128, 512], F32, "lbB")
        for (dst, src) in ((wr_sb, wr), (wk_sb, wk), (wv_sb, wv), (w1_sb, w1), (a1_sb, a1), (g1_sb, g1)):
            v = src.t.rearrange("(kc p) n -> p kc n", p=128)
            fw.dma("pool", [(dst[:, 0:4, :], v[:, 0:4, :]), (dst[:, 4:8, :], v[:, 4:8, :])], writes=[dst])
        fw.dma("pool", w2_sb[:], w2.t[:, :], writes=[w2_sb])
        fw.dma("pool", a2_sb[:], a2.t[:, :], writes=[a2_sb])
        fw.dma("pool", g2_sb[:], g2.t[:, :], writes=[g2_sb])
        fw.dma("sp", mu_sb[:], mu.t[:, :, :], writes=[mu_sb])
        fw.dma("sp", pf_sb[:], pf.t[:, :, :], writes=[pf_sb])
        fw.dma("sp", lwB[:], lnxw.t.partition_broadcast(128), writes=[lwB])
        fw.dma("sp", lbB[:], lnxb.t.partition_broadcast(128), writes=[lbB])
        rmask = sb([128, 512], F32, "rmask")
        fw.op("pool", lambda e: e.memset(rmask[:], 1.0), writes=[rmask])
        for ch in range(4):
            fw.op("pool", lambda e: e.memset(rmask[:, ch * T:ch * T + 1], 0.0), writes=[rmask])
        MK_SI = sb([128, 2, 2, 128], BF16, "MK_SI")
        MK_SL = sb([128, 4, 128], BF16, "MK_SL")
        fw.op("pool", lambda e: e.memset(MK_SI[:], 1.0), writes=[MK_SI])
        fw.op("pool", lambda e: e.memset(MK_SL[:], 1.0), writes=[MK_SL])
        for cb in range(2):
            fw.op("pool", lambda e: e.affine_select(out=MK_SI[:, cb, 0, :], in_=MK_SI[:, cb, 0, :], pattern=[[1, 128]], compare_op=ALU.is_gt,
                                                    fill=0.0, base=0, channel_multiplier=-1), reads=[MK_SI], writes=[MK_SI])
            fw.op("pool", lambda e: e.affine_select(out=MK_SI[:, cb, 1, :], in_=MK_SI[:, cb, 1, :], pattern=[[1, 128]], compare_op=ALU.is_ge,
                                                    fill=0.0, base=0, channel_multiplier=-1), reads=[MK_SI], writes=[MK_SI])
        for ch in range(4):
            fw.op("pool", lambda e: e.affine_select(out=MK_SL[:, ch, :], in_=MK_SL[:, ch, :], pattern=[[-1, 128]], compare_op=ALU.is_gt,
                                                    fill=0.0, base=0, channel_multiplier=1), reads=[MK_SL], writes=[MK_SL])
        Ind8 = sb([128, 4, 8], BF16, "Ind8")
        fw.op("pool", lambda e: e.memset(Ind8[:], 0.0), writes=[Ind8])
        for c in range(4):
            fw.op("pool", lambda e: e.memset(Ind8[0:64, c, 2 * c:2 * c + 1], 1.0), writes=[Ind8])
            fw.op("pool", lambda e: e.memset(Ind8[64:128, c, 2 * c + 1:2 * c + 2], 1.0), writes=[Ind8])
        BOnes = sb([128, 128], BF16, "BOnes")
        fw.op("pool", lambda e: e.memset(BOnes[:], 0.0), writes=[BOnes])
        fw.op("pool", lambda e: e.memset(BOnes[0:64, 0:64], 1.0), writes=[BOnes])
        fw.op("pool", lambda e: e.memset(BOnes[64:128, 64:128], 1.0), writes=[BOnes])
        xTb = [sb([128, 8, 514], BF16, "xTb0")] * 2
        xxa = sb([128, 8, 512], BF16, "xxa")
        mix = [sb([128, 8, 512], BF16, "mix%d" % i) for i in range(2)]
        thw = sb([64, 512], BF16, "thw"); tha = sb([64, 512], BF16, "tha"); thg = sb([128, 512], BF16, "thg")
        AR = sb([128, 4, 4, 2, 128], BF16, "AR")
        BT = sb([128, 4, 512], BF16, "BT"); KTt = sb([128, 4, 512], BF16, "KTt")
        bpT = [sb([128, 512], BF16, "bpT%d" % i) for i in range(2)]
        kpT = [sb([128, 512], BF16, "kpT%d" % i) for i in range(2)]
        Atok = sb([128, 4, 512], BF16, "Atok"); Bp = sb([128, 4, 512], BF16, "Bp"); Kp = sb([128, 4, 512], BF16, "Kp")
        Vtoks = [sb([128, 4, 512], BF16, "Vtok%d" % i) for i in range(2)]; gtoks = [sb([128, 4, 512], BF16, "gtok%d" % i) for i in range(2)]
        prodT = sb([128, 4, 512], BF16, "prodT")
        rkss = [sb([128, 4, 8], F32, "rks%d" % i) for i in range(2)]
        Gend = sb([128, 4, 4], F32, "Gend")
        Ysb = sb([128, 4, 512], F32, "Ysb")
        sg = sb([128, 512], F32, "sg"); al = sb([128, 512], F32, "al")
        Ec = sb([128, 512], F32, "Ec"); Em = sb([128, 512], F32, "Em"); Ee = sb([128, 512], F32, "Ee")
        e1 = Em; e2 = sb([128, 512], F32, "e2"); e3 = sb([128, 512], F32, "e3"); e4 = Ee
        rsb = sb([128, 512], F32, "rsb"); ksb = sb([128, 512], F32, "ksb"); kkf = sb([128, 512], F32, "kkf"); sqb = sb([128, 512], BF16, "sqb"); nrm = sb([128, 512], F32, "nrm")
        kkn = sb([128, 512], F32, "kkn"); tm1 = sb([128, 512], F32, "tm1"); kmod = sb([128, 512], F32, "kmod"); ka = sb([128, 512], F32, "ka")
        Zs = [sb([128, 4, 64], BF16, "Zs%d" % i) for i in range(2)]
        for i in range(2):
            fw.op("pool", lambda e: e.memset(Zs[i][:], 0.0), writes=[Zs[i]])
        NM1 = [sb([128, 4, 2, 128], BF16, "NM1_%d" % i) for i in range(2)]
        LM = [sb([128, 4, 2, 128], BF16, "LM_%d" % i) for i in range(2)]
        Npp = [[sb([128, 4, 128], BF16, "N_%d_%d" % (i, j)) for j in range(2)] for i in range(2)]
        Ntp = [[sb([128, 4, 128], BF16, "Nt_%d_%d" % (i, j)) for j in range(2)] for i in range(2)]
        Xp = [[sb([128, 4, 128], BF16, "X_%d_%d" % (i, j)) for j in range(2)] for i in range(2)]
        GT = [sb([128, 4, 128], BF16, "GT%d" % i) for i in range(2)]
        PTm = [sb([128, 4, 64], BF16, "PTm%d" % i) for i in range(2)]
        for i in range(2):
            fw.op("pool", lambda e: e.memset(GT[i][:], 0.0), writes=[GT[i]])
            fw.op("pool", lambda e: e.memset(PTm[i][:], 0.0), writes=[PTm[i]])
        ysqs = [sb([128, 512], BF16, "ysq%d" % i) for i in range(4)]; bvs = [sb([128, 512], BF16, "bv%d" % i) for i in range(4)]
        mtmpb = [ysqs[0], ysqs[1]]
        st8 = [sb([128, 8, 8], F32, "st8_%d" % i) for i in range(4)]
        zbs = [sb([128, 512], BF16, "zb%d" % i) for i in range(4)]
        zstg = [sb([128, 4, 512], BF16, "zstg0")] * 2
        pbA = [fw.psum([128, 512], F32, "pbA%d" % i) for i in range(2)]
        pbB = [fw.psum([128, 512], F32, "pbB%d" % i) for i in range(3)]
        psYt = fw.psum([128, 4, 128], F32, "psY")
        psZt = fw.psum([128, 512], F32, "psZ")
        psYs = [psYt, psYt]
        psZs = [psZt, psZt]
        ptb = fw.psum([128, 1024], BF16, "ptb")
        pbi = [0, 0]

        def nbA():
            b = pbA[pbi[0] % 2]
            pbi[0] += 1
            return b

        def nbB():
            b = pbB[pbi[1] % 3]
            pbi[1] += 1
            return b

        Ysbs = [Buf(Ysb.t, "Ysb_s%d" % s) for s in range(4)]
        ARc = [Buf(AR.t, "AR_c%d" % c) for c in range(4)]
        BTc = [Buf(BT.t, "BT_c%d" % c) for c in range(4)]
        KTc = [Buf(KTt.t, "KT_c%d" % c) for c in range(4)]
        Atc = [Buf(Atok.t, "At_c%d" % c) for c in range(4)]
        Bpc = [Buf(Bp.t, "Bp_c%d" % c) for c in range(4)]
        Kpc = [Buf(Kp.t, "Kp_c%d" % c) for c in range(4)]
        prc = [Buf(prodT.t, "pr_c%d" % c) for c in range(4)]
        Gec = [Buf(Gend.t, "Ge_c%d" % c) for c in range(4)]


        v4 = lambda ap: ap.rearrange("p (c t) -> p c t", t=T)
        st_mix = {}

        def P(ti):
            t0 = ti * 512
            xb = xTb[ti % 2]
            Vtok = Vtoks[ti % 2]; gtok = gtoks[ti % 2]
            if ti == 0:
                fw.op("pool", lambda e: e.memset(xb[:, :, 0:2], 0.0), writes=[xb])
            for (lo, hi, src) in x_fn(t0):
                if hi - lo == 1:
                    fw.dma("sp", xb[:, :, lo + 1:hi + 1], src, writes=[xb], allow_slow_non_contiguous=True)
                else:
                    fw.dma("sp", xb[:, :, lo + 1:hi + 1], src, writes=[xb])
            mixi = [0]

            xxb = [None]

            def make_mix(n):
                m = mix[mixi[0] % 2]
                mixi[0] += 1
                for kc in range(8):
                    tmpm = mtmpb[kc % 2]
                    fw.op("dve", lambda e: e.tensor_scalar(out=tmpm[:], in0=xxa[:, kc, :], scalar1=mu_sb[:, n, kc:kc + 1], scalar2=None, op0=ALU.mult),
                          reads=[xxa, mu_sb], writes=[tmpm])
                    fw.op("dve", lambda e: e.tensor_tensor(out=m[:, kc, :], in0=tmpm[:], in1=xb[:, kc, 2:514], op=ALU.add),
                          reads=[tmpm, xb], writes=[m])
                return m

            for kc in range(8):
                fw.op("dve", lambda e: e.tensor_tensor(out=xxa[:, kc, :], in0=xb[:, kc, 1:513], in1=xb[:, kc, 2:514], op=ALU.subtract),
                      reads=[xb], writes=[xxa])
            m = make_mix(1)
            p = nbA(); proj_fm(m, w1_sb, 0, 64, p)
            fw.op("act", lambda e: e.activation(out=thw[:], in_=p[0:64, :], func=AF.Tanh), reads=[p], writes=[thw])
            yield
            m = make_mix(4)
            p = nbA(); proj_fm(m, a1_sb, 0, 64, p)
            fw.op("act", lambda e: e.activation(out=tha[:], in_=p[0:64, :], func=AF.Copy), reads=[p], writes=[tha])
            yield
            m = make_mix(5)
            p = nbA(); proj_fm(m, g1_sb, 0, 128, p)
            fw.op("act", lambda e: e.activation(out=thg[:], in_=p[:], func=AF.Sigmoid), reads=[p], writes=[thg])
            yield
            for sub in range(4):
                p = nbA()
                fw.op("pe", lambda e: e.matmul(p[:], lhsT=thg[:, sub * 128:(sub + 1) * 128], rhs=g2_sb[:], start=True, stop=True),
                      reads=[thg, g2_sb], writes=[p])
                fw.op("act", lambda e: e.activation(out=gtok[:, sub, :], in_=p[:], func=AF.Copy), reads=[p], writes=[gtok])
            yield
            m = make_mix(3)
            for sub in range(4):
                p = nbA()
                for kc in range(8):
                    fw.op("pe", lambda e: e.matmul(p[:], lhsT=m[:, kc, sub * 128:(sub + 1) * 128], rhs=wv_sb[:, kc, :], start=(kc == 0), stop=(kc == 7)),
                          reads=[m, wv_sb], writes=[p])
                fw.op("act", lambda e: e.activation(out=Vtok[:, sub, :], in_=p[:], func=AF.Copy), reads=[p], writes=[Vtok])
            yield
            mr = make_mix(0)
            yield
            mk = make_mix(2)
            st_mix[ti] = (mr, mk)
            yield

        def proj_fm(m, w_sb, c0, ncol, p):
            for kc in range(8):
                fw.op("pe", lambda e: e.matmul(p[0:ncol, :], lhsT=w_sb[:, kc, c0:c0 + ncol], rhs=m[:, kc, :], start=(kc == 0), stop=(kc == 7)),
                      reads=[w_sb, m], writes=[p])

        def R1a(ti, c):
            mr, mk = st_mix[ti]
            pr = nbA(); proj_fm(mr, wr_sb, c * 128, 128, pr)
            fw.op("act", lambda e: e.activation(out=rsb[:], in_=pr[:], func=AF.Copy), reads=[pr], writes=[rsb])
            yield
            pk = nbA(); proj_fm(mk, wk_sb, c * 128, 128, pk)
            fw.op("act", lambda e: e.activation(out=ksb[:], in_=pk[:], func=AF.Copy), reads=[pk], writes=[ksb])
            yield
            pw = nbA()
            fw.op("pe", lambda e: e.matmul(pw[:], lhsT=w2_sb[:, c * 128:(c + 1) * 128], rhs=thw[:], start=True, stop=True),
                  reads=[w2_sb, thw], writes=[pw])
            fw.op("act", lambda e: e.activation(out=sg[:], in_=pw[:], func=AF.Sigmoid, bias=pf_sb[:, c, 0:1], scale=1.0),
                  reads=[pw, pf_sb], writes=[sg])
            yield
            pa = nbA()
            fw.op("pe", lambda e: e.matmul(pa[:], lhsT=a2_sb[:, c * 128:(c + 1) * 128], rhs=tha[:], start=True, stop=True),
                  reads=[a2_sb, tha], writes=[pa])
            fw.op("act", lambda e: e.activation(out=al[:], in_=pa[:], func=AF.Sigmoid, bias=pf_sb[:, c, 1:2], scale=1.0),
                  reads=[pa, pf_sb], writes=[al])
            yield
            fw.op("dve", lambda e: e.tensor_tensor_scan(out=Ec[:], data0=rmask[:], data1=sg[:], initial=0.0, op0=ALU.mult, op1=ALU.add),
                  reads=[rmask, sg], writes=[Ec])
            fw.op("dve", lambda e: e.tensor_tensor(out=Em[:], in0=Ec[:], in1=sg[:], op=ALU.subtract), reads=[Ec, sg], writes=[Em])
            Ec3 = Ec[:].rearrange("p (c t) -> p c t", t=T)
            fw.op("dve", lambda e: e.tensor_tensor(out=Ee[:].rearrange("p (c t) -> p c t", t=T), in0=Ec3[:, :, T - 1:T].to_broadcast([128, 4, T]),
                                                   in1=Ec3, op=ALU.subtract), reads=[Ec], writes=[Ee])
            yield
            fw.op("act", lambda e: e.activation(out=e1[:], in_=Em[:], func=AF.Exp, scale=-C0), reads=[Em], writes=[e1])
            fw.op("act", lambda e: e.activation(out=e2[:], in_=Ec[:], func=AF.Exp, scale=C0), reads=[Ec], writes=[e2])
            fw.op("act", lambda e: e.activation(out=e3[:], in_=Ec[:], func=AF.Exp, scale=-C0), reads=[Ec], writes=[e3])
            fw.op("act", lambda e: e.activation(out=e4[:], in_=Ee[:], func=AF.Exp, scale=-C0), reads=[Ee], writes=[e4])
            fw.op("act", lambda e: e.activation(out=Gend[:, c, :], in_=e3[:].rearrange("p (c t) -> p c t", t=T)[:, :, T - 1], func=AF.Copy),
                  reads=[e3], writes=[Gec[c]])
            yield
            fw.op("dve", lambda e: e.tensor_scalar(out=kkf[:], in0=ksb[:], scalar1=pf_sb[:, c, 2:3], scalar2=None, op0=ALU.mult),
                  reads=[ksb, pf_sb], writes=[kkf])
            fw.op("act", lambda e: e.activation(out=sqb[:], in_=kkf[:], func=AF.Square), reads=[kkf], writes=[sqb])
            yield
            pn = nbA()
            fw.op("pe", lambda e: e.matmul(pn[:], lhsT=BOnes[:], rhs=sqb[:], start=True, stop=True), reads=[BOnes, sqb], writes=[pn])
            fw.op("act", lambda e: e.activation(out=nrm[:], in_=pn[:], func=AF.Sqrt), reads=[pn], writes=[nrm])
            yield
            fw.op("dve", lambda e: e.tensor_scalar(out=nrm[:], in0=nrm[:], scalar1=1e-12, scalar2=None, op0=ALU.max), reads=[nrm], writes=[nrm])
            fw.op("dve", lambda e: e.reciprocal(out=nrm[:], in_=nrm[:]), reads=[nrm], writes=[nrm])
            fw.op("dve", lambda e: e.tensor_tensor(out=kkn[:], in0=kkf[:], in1=nrm[:], op=ALU.mult), reads=[kkf, nrm], writes=[kkn])
            yield
            fw.op("dve", lambda e: e.tensor_scalar(out=tm1[:], in0=al[:], scalar1=-1.0, scalar2=pf_sb[:, c, 3:4], op0=ALU.add, op1=ALU.mult),
                  reads=[al, pf_sb], writes=[tm1])
            fw.op("dve", lambda e: e.scalar_tensor_tensor(out=kmod[:], in0=tm1[:], scalar=1.0, in1=ksb[:], op0=ALU.add, op1=ALU.mult),
                  reads=[tm1, ksb], writes=[kmod])
            yield
            fw.op("dve", lambda e: e.scalar_tensor_tensor(out=AR[:, c, :, 0, :], in0=v4(kkn[:]), scalar=-1.0, in1=v4(e1[:]), op0=ALU.mult, op1=ALU.mult),
                  reads=[kkn, e1], writes=[ARc[c]])
            fw.op("pool", lambda e: e.tensor_tensor(out=AR[:, c, :, 1, :], in0=v4(rsb[:]), in1=v4(e3[:]), op=ALU.mult), reads=[rsb, e3], writes=[ARc[c]])
            fw.op("pool", lambda e: e.tensor_tensor(out=ka[:], in0=kkn[:], in1=al[:], op=ALU.mult), reads=[kkn, al], writes=[ka])
            fw.op("pool", lambda e: e.tensor_tensor(out=BT[:, c, :], in0=ka[:], in1=e2[:], op=ALU.mult), reads=[ka, e2], writes=[BTc[c]])
            fw.op("pool", lambda e: e.tensor_tensor(out=KTt[:, c, :], in0=kmod[:], in1=e2[:], op=ALU.mult), reads=[kmod, e2], writes=[KTc[c]])
            yield
            bpb = bpT[c % 2]; kpb = kpT[c % 2]
            fw.op("pool", lambda e: e.tensor_tensor(out=bpb[:], in0=ka[:], in1=e4[:], op=ALU.mult), reads=[ka, e4], writes=[bpb])
            fw.op("pool", lambda e: e.tensor_tensor(out=kpb[:], in0=kmod[:], in1=e4[:], op=ALU.mult), reads=[kmod, e4], writes=[kpb])
            fw.op("dve", lambda e: e.scalar_tensor_tensor(out=prodT[:, c, :], in0=rsb[:], scalar=pf_sb[:, c, 4:5], in1=kmod[:], op0=ALU.mult, op1=ALU.mult),
                  reads=[rsb, pf_sb, kmod], writes=[prc[c]])
            yield

        def R1b(ti, c):
            bpb = bpT[c % 2]; kpb = kpT[c % 2]
            for which, (src_fn, dst, dstb, srcbuf) in enumerate(((lambda sub: AR[:, c, sub, 0, :], Atok, Atc[c], ARc[c]),
                                                                 (lambda sub: bpb[:, sub * 128:(sub + 1) * 128], Bp, Bpc[c], bpb),
                                                                 (lambda sub: kpb[:, sub * 128:(sub + 1) * 128], Kp, Kpc[c], kpb))):
                for sub in range(4):
                    fw.op("pe", lambda e: e.transpose(ptb[:, sub * 128:(sub + 1) * 128], src_fn(sub), identb[:]), reads=[srcbuf, identb], writes=[ptb])
                if which != 1:
                    fw.op("act", lambda e: e.activation(out=dst[:, :, c * 128:(c + 1) * 128], in_=ptb[:, 0:512].rearrange("p (s f) -> p s f", s=4), func=AF.Copy),
                          reads=[ptb], writes=[dstb])
                else:
                    fw.op("dve", lambda e: e.tensor_copy(out=dst[:, :, c * 128:(c + 1) * 128], in_=ptb[:, 0:512].rearrange("p (s f) -> p s f", s=4)),
                          reads=[ptb], writes=[dstb])

        def RKS(ti):
            rks = rkss[ti % 2]
            for sub in range(4):
                p = nbA()
                for c in range(4):
                    fw.op("pe", lambda e: e.matmul(p[:, 0:8], lhsT=prodT[:, c, sub * 128:(sub + 1) * 128], rhs=Ind8[:, c, :], start=(c == 0), stop=(c == 3)),
                          reads=[prc[c], Ind8], writes=[p])
                fw.op("act", lambda e: e.activation(out=rks[:, sub, :], in_=p[:, 0:8], func=AF.Copy), reads=[p], writes=[rks])

        def R2a(ti, c):
            Vtok = Vtoks[ti % 2]
            hs = [(2 * c + hb, 64 * hb) for hb in range(2)]
            for hi, (h, r0) in enumerate(hs):
                for (lh, lhb, dstb) in ((BT, BTc[c], NM1[hi]), (KTt, KTc[c], LM[hi])):
                    for half in range(2):
                        p = nbB()
                        for cc in range(2):
                            ch = half * 2 + cc
                            fw.op("pe", lambda e: e.matmul(p[:, cc * 256:(cc + 1) * 256], lhsT=lh[r0:r0 + 64, c, ch * T:(ch + 1) * T],
                                                           rhs=AR[r0:r0 + 64, c, ch, :, :], start=True, stop=True),
                                  reads=[lhb, ARc[c]], writes=[p])
                        fw.op("dve", lambda e: e.tensor_tensor(out=dstb[:, half * 2:half * 2 + 2, :, :], in0=p[:].rearrange("p (a b i) -> p a b i", a=2, b=2),
                                                               in1=MK_SI[:], op=ALU.mult), reads=[p, MK_SI], writes=[dstb])
                p = nbB()
                for ch in range(4):
                    fw.op("pe", lambda e: e.matmul(p[:, ch * T:(ch + 1) * T], lhsT=AR[r0:r0 + 64, c, ch, 0, :], rhs=BT[r0:r0 + 64, c, ch * T:(ch + 1) * T],
                                                   start=True, stop=True), reads=[ARc[c], BTc[c]], writes=[p])
                fw.op("dve", lambda e: e.tensor_tensor(out=Npp[hi][0][:], in0=p[:].rearrange("p (c j) -> p c j", c=4), in1=MK_SL[:], op=ALU.mult),
                      reads=[p, MK_SL], writes=[Npp[hi][0]])
            for hi, (h, r0) in enumerate(hs):
                p = nbB()
                for ch in range(4):
                    fw.op("pe", lambda e: e.matmul(p[:, ch * 64:(ch + 1) * 64], lhsT=LM[hi][:, ch, 0, :], rhs=Vtok[:, ch, h * 64:(h + 1) * 64],
                                                   start=True, stop=True), reads=[LM[hi], Vtok], writes=[p])
                X0 = Xp[hi][0]
                fw.op("act", lambda e: e.activation(out=X0[:, :, 64:128], in_=p[:, 0:256].rearrange("p (c v) -> p c v", c=4), func=AF.Copy),
                      reads=[p], writes=[X0])
                fw.op("dve", lambda e: e.tensor_copy(out=X0[:, :, 0:64], in_=Atok[:, :, c * 128 + r0:c * 128 + r0 + 64]), reads=[Atc[c]], writes=[X0])

        def R2b(ti, c):
            Vtok = Vtoks[ti % 2]
            hs = [(2 * c + hb, 64 * hb) for hb in range(2)]
            for k in range(7):
                for hi, (h, r0) in enumerate(hs):
                    Xc = Xp[hi][k % 2]; Xn = Xp[hi][(k + 1) % 2]
                    Ntk = (lambda ch: NM1[hi][:, ch, 0, :]) if k == 0 else (lambda ch: Ntp[hi][k % 2][:, ch, :])
                    Ntbuf = NM1[hi] if k == 0 else Ntp[hi][k % 2]
                    Nk = Npp[hi][k % 2]
                    p = nbB()
                    for ch in range(4):
                        fw.op("pe", lambda e: e.matmul(p[:, ch * T:(ch + 1) * T], lhsT=identb[:], rhs=Xc[:, ch, :], start=True, stop=False),
                              reads=[identb, Xc], writes=[p])
                        fw.op("pe", lambda e: e.matmul(p[:, ch * T:(ch + 1) * T], lhsT=Ntk(ch), rhs=Xc[:, ch, :], start=False, stop=True),
                              reads=[Ntbuf, Xc], writes=[p])
                    fw.op("act", lambda e: e.activation(out=Xn[:], in_=p[:].rearrange("p (c v) -> p c v", c=4), func=AF.Copy), reads=[p], writes=[Xn])
                    if k < 6:
                        p2 = nbB()
                        for ch in range(4):
                            fw.op("pe", lambda e: e.matmul(p2[:, ch * T:(ch + 1) * T], lhsT=Nk[:, ch, :], rhs=Ntk(ch), start=True, stop=True),
                                  reads=[Nk, Ntbuf], writes=[p2])
                        Ntn = Ntp[hi][(k + 1) % 2]
                        fw.op("act", lambda e: e.activation(out=Ntn[:], in_=p2[:].rearrange("p (c v) -> p c v", c=4), func=AF.Copy), reads=[p2], writes=[Ntn])
                    if k < 5:
                        p3 = nbB()
                        for ch in range(4):
                            fw.op("pe", lambda e: e.matmul(p3[:, ch * T:(ch + 1) * T], lhsT=Ntk(ch), rhs=Nk[:, ch, :], start=True, stop=True),
                                  reads=[Nk, Ntbuf], writes=[p3])
                        Nn = Npp[hi][(k + 1) % 2]
                        fw.op("dve", lambda e: e.tensor_copy(out=Nn[:], in_=p3[:].rearrange("p (c v) -> p c v", c=4)), reads=[p3], writes=[Nn])
                yield
            for hi, (h, r0) in enumerate(hs):
                Xf = Xp[hi][7 % 2]
                p = nbB()
                for ch in range(4):
                    fw.op("pe", lambda e: e.matmul(p[r0:r0 + 64, ch * T:(ch + 1) * T], lhsT=Xf[:, ch, 0:64], rhs=NM1[hi][:, ch, 1, :], start=True, stop=False),
                          reads=[Xf, NM1[hi]], writes=[p])
                    fw.op("pe", lambda e: e.matmul(p[r0:r0 + 64, ch * T:(ch + 1) * T], lhsT=identb[:, r0:r0 + 64], rhs=AR[:, c, ch, 1, :],
                                                   start=False, stop=True), reads=[identb, ARc[c]], writes=[p])
                fw.op("act", lambda e: e.activation(out=GT[hi][r0:r0 + 64, :, :], in_=p[r0:r0 + 64, :].rearrange("p (c v) -> p c v", c=4), func=AF.Copy),
                      reads=[p], writes=[GT[hi]])
                p = nbB()
                for ch in range(4):
                    fw.op("pe", lambda e: e.matmul(p[r0:r0 + 64, ch * 64:(ch + 1) * 64], lhsT=Xf[:, ch, 0:64], rhs=Bp[:, ch, c * 128 + r0:c * 128 + r0 + 64],
                                                   start=True, stop=True), reads=[Xf, Bpc[c]], writes=[p])
                for ch in range(4):
                    fw.op("dve", lambda e: e.scalar_tensor_tensor(out=PTm[hi][r0:r0 + 64, ch, :], in0=identf[r0:r0 + 64, r0:r0 + 64], scalar=Gend[r0:r0 + 64, c, ch:ch + 1],
                                                                  in1=p[r0:r0 + 64, ch * 64:(ch + 1) * 64], op0=ALU.mult, op1=ALU.add),
                          reads=[identf, Gec[c], p], writes=[PTm[hi]])
                yield
            for ch in range(4):
                yield
                assert ti == 0 or (ti - 1) in r3_done, "epilogue of the previous tile must be emitted before Y of this tile is written"
                for hi, (h, r0) in enumerate(hs):
                    Xf = Xp[hi][7 % 2]
                    zi = (ti * 4 + ch) % 2
                    Zc = Zs[zi]; Zn = Zs[1 - zi]
                    psY = psYs[hi]; psZ = psZs[hi]
                    ycol = slice(hi * 64, hi * 64 + 64)
                    fw.op("pe", lambda e: e.matmul(psY[:, ch, ycol], lhsT=NM1[hi][:, ch, 1, :], rhs=Xf[:, ch, 64:128], start=True, stop=False),
                          reads=[NM1[hi], Xf], writes=[psY])
                    fw.op("pe", lambda e: e.matmul(psY[:, ch, ycol], lhsT=LM[hi][:, ch, 1, :], rhs=Vtok[:, ch, h * 64:(h + 1) * 64], start=False, stop=False),
                          reads=[LM[hi], Vtok], writes=[psY])
                    fw.op("pe", lambda e: e.matmul(psY[:, ch, ycol], lhsT=GT[hi][:, ch, :], rhs=Zc[:, c, :], start=False, stop=True),
                          reads=[GT[hi], Zc], writes=[psY])
                    fw.op("act", lambda e: e.activation(out=Ysb[:, ch, h * 64:(h + 1) * 64], in_=psY[:, ch, ycol], func=AF.Copy), reads=[psY], writes=[Ysbs[ch]])
                    fw.op("pe", lambda e: e.matmul(psZ[r0:r0 + 64, 0:64], lhsT=Bp[:, ch, c * 128 + r0:c * 128 + r0 + 64], rhs=Xf[:, ch, 64:128], start=True, stop=False),
                          reads=[Bpc[c], Xf], writes=[psZ])
                    fw.op("pe", lambda e: e.matmul(psZ[r0:r0 + 64, 0:64], lhsT=Kp[:, ch, c * 128 + r0:c * 128 + r0 + 64], rhs=Vtok[:, ch, h * 64:(h + 1) * 64], start=False, stop=False),
                          reads=[Kpc[c], Vtok], writes=[psZ])
                    fw.op("pe", lambda e: e.matmul(psZ[r0:r0 + 64, 0:64], lhsT=PTm[hi][:, ch, :], rhs=Zc[:, c, :], start=False, stop=True),
                          reads=[PTm[hi], Zc], writes=[psZ])
                    fw.op("dve", lambda e: e.tensor_copy(out=Zn[r0:r0 + 64, c, :], in_=psZ[r0:r0 + 64, 0:64]), reads=[psZ], writes=[Zn])

        r3_done = set()

        def R3(ti):
            t0 = ti * 512
            Vtok = Vtoks[ti % 2]; gtok = gtoks[ti % 2]; rks = rkss[ti % 2]
            S4 = range(4)
            Y = lambda sub: Ysb[:, sub, :]
            Y3 = lambda sub: Ysb[:, sub, :].rearrange("p (h v) -> p h v", h=8)
            bc = lambda ap: ap.unsqueeze(2).to_broadcast([128, 8, 64])
            yield
            for sub in S4:
                fw.op("dve", lambda e: e.tensor_reduce(out=st8[sub][:, 0, :], in_=Y3(sub), axis=AX.X, op=ALU.add), reads=[Ysbs[sub]], writes=[st8[sub]])
            yield
            for sub in S4:
                fw.op("act", lambda e: e.activation(out=ysqs[sub][:], in_=Y(sub), func=AF.Square), reads=[Ysbs[sub]], writes=[ysqs[sub]])
            yield
            for sub in S4:
                fw.op("dve", lambda e: e.tensor_reduce(out=st8[sub][:, 1, :], in_=ysqs[sub][:].rearrange("p (h v) -> p h v", h=8), axis=AX.X, op=ALU.add),
                      reads=[ysqs[sub]], writes=[st8[sub]])
            yield
            for sub in S4:
                s8 = st8[sub]
                fw.op("dve", lambda e: e.tensor_scalar(out=s8[:, 2, :], in0=s8[:, 0, :], scalar1=1.0 / 64, scalar2=None, op0=ALU.mult), reads=[s8], writes=[s8])
            yield
            for sub in S4:
                s8 = st8[sub]
                fw.op("dve", lambda e: e.tensor_tensor(out=s8[:, 3, :], in0=s8[:, 2, :], in1=s8[:, 2, :], op=ALU.mult), reads=[s8], writes=[s8])
            yield
            for sub in S4:
                s8 = st8[sub]
                fw.op("dve", lambda e: e.scalar_tensor_tensor(out=s8[:, 4, :], in0=s8[:, 1, :], scalar=1.0 / 64, in1=s8[:, 3, :], op0=ALU.mult, op1=ALU.subtract),
                      reads=[s8], writes=[s8])
            yield
            for sub in S4:
                s8 = st8[sub]
                fw.op("act", lambda e: e.activation(out=s8[:, 5, :], in_=s8[:, 4, :], func=AF.Sqrt, bias=LNX_EPS, scale=1.0), reads=[s8], writes=[s8])
            yield
            for sub in S4:
                s8 = st8[sub]
                fw.op("dve", lambda e: e.reciprocal(out=s8[:, 6, :], in_=s8[:, 5, :]), reads=[s8], writes=[s8])
            yield
            for sub in S4:
                fw.op("dve", lambda e: e.tensor_tensor(out=Y3(sub), in0=Y3(sub), in1=bc(st8[sub][:, 2, :]), op=ALU.subtract),
                      reads=[Ysbs[sub], st8[sub]], writes=[Ysbs[sub]])
            yield
            for sub in S4:
                fw.op("dve", lambda e: e.tensor_tensor(out=bvs[sub][:].rearrange("p (h v) -> p h v", h=8), in0=Vtok[:, sub, :].rearrange("p (h v) -> p h v", h=8),
                                                        in1=bc(rks[:, sub, :]), op=ALU.mult), reads=[Vtok, rks], writes=[bvs[sub]])
            yield
            for sub in S4:
                fw.op("dve", lambda e: e.tensor_tensor(out=Y3(sub), in0=Y3(sub), in1=bc(st8[sub][:, 6, :]), op=ALU.mult),
                      reads=[Ysbs[sub], st8[sub]], writes=[Ysbs[sub]])
            yield
            for sub in S4:
                fw.op("dve", lambda e: e.tensor_tensor(out=Y(sub), in0=Y(sub), in1=lwB[:], op=ALU.mult), reads=[Ysbs[sub], lwB], writes=[Ysbs[sub]])
            yield
            for sub in S4:
                fw.op("dve", lambda e: e.tensor_tensor(out=Y(sub), in0=Y(sub), in1=lbB[:], op=ALU.add), reads=[Ysbs[sub], lbB], writes=[Ysbs[sub]])
            yield
            for sub in S4:
                fw.op("dve", lambda e: e.tensor_tensor(out=Y(sub), in0=Y(sub), in1=bvs[sub][:], op=ALU.add), reads=[Ysbs[sub], bvs[sub]], writes=[Ysbs[sub]])
            yield
            for sub in S4:
                fw.op("dve", lambda e: e.tensor_tensor(out=zbs[sub][:], in0=Y(sub), in1=gtok[:, sub, :], op=ALU.mult), reads=[Ysbs[sub], gtok], writes=[zbs[sub]])
            yield
            zs = zstg[ti % 2]
            yield
            for sub in S4:
                for pr_ in range(4):
                    fw.op("pe", lambda e: e.transpose(ptb[:, pr_ * 128:(pr_ + 1) * 128], zbs[sub][:, pr_ * 128:(pr_ + 1) * 128], identb[:]),
                          reads=[zbs[sub], identb], writes=[ptb])
                fw.op("act", lambda e: e.activation(out=zs[:, :, sub * 128:(sub + 1) * 128], in_=ptb[:, 0:512].rearrange("p (r t) -> p r t", r=4), func=AF.Copy),
                      reads=[ptb], writes=[zs])
            fw.dma("sp", [(dst, zs[:, :, c_a:c_b]) for (dst, c_a, c_b) in zout_fn(t0)], reads=[zs])
            r3_done.add(ti)

        def run(*gens, rate=1):
            gens = [[g, (1 if i == 0 else rate)] for i, g in enumerate(gens) if g is not None]
            while gens:
                for ent in list(gens):
                    for _ in range(ent[1]):
                        try:
                            next(ent[0])
                        except StopIteration:
                            gens.remove(ent)
                            break

        def chain(*gens):
            for g in gens:
                if g is not None:
                    yield from g

        run(P(0))
        for c in range(4):
            run(R1a(0, c))
            R1b(0, c)
        RKS(0)
        for ti in range(n_tiles):
            nxt = ti + 1 < n_tiles
            R2a(ti, 0)
            run(R2b(ti, 0), chain(R3(ti - 1) if ti > 0 else None, R1a(ti, 3) if ti > 0 else None), rate=3)
            if ti > 0:
                R1b(ti, 3)
                RKS(ti)
            R2a(ti, 1)
            run(R2b(ti, 1), P(ti + 1) if nxt else None)
            R2a(ti, 2)
            run(R2b(ti, 2), chain(R1a(ti + 1, 0), R1a(ti + 1, 1)) if nxt else None, rate=2)
            if nxt:
                R1b(ti + 1, 0)
                R1b(ti + 1, 1)
            R2a(ti, 3)
            run(R2b(ti, 3), R1a(ti + 1, 2) if nxt else None)
            if nxt:
                R1b(ti + 1, 2)
        run(R3(n_tiles - 1))
        fw.barrier()
        fw.es = old_es
import ml_dtypes
_bf = ml_dtypes.bfloat16
NCORES = 8
SEQ = 8192
HALF = 4096
PADC = 128
CW = PADC + HALF
CS = CW + 64
RL = 2 * CS
PAIRS = [[0, 1], [2, 3], [4, 5], [6, 7]]


def _din(nc, name, shape, dt=F32):
    return Buf(nc.dram_tensor(name, list(shape), dt, kind="ExternalInput").ap(), name)


def _dout(nc, name, shape, dt=F32):
    return Buf(nc.dram_tensor(name, list(shape), dt, kind="ExternalOutput").ap(), name)


def _dscr(nc, name, shape, dt):
    return Buf(nc.dram_tensor(name, list(shape), dt).ap(), name)


def _lay_wgu(w):
    return np.ascontiguousarray(w.reshape(8, 128, 11, 2, 128).transpose(2, 1, 3, 0, 4))


def build_fused():
    nc = bass.Bass("TRN2", target_bir_lowering=False)
    x_full = _din(nc, "x_full", [SEQ, D]); x_half = _din(nc, "x_half", [128 + HALF, D])
    gmix0 = _din(nc, "gmix0", [D]); wqkv = _din(nc, "wqkv", [D, 1536]); cos = _din(nc, "cos", [SEQ, 64]); sinm = _din(nc, "sinm", [SEQ, 64])
    ffn_in = []
    for l in range(2):
        ffn_in.append(dict(wo=_din(nc, "wo%d" % l, [D, D]), gffn=_din(nc, "gffn%d" % l, [D]), wg=_din(nc, "wg%d" % l, [11, 128, 2, 8, 128]),
                           wu=_din(nc, "wu%d" % l, [11, 128, 2, 8, 128]), wd=_din(nc, "wd%d" % l, [DFF, D]), cw=_din(nc, "cw%d" % l, [128, NFC, 3]),
                           cb=_din(nc, "cb%d" % l, [128, NFC]), gnext=_din(nc, "gnext%d" % l, [D])))
    mu = _din(nc, "mu", [128, 6, 8]); wr = _din(nc, "wr", [D, 512]); wk = _din(nc, "wk", [D, 512]); wv = _din(nc, "wv", [D, 512])
    w1 = _din(nc, "w1", [D, 64]); a1 = _din(nc, "a1", [D, 64]); g1 = _din(nc, "g1", [D, 128]); w2 = _din(nc, "w2", [64, 512]); a2 = _din(nc, "a2", [64, 512]); g2 = _din(nc, "g2", [128, 512])
    pf = _din(nc, "pf", [128, 4, 5]); lnxw = _din(nc, "lnxw", [512]); lnxb = _din(nc, "lnxb", [512])
    out = _dout(nc, "out", [HALF, D], F32)
    QT = _dscr(nc, "QT", [512, SEQ], BF16); KT = _dscr(nc, "KT", [512, SEQ], BF16)
    attnS = _dscr(nc, "attnS", [8, 64, RL], BF16); attnG = _dscr(nc, "attnG", [8, 128, RL], BF16)
    xnS = _dscr(nc, "xnS", [8, 128, HALF], BF16); xnG = _dscr(nc, "xnG", [8, 256, HALF], BF16)
    h2 = _dscr(nc, "h2", [HALF, D], F32); tailS = _dscr(nc, "tailS", [128, D], F32); tailG = _dscr(nc, "tailG", [256, D], F32)
    tail3 = _dscr(nc, "tail3", [256, D], F32)
    mixL = _dscr(nc, "mixL", [D, CW], BF16)
    zS = _dscr(nc, "zS", [8, 64, RL], BF16); zG = _dscr(nc, "zG", [8, 128, RL], BF16)
    with ExitStack() as es:
        fw = FW(nc, es)
        ident, identf = make_ident(fw)
        pid = nc.partition_id(engines=[mybir.EngineType.SP])
        r = pid % 2
        with ExitStack() as ez:
            fw.es = ez
            zt = fw.sbuf([128, 1024], F32, "zeros_f")
            ztb = fw.sbuf([128, 4, 128], BF16, "zeros_b")
            fw.op("pool", lambda e: e.memset(zt[:], 0.0), writes=[zt])
            fw.op("pool", lambda e: e.memset(ztb[:], 0.0), writes=[ztb])
            fw.dma("sp", [(attnS.t.rearrange("(pr jl) q t -> (jl q) pr t", jl=2)[:, :, 0:PADC], ztb[:]),
                          (zS.t.rearrange("(pr jl) q t -> (jl q) pr t", jl=2)[:, :, 0:PADC], ztb[:]),
                          (tail3.t[0:128, :], zt[:])], reads=[zt, ztb])
            for S_ in (attnS, zS):
                Sv = S_.t.rearrange("(pr jl) q t -> (jl q) pr t", jl=2)
                fw.dma("sp", [(Sv[:, :, CW:CS], ztb[:, :, 0:CS - CW]), (Sv[:, :, CS + CW:RL], ztb[:, :, 0:RL - CS - CW])], reads=[ztb])
            fw.barrier()
            fw.es = es
        def half_cols(t0):
            pc = PADC + t0
            if t0 + 512 <= HALF:
                res = [(pc, 0, 512)]
                if t0 + 512 == HALF:
                    res.append((CS, 512 - PADC, 512))
                return res
            return [(CS + pc - HALF, 0, 512)]

        def localize(G, L):
            for rk in range(2):
                fw.dma("sp", L.t[rk * 512:(rk + 1) * 512, :].rearrange("(j q) t -> j q t", q=64),
                       G.t[:, rk * 64:(rk + 1) * 64, bass.ds(r * CS, CW)])

        def local_mix(L):
            Lv = L.t.rearrange("(kc p) t -> p kc t", p=128)
            return lambda c0, n: [(0, 128, 0, 8, Lv[:, :, c0:c0 + n])]

        phase_a(fw, ident, identf, x_full, gmix0, wqkv, cos, sinm, QT, KT,
                lambda h, qt: [(attnS.t[h, :, c:c + (b_ - a_)], a_, b_) for (c, a_, b_) in half_cols(qt * 512)])
        for j in range(8):
            fw.collective("AllGather", attnS.t[j], attnG.t[j], PAIRS)
        fw.barrier()
        localize(attnG, mixL)
        fw.barrier()
        f = ffn_in[0]
        phase_f(fw, ident, lambda s: x_half.t[s * 128:(s + 1) * 128, :], local_mix(mixL),
                f["wo"], f["gffn"], f["wg"], f["wu"], f["wd"], f["cw"], f["cb"], f["gnext"], h2, tailS, None, xnS, HALF // 512, False)
        for j in range(8):
            fw.collective("AllGather", xnS.t[j], xnG.t[j], PAIRS)
        fw.collective("AllGather", tailS.t[:, :], tailG.t[:, :], PAIRS)
        fw.barrier()
        fw.dma("sp", tail3.t[128:256, :], tailG.t[0:128, :])
        fw.barrier()
        xg_v = xnG.t.rearrange("kc (hf p) t -> hf p kc t", hf=2)

        def x_fn(t0):
            hf, tl = t0 // HALF, t0 % HALF
            if t0 == 0:
                return [(1, 513, xg_v[0][:, :, 0:512])]
            if tl == 0:
                return [(0, 1, xg_v[hf - 1][:, :, HALF - 1:HALF]), (1, 513, xg_v[hf][:, :, 0:512])]
            return [(0, 513, xg_v[hf][:, :, tl - 1:tl + 512])]

        zS_v = zS.t.rearrange("(pr jl) q t -> (jl q) pr t", jl=2)
        phase_r(fw, ident, identf, x_fn, mu, wr, wk, wv, w1, a1, g1, w2, a2, g2, pf, lnxw, lnxb,
                lambda t0: [(zS_v[:, :, c:c + (b_ - a_)], a_, b_) for (c, a_, b_) in half_cols(t0)], SEQ // 512)
        for j in range(8):
            fw.collective("AllGather", zS.t[j], zG.t[j], PAIRS)
        fw.barrier()
        localize(zG, mixL)
        fw.barrier()
        f = ffn_in[1]
        phase_f(fw, ident, lambda s: (tail3.t[bass.ds(r * 128, 128), :] if s == 0 else h2.t[(s - 1) * 128:s * 128, :]), local_mix(mixL),
                f["wo"], f["gffn"], f["wg"], f["wu"], f["wd"], f["cw"], f["cb"], f["gnext"], None, None, out, None, HALF // 512, True)
        fw.finish([])
    return nc


def _rope_tables():
    inv = (1.0 / (10000.0 ** (np.arange(0, 64, 2, dtype=np.float32) / np.float32(64)))).astype(np.float32)
    ang = np.arange(SEQ, dtype=np.float32)[:, None] * inv[None, :]
    ang = np.concatenate([ang, ang], -1)
    cos = np.cos(ang).astype(np.float32); sin = np.sin(ang).astype(np.float32)
    sinm = np.concatenate([-sin[:, :32], sin[:, 32:]], -1).astype(np.float32)
    return np.ascontiguousarray(cos), np.ascontiguousarray(sinm)


def kernel(x, norm_mix, norm_ffn, norm_final, attn_w_qkv, attn_w_o,
           rwkv_mu, rwkv_w_rkv, rwkv_w0, rwkv_w1, rwkv_w2, rwkv_a0, rwkv_a1,
           rwkv_a2, rwkv_g1, rwkv_g2, rwkv_k_k, rwkv_k_a, rwkv_r_k,
           rwkv_lnx_w, rwkv_lnx_b, rwkv_w_o,
           ffn_w_gate, ffn_w_up, ffn_conv_w, ffn_conv_b, ffn_w_down):
    f32 = lambda a: np.ascontiguousarray(np.asarray(a, dtype=np.float32))
    x = f32(x)
    cores = list(range(NCORES))
    cos, sinm = _rope_tables()
    wqkv = f32(attn_w_qkv)[0]
    wrkv = f32(rwkv_w_rkv)[0]
    mu_l = np.ascontiguousarray(f32(rwkv_mu)[0].reshape(6, 8, 128).transpose(2, 0, 1))
    shared = {"gmix0": f32(norm_mix)[0], "cos": cos, "sinm": sinm, "mu": mu_l,
              "w1": f32(rwkv_w1)[0], "a1": f32(rwkv_a1)[0], "g1": f32(rwkv_g1)[0]}
    wos = [f32(attn_w_o)[0], f32(rwkv_w_o)[0]]
    gnexts = [f32(norm_mix)[1], f32(norm_final)]
    for l in range(2):
        cw = f32(ffn_conv_w)[l]; cb = f32(ffn_conv_b)[l]
        shared.update({"wo%d" % l: wos[l], "gffn%d" % l: f32(norm_ffn)[l], "wg%d" % l: _lay_wgu(f32(ffn_w_gate)[l]), "wu%d" % l: _lay_wgu(f32(ffn_w_up)[l]),
                       "wd%d" % l: f32(ffn_w_down)[l], "cw%d" % l: np.ascontiguousarray(cw.T.reshape(NFC, 128, 3).transpose(1, 0, 2)),
                       "cb%d" % l: np.ascontiguousarray(cb.reshape(NFC, 128).T), "gnext%d" % l: gnexts[l]})
    maps = []
    for c in cores:
        b, r = c // 2, c % 2
        own = slice(512 * r, 512 * r + 512)
        wc = np.ascontiguousarray(np.concatenate([wqkv[:, 0:1024][:, own], wqkv[:, 1024:2048][:, own], wqkv[:, 2048:3072][:, own]], 1))
        xh = np.zeros((128 + HALF, D), np.float32)
        if r == 0:
            xh[128:] = x[b][0:HALF]
        else:
            xh[:] = x[b][HALF - 128:SEQ]
        pfl = np.stack([f32(rwkv_w0)[0][own], f32(rwkv_a0)[0][own], f32(rwkv_k_k)[0][own], f32(rwkv_k_a)[0][own], f32(rwkv_r_k)[0].reshape(-1)[own]], -1)
        m = dict(shared)
        m.update({"x_full": x[b], "x_half": xh, "wqkv": wc,
                  "wr": np.ascontiguousarray(wrkv[0][:, own]), "wk": np.ascontiguousarray(wrkv[1][:, own]), "wv": np.ascontiguousarray(wrkv[2][:, own]),
                  "w2": np.ascontiguousarray(f32(rwkv_w2)[0][:, own]), "a2": np.ascontiguousarray(f32(rwkv_a2)[0][:, own]),
                  "g2": np.ascontiguousarray(f32(rwkv_g2)[0][:, own]),
                  "pf": np.ascontiguousarray(pfl.reshape(4, 128, 5).transpose(1, 0, 2)),
                  "lnxw": np.ascontiguousarray(f32(rwkv_lnx_w)[0][own]), "lnxb": np.ascontiguousarray(f32(rwkv_lnx_b)[0][own])})
        maps.append(m)
    res = run_bass_kernel_spmd(build_fused(), maps, core_ids=cores)
    out = np.stack([np.concatenate([np.asarray(res.results[2 * b]["out"]), np.asarray(res.results[2 * b + 1]["out"])], 0) for b in range(4)], 0)
    return out.astype(np.float32)
```
